# Optimizing a Trainium2 kernel written in Bass

```python
import jax, jax.numpy as jnp
from jax import lax
import numpy as np

D_MODEL = 2048
BATCH = 4
SEQ = 2048
DEPTH = 1
DEC_BATCH = 128
DEC_SEQ = 8
PAST_LEN = 16384
PAGE_SIZE = 128

MIX_WIDTH = D_MODEL
C_CONV = MIX_WIDTH // 2
CONV_WIDTH = 31
RWKV_HEAD_DIM = 64
RWKV_WIDTH = MIX_WIDTH - C_CONV
RWKV_HEADS = RWKV_WIDTH // RWKV_HEAD_DIM
DECAY_LORA = 96
A_LORA = 96
GATE_LORA = 256
N_RWKV_COLS = 3 * RWKV_WIDTH + DECAY_LORA + A_LORA + GATE_LORA
N_IN_COLS = 2 * C_CONV + N_RWKV_COLS
N_MEM = 256
XA_HEADS = 4
XA_HEAD_DIM = D_MODEL // XA_HEADS
D_FF = 11 * D_MODEL // 4
FFN_CONV_WIDTH = 3
RMS_EPS = 1e-6
LN_EPS = 1e-5
GN_EPS = 64e-5

kernel_name = 'hymba_conformer_rwkv7_convffn_xattn_step'


def rms_norm(x, g):
    xf = x.astype(jnp.float32)
    y = xf * lax.rsqrt(jnp.mean(xf * xf, axis=-1, keepdims=True) + RMS_EPS)
    return (y * g.astype(jnp.float32)).astype(x.dtype)


def layer_norm(x, g, b):
    xf = x.astype(jnp.float32)
    mu = jnp.mean(xf, axis=-1, keepdims=True)
    var = jnp.mean(jnp.square(xf - mu), axis=-1, keepdims=True)
    y = (xf - mu) * lax.rsqrt(var + LN_EPS) * g.astype(jnp.float32) + b.astype(jnp.float32)
    return y.astype(x.dtype)


def causal_dwconv(u, past, w, bias):
    k = w.shape[0]
    t = u.shape[1]
    full = jnp.concatenate([past.astype(u.dtype), u], axis=1)
    wf = w.astype(jnp.float32)
    acc = bias.astype(jnp.float32)
    for j in range(k):
        acc = acc + full[:, j:j + t].astype(jnp.float32) * wf[j]
    return acc.astype(u.dtype), full[:, t:]


def wkv_step(S, inp):
    r, w, k, v, a, b = inp
    sa = jnp.einsum('bhij,bhj->bhi', S, a)
    S = S * w[:, :, None, :] + sa[..., None] * b[:, :, None, :] + v[..., None] * k[:, :, None, :]
    y = jnp.einsum('bhij,bhj->bhi', S, r)
    return S, y


def rwkv7_mix(xs, wkv, p):
    f32 = jnp.float32
    b_, t_ = xs.shape[:2]
    W = RWKV_WIDTH
    heads = lambda z: z.reshape(b_, t_, RWKV_HEADS, RWKV_HEAD_DIM)
    r = xs[..., :W].astype(f32)
    k = xs[..., W:2 * W].astype(f32)
    v = xs[..., 2 * W:3 * W].astype(f32)
    o = 3 * W
    wd = xs[..., o:o + DECAY_LORA]
    ad = xs[..., o + DECAY_LORA:o + DECAY_LORA + A_LORA]
    gd = xs[..., o + DECAY_LORA + A_LORA:]
    w_log = -jax.nn.softplus(-(p['w0'] + jnp.tanh(wd) @ p['w_lora']).astype(f32)) - 0.5
    decay = jnp.exp(-jnp.exp(w_log))
    a = jax.nn.sigmoid((p['a0'] + ad @ p['a_lora']).astype(f32))
    g = (jax.nn.sigmoid(gd) @ p['g_lora']).astype(f32)
    kk = heads(k * p['k_k'].astype(f32))
    kk = kk / jnp.maximum(jnp.sqrt(jnp.sum(kk * kk, axis=-1, keepdims=True)), 1e-12)
    k = k * (1.0 + (a - 1.0) * p['k_a'].astype(f32))
    rh, kh, vh, ah = heads(r), heads(k), heads(v), heads(a)
    seq = tuple(jnp.moveaxis(z, 1, 0) for z in (rh, heads(decay), kh, vh, -kk, kk * ah))
    S_new, ys = lax.scan(wkv_step, wkv.astype(f32), seq)
    y = jnp.moveaxis(ys, 0, 1)
    mu = jnp.mean(y, axis=-1, keepdims=True)
    var = jnp.mean(jnp.square(y - mu), axis=-1, keepdims=True)
    yn = ((y - mu) * lax.rsqrt(var + GN_EPS)).reshape(b_, t_, W)
    yn = yn * p['ln_x_g'].astype(f32) + p['ln_x_b'].astype(f32)
    bonus = jnp.sum(rh * kh * heads(jnp.broadcast_to(p['r_k'].astype(f32), r.shape)), axis=-1, keepdims=True) * vh
    out = (yn + bonus.reshape(b_, t_, W)) * g
    return out.astype(xs.dtype), S_new


def memory_kv(mem, p):
    b_ = mem.shape[0]
    m = rms_norm(mem, p['norm_mem'])
    k = (m @ p['w_k']).reshape(b_, N_MEM, XA_HEADS, XA_HEAD_DIM)
    v = (m @ p['w_v']).reshape(b_, N_MEM, XA_HEADS, XA_HEAD_DIM)
    return k, v


def trunk_layer(x, mem_k, mem_v, conv_buf, shift, wkv, ffn_buf, p):
    b_, t_ = x.shape[:2]
    h = rms_norm(x, p['norm_mix_pre'])
    proj = h @ p['w_in']
    glu = proj[..., :C_CONV] * jax.nn.sigmoid(proj[..., C_CONV:2 * C_CONV])
    cv, new_conv = causal_dwconv(glu, conv_buf, p['conv_dw'], p['conv_dw_b'])
    cv = jax.nn.silu(layer_norm(cv, p['conv_ln_g'], p['conv_ln_b']))
    pr = proj[..., 2 * C_CONV:]
    prev = jnp.concatenate([shift[:, None].astype(pr.dtype), pr[:, :-1]], axis=1)
    xs = pr + (prev - pr) * p['rwkv_mu']
    new_shift = pr[:, -1]
    rw, new_wkv = rwkv7_mix(xs, wkv, p)
    mix = jnp.concatenate([cv, rw.astype(cv.dtype)], axis=-1) @ p['w_out']
    x = x + rms_norm(mix, p['norm_mix_post'])
    h = rms_norm(x, p['norm_xa_pre'])
    q = (h @ p['w_q']).reshape(b_, t_, XA_HEADS, XA_HEAD_DIM)
    s = jnp.einsum('bthd,bmhd->bhtm', q, mem_k.astype(q.dtype)).astype(jnp.float32) * (XA_HEAD_DIM ** -0.5)
    attn = jax.nn.softmax(s, axis=-1).astype(x.dtype)
    o = jnp.einsum('bhtm,bmhd->bthd', attn, mem_v.astype(x.dtype)).reshape(b_, t_, XA_HEADS * XA_HEAD_DIM)
    x = x + rms_norm(o @ p['w_o'], p['norm_xa_post'])
    h = rms_norm(x, p['norm_ffn_pre'])
    up = h @ p['w_up']
    uc, new_ffn = causal_dwconv(up, ffn_buf, p['ffn_dw'], p['ffn_dw_b'])
    act = jax.nn.silu(uc[..., :D_FF]) * uc[..., D_FF:]
    x = x + rms_norm(act @ p['w_down'], p['norm_ffn_post'])
    return x, new_conv, new_shift, new_wkv, new_ffn


def setup_inputs(seed: int = 0) -> dict:
    key = jax.random.key(seed)
    ks = iter(jax.random.split(key, 64))
    nrm = lambda shape, s=1.0: jax.random.normal(next(ks), shape, jnp.float32) * s
    gain = lambda n: 1.0 + nrm((DEPTH, n), 0.05)
    L = DEPTH
    return {
        'x_prompt': nrm((BATCH, SEQ, D_MODEL)),
        'x_sample': nrm((DEC_BATCH, DEC_SEQ, D_MODEL)),
        'cache_mem_k': nrm((L, DEC_BATCH, N_MEM, XA_HEADS, XA_HEAD_DIM)),
        'cache_mem_v': nrm((L, DEC_BATCH, N_MEM, XA_HEADS, XA_HEAD_DIM)),
        'state_conv': nrm((L, DEC_BATCH, CONV_WIDTH - 1, C_CONV), 0.5),
        'state_shift': nrm((L, DEC_BATCH, N_RWKV_COLS)),
        'state_wkv': nrm((L, DEC_BATCH, RWKV_HEADS, RWKV_HEAD_DIM, RWKV_HEAD_DIM), 0.1),
        'state_ffn': nrm((L, DEC_BATCH, FFN_CONV_WIDTH - 1, 2 * D_FF)),
        'mem_prompt': nrm((BATCH, N_MEM, D_MODEL)),
        'norm_mix_pre': gain(D_MODEL),
        'w_in': nrm((L, D_MODEL, N_IN_COLS), D_MODEL ** -0.5),
        'conv_dw': nrm((L, CONV_WIDTH, C_CONV), CONV_WIDTH ** -0.5),
        'conv_dw_b': nrm((L, C_CONV), 0.02),
        'conv_ln_g': gain(C_CONV),
        'conv_ln_b': nrm((L, C_CONV), 0.02),
        'rwkv_mu': jax.random.uniform(next(ks), (L, N_RWKV_COLS), jnp.float32),
        'w0': jax.random.uniform(next(ks), (L, RWKV_WIDTH), jnp.float32, -6.0, 1.0),
        'w_lora': nrm((L, DECAY_LORA, RWKV_WIDTH), 0.1 * DECAY_LORA ** -0.5),
        'a0': nrm((L, RWKV_WIDTH), 0.1),
        'a_lora': nrm((L, A_LORA, RWKV_WIDTH), 0.1 * A_LORA ** -0.5),
        'g_lora': nrm((L, GATE_LORA, RWKV_WIDTH), GATE_LORA ** -0.5),
        'k_k': 0.85 + nrm((L, RWKV_WIDTH), 0.05),
        'k_a': 1.0 + nrm((L, RWKV_WIDTH), 0.05),
        'r_k': nrm((L, RWKV_WIDTH), 0.1),
        'ln_x_g': gain(RWKV_WIDTH),
        'ln_x_b': nrm((L, RWKV_WIDTH), 0.02),
        'w_out': nrm((L, MIX_WIDTH, D_MODEL), MIX_WIDTH ** -0.5),
        'norm_mix_post': gain(D_MODEL),
        'norm_xa_pre': gain(D_MODEL),
        'norm_mem': gain(D_MODEL),
        'w_q': nrm((L, D_MODEL, XA_HEADS * XA_HEAD_DIM), D_MODEL ** -0.5),
        'w_k': nrm((L, D_MODEL, XA_HEADS * XA_HEAD_DIM), D_MODEL ** -0.5),
        'w_v': nrm((L, D_MODEL, XA_HEADS * XA_HEAD_DIM), D_MODEL ** -0.5),
        'w_o': nrm((L, XA_HEADS * XA_HEAD_DIM, D_MODEL), (XA_HEADS * XA_HEAD_DIM) ** -0.5),
        'norm_xa_post': gain(D_MODEL),
        'norm_ffn_pre': gain(D_MODEL),
        'w_up': nrm((L, D_MODEL, 2 * D_FF), D_MODEL ** -0.5),
        'ffn_dw': nrm((L, FFN_CONV_WIDTH, 2 * D_FF), FFN_CONV_WIDTH ** -0.5),
        'ffn_dw_b': nrm((L, 2 * D_FF), 0.02),
        'w_down': nrm((L, D_FF, D_MODEL), D_FF ** -0.5),
        'norm_ffn_post': gain(D_MODEL),
    }


def reference(x_prompt, x_sample, cache_mem_k, cache_mem_v, state_conv, state_shift, state_wkv, state_ffn,
              mem_prompt, norm_mix_pre, w_in, conv_dw, conv_dw_b, conv_ln_g, conv_ln_b, rwkv_mu, w0, w_lora,
              a0, a_lora, g_lora, k_k, k_a, r_k, ln_x_g, ln_x_b, w_out, norm_mix_post, norm_xa_pre, norm_mem,
              w_q, w_k, w_v, w_o, norm_xa_post, norm_ffn_pre, w_up, ffn_dw, ffn_dw_b, w_down, norm_ffn_post):
    bp = x_prompt.shape[0]
    dt = x_prompt.dtype
    yp, ys = x_prompt, x_sample
    conv_p, conv_s, shift_p, shift_s, wkv_p, wkv_s, ffn_p, ffn_s, memk_p, memv_p = ([] for _ in range(10))
    for l in range(DEPTH):
        p = {'norm_mix_pre': norm_mix_pre[l], 'w_in': w_in[l], 'conv_dw': conv_dw[l], 'conv_dw_b': conv_dw_b[l],
             'conv_ln_g': conv_ln_g[l], 'conv_ln_b': conv_ln_b[l], 'rwkv_mu': rwkv_mu[l], 'w0': w0[l],
             'w_lora': w_lora[l], 'a0': a0[l], 'a_lora': a_lora[l], 'g_lora': g_lora[l], 'k_k': k_k[l],
             'k_a': k_a[l], 'r_k': r_k[l], 'ln_x_g': ln_x_g[l], 'ln_x_b': ln_x_b[l], 'w_out': w_out[l],
             'norm_mix_post': norm_mix_post[l], 'norm_xa_pre': norm_xa_pre[l], 'norm_mem': norm_mem[l],
             'w_q': w_q[l], 'w_k': w_k[l], 'w_v': w_v[l], 'w_o': w_o[l], 'norm_xa_post': norm_xa_post[l],
             'norm_ffn_pre': norm_ffn_pre[l], 'w_up': w_up[l], 'ffn_dw': ffn_dw[l], 'ffn_dw_b': ffn_dw_b[l],
             'w_down': w_down[l], 'norm_ffn_post': norm_ffn_post[l]}
        mk, mv = memory_kv(mem_prompt, p)
        yp, c1, s1, w1, f1 = trunk_layer(
            yp, mk, mv,
            jnp.zeros((bp, CONV_WIDTH - 1, C_CONV), dt),
            jnp.zeros((bp, N_RWKV_COLS), dt),
            jnp.zeros((bp, RWKV_HEADS, RWKV_HEAD_DIM, RWKV_HEAD_DIM), jnp.float32),
            jnp.zeros((bp, FFN_CONV_WIDTH - 1, 2 * D_FF), dt), p)
        ys, c2, s2, w2, f2 = trunk_layer(ys, cache_mem_k[l], cache_mem_v[l], state_conv[l], state_shift[l],
                                         state_wkv[l], state_ffn[l], p)
        conv_p.append(c1); conv_s.append(c2); shift_p.append(s1); shift_s.append(s2)
        wkv_p.append(w1); wkv_s.append(w2); ffn_p.append(f1); ffn_s.append(f2)
        memk_p.append(mk); memv_p.append(mv)
    return (yp, ys, jnp.stack(conv_p), jnp.stack(conv_s), jnp.stack(shift_p), jnp.stack(shift_s),
            jnp.stack(wkv_p), jnp.stack(wkv_s), jnp.stack(ffn_p), jnp.stack(ffn_s),
            jnp.stack(memk_p), jnp.stack(memv_p))
```

```python
import numpy as np
from contextlib import ExitStack
import concourse.bass as bass
import concourse.mybir as mybir
from concourse.bass_utils import run_bass_kernel_spmd

F32 = mybir.dt.float32
BF16 = mybir.dt.bfloat16
AF = mybir.ActivationFunctionType
ALU = mybir.AluOpType
ENGS = ("pe", "act", "dve", "pool", "sp")
SERIAL = False
SKIP_MIX = False

D = 2048
KC = 16
NT = 2176
NCH = 1280
CH0 = 896
DFF = 5632
NFC = 44
RMS_EPS = 1e-6
LN_EPS = 1e-5
GN_EPS = 64e-5
LDK = 0.6065306597126334


class Res:
    __slots__ = ("last_write", "reads")

    def __init__(self):
        self.last_write = None
        self.reads = []


class Op:
    __slots__ = ("eng", "fn", "deps", "signal", "idx", "is_dma", "dma_sem", "dma_cnt", "sigcount", "prewait")


class Sched:
    def __init__(self, nc, n_dma_sems=80):
        self.nc = nc
        self.ops = []
        self.per_eng = {e: [] for e in ENGS}
        self.n_dma_sems = n_dma_sems
        self.dma_rr = {"hw": 0, "sw": 0}
        self.dma_uses = [0] * n_dma_sems
        self.dma_last_op = [None] * n_dma_sems
        self.res = {}
        self.pending = {e: set() for e in ENGS}
        self.dma_unconsumed = set()
        self.excl = set("ps%d" % i for i in range(8))

    def R(self, name):
        r = self.res.get(name)
        if r is None:
            r = Res()
            self.res[name] = r
        return r

    def barrier(self):
        last = set()
        for e in ENGS:
            if self.per_eng[e]:
                last.add(self.per_eng[e][-1])
        last |= self.dma_unconsumed
        for e in ENGS:
            self.pending[e] |= last
        self.dma_unconsumed = set()

    def _mk(self, eng, fn, reads, writes, is_dma):
        op = Op()
        op.eng = eng
        op.fn = fn
        op.is_dma = is_dma
        op.signal = False
        op.dma_sem = None
        op.dma_cnt = 0
        op.prewait = None
        oid = len(self.ops)
        deps = set(self.pending[eng])
        self.pending[eng] = set()
        if SERIAL:
            for e_ in ENGS:
                if self.per_eng[e_]:
                    deps.add(self.per_eng[e_][-1])
        writes = list(writes) + [r for r in reads if r in self.excl and r not in writes]
        reads = [r for r in reads if r not in self.excl]
        rl = [self.R(r) for r in reads]
        wl = [self.R(w) for w in writes]
        for r in rl:
            if r.last_write is not None:
                deps.add(r.last_write)
        for w in wl:
            if w.last_write is not None:
                deps.add(w.last_write)
            deps.update(w.reads)
        for r in rl:
            r.reads.append(oid)
        for w in wl:
            w.last_write = oid
            w.reads = []
        deps.discard(oid)
        for d in deps:
            self.dma_unconsumed.discard(d)
        op.deps = deps
        op.idx = len(self.per_eng[eng])
        self.ops.append(op)
        self.per_eng[eng].append(oid)
        if is_dma:
            half = self.n_dma_sems // 2
            kind = "sw" if eng == "pool" else "hw"
            k = self.dma_rr[kind] + (half if kind == "sw" else 0)
            self.dma_rr[kind] = (self.dma_rr[kind] + 1) % half
            op.dma_sem = k
            self.dma_uses[k] += 1
            op.dma_cnt = self.dma_uses[k]
            op.prewait = self.dma_last_op[k]
            self.dma_last_op[k] = oid
            self.dma_unconsumed.add(oid)
        return oid

    def op(self, eng, fn, reads=(), writes=()):
        return self._mk(eng, fn, reads, writes, False)

    def dma(self, eng, fn, reads=(), writes=()):
        return self._mk(eng, fn, reads, writes, True)

    def emit(self):
        nc = self.nc
        ops = self.ops
        known = {e: {f: -1 for f in ENGS} for e in ENGS}
        dma_known = {e: set() for e in ENGS}
        need = []
        for oid, op in enumerate(ops):
            e = op.eng
            lst = []
            cmax = {}
            for d in op.deps:
                dop = ops[d]
                if dop.is_dma:
                    if d not in dma_known[e]:
                        lst.append(("d", d))
                        dma_known[e].add(d)
                else:
                    f = dop.eng
                    if f == "pe" and e == "pe" and not op.is_dma:
                        continue
                    if dop.idx <= known[e][f]:
                        continue
                    if f not in cmax or ops[cmax[f]].idx < dop.idx:
                        cmax[f] = d
            for f, d in cmax.items():
                lst.append(("c", d))
                known[e][f] = ops[d].idx
                ops[d].signal = True
            if op.is_dma and op.prewait is not None and op.prewait not in dma_known[e]:
                lst.append(("d", op.prewait))
                dma_known[e].add(op.prewait)
            need.append(lst)
        for e in ENGS:
            c = 0
            for oid in self.per_eng[e]:
                op = ops[oid]
                if op.signal and not op.is_dma:
                    c += 1
                op.sigcount = c
        with ExitStack() as es:
            csem = {e: es.enter_context(nc.semaphore("cs_" + e)) for e in ENGS}
            dsem = [es.enter_context(nc.semaphore("ds_%d" % i)) for i in range(self.n_dma_sems)]
            block = es.enter_context(nc.Block())

            def run(e, engobj):
                for oid in self.per_eng[e]:
                    op = ops[oid]
                    for kind, d in need[oid]:
                        dop = ops[d]
                        if kind == "c":
                            engobj.wait_ge(csem[dop.eng], dop.sigcount)
                        else:
                            engobj.wait_ge(dsem[dop.dma_sem], 16 * dop.dma_cnt)
                    ins = op.fn(engobj)
                    if op.is_dma:
                        ins.then_inc(dsem[op.dma_sem], 16)
                    elif op.signal:
                        ins.then_inc(csem[e], 1)
                last = {}
                for oid in self.per_eng[e]:
                    op = ops[oid]
                    if op.is_dma:
                        last[op.dma_sem] = max(last.get(op.dma_sem, 0), op.dma_cnt)
                for k, cnt in last.items():
                    engobj.wait_ge(dsem[k], 16 * cnt)

            @block.tensor
            def _(eng):
                run("pe", eng)

            @block.scalar
            def _(eng):
                run("act", eng)

            @block.vector
            def _(eng):
                run("dve", eng)

            @block.gpsimd
            def _(eng):
                run("pool", eng)

            @block.sync
            def _(eng):
                run("sp", eng)


def relayout_w(W, ncol):
    K, N = W.shape
    return np.ascontiguousarray(W.reshape(K // 128, 128, N // ncol, ncol).transpose(2, 1, 0, 3))


def fm(v):
    v = np.asarray(v, np.float32).reshape(-1)
    n = v.shape[0]
    nch = (n + 127) // 128
    p = np.zeros(nch * 128, np.float32)
    p[:n] = v
    return p.reshape(nch, 128).T


VEC_LAYOUT = {}


def build_vecs(inp):
    cols = []
    pos = [0]

    def add(name, arr):
        VEC_LAYOUT[name] = pos[0]
        cols.append(arr)
        pos[0] += arr.shape[1]

    for nm in ["norm_mix_pre", "norm_mix_post", "norm_xa_pre", "norm_xa_post", "norm_ffn_pre",
               "norm_ffn_post", "norm_mem"]:
        add(nm, fm(inp[nm][0]))
    add("conv_dw", np.ascontiguousarray(inp["conv_dw"][0].reshape(31, 8, 128).transpose(2, 1, 0)).reshape(128, 248))
    for nm in ["conv_dw_b", "conv_ln_g", "conv_ln_b"]:
        add(nm, fm(inp[nm][0]))
    mu = inp["rwkv_mu"][0]
    add("mu", np.concatenate([fm(mu[:3072]), fm(mu[3072:3168]), fm(mu[3168:3264]), fm(mu[3264:3520])], 1))
    for nm in ["w0", "a0", "k_k", "k_a", "r_k", "ln_x_g", "ln_x_b"]:
        add(nm, fm(inp[nm][0]))
    add("ffn_dw", np.ascontiguousarray(inp["ffn_dw"][0].reshape(3, 88, 128).transpose(2, 1, 0)).reshape(128, 264))
    add("ffn_dw_b", fm(inp["ffn_dw_b"][0]))
    return np.ascontiguousarray(np.concatenate(cols, 1)), pos[0]


CONST_LAYOUT = {}


def build_consts():
    cols = []
    pos = [0]

    def add(name, arr):
        CONST_LAYOUT[name] = (pos[0], arr.shape[1])
        cols.append(arr.astype(np.float32))
        pos[0] += arr.shape[1]

    i = np.arange(128)
    s, t = i[:, None], i[None, :]
    same = (s // 8) == (t // 8)
    add("ident", (s == t))
    add("su", (t > s))
    add("iu", (t >= s))
    add("sl", (t < s))
    add("su_bd", (t > s) & same)
    add("iu_bd", (t >= s) & same)
    add("sl_bd", (t < s) & same)
    add("blk", (s // 64) == (t // 64))
    sm = np.ones(1280)
    sm[0:1152:128] = 0
    sm[1152:1280:8] = 0
    add("scan", np.broadcast_to(sm[None, :], (128, 1280)))
    add("seqm", (i[:, None] // 8) == np.arange(16)[None, :])
    return np.ascontiguousarray(np.concatenate(cols, 1)), pos[0]


class StopBuild(Exception):
    pass


STOP = None
DBG_TILE = 0
DUMPS = {}


def build_nc(NV, NCONST):
    nc = bass.Bass("TRN2", target_bir_lowering=False)
    S = Sched(nc)

    def ck(name):
        if STOP == name:
            raise StopBuild()

    def dump(name, ap, rd, dt=F32):
        if STOP is None:
            return
        d = nc.dram_tensor("dbg_" + name, list(ap.shape), dt, kind="ExternalOutput").ap()
        S.dma("sp", lambda e: e.dma_start(out=d, in_=ap), rd, [])

    def din(name, shape):
        return nc.dram_tensor(name, list(shape), F32, kind="ExternalInput").ap()

    def dout(name, shape):
        return nc.dram_tensor(name, list(shape), F32, kind="ExternalOutput").ap()

    xT = din("xT", [D, NT])
    flag = din("flag", [128, 1])
    memT = din("memT", [D, 256])
    kTs = din("kTs", [16, D, 256])
    vs = din("vs", [16, 256, D])
    convst = din("convst", [1024, 16, 30])
    shiftst = din("shiftst", [128, 28, 16])
    wkvT = din("wkvT", [8, 128, 16, 64])
    ffnst = din("ffnst", [88, 128, 16, 2])
    vecs_d = din("vecs_in", [128, NV])
    consts_d = din("consts_in", [128, NCONST])
    w_in_l = din("w_in_l", [44, 128, 16, 128])
    w_lora_l = din("w_lora_l", [128, 1024])
    a_lora_l = din("a_lora_l", [128, 1024])
    g_lora_l = din("g_lora_l", [128, 2, 1024])
    w_out_l = din("w_out_l", [16, 128, 16, 128])
    w_q_l = din("w_q_l", [16, 128, 16, 128])
    w_k_l = din("w_k_l", [16, 128, 16, 128])
    w_v_l = din("w_v_l", [4, 128, 16, 512])
    w_o_l = din("w_o_l", [16, 128, 16, 128])
    w_up_l = din("w_up_l", [88, 128, 16, 128])
    w_down_l = din("w_down_l", [16, 128, 44, 128])

    yT = dout("yT", [D, NCH])
    ncp = dout("ncp", [1024, 30])
    ncs = dout("ncs", [1024, 16, 30])
    nsh = dout("nsh", [128, 28, 17])
    nwp = dout("nwp", [8, 128, 64])
    nws = dout("nws", [8, 128, 16, 64])
    nfp = dout("nfp", [88, 128, 2])
    nfs = dout("nfs", [88, 128, 16, 2])
    mkT = dout("mkT", [D, 256])
    mv = dout("mv", [256, D])

    x1T = nc.dram_tensor("x1T", [D, NCH], F32, kind="Internal").ap()
    x2T = nc.dram_tensor("x2T", [D, NCH], F32, kind="Internal").ap()
    mixS = nc.dram_tensor("mixS", [D, NCH], BF16, kind="Internal").ap()

    V = VEC_LAYOUT
    C = CONST_LAYOUT

    def ACT(out, in_, func, rd, wr, bias=None, scale=None):
        kw = {}
        if bias is not None:
            kw["bias"] = bias
        if scale is not None:
            kw["scale"] = scale
        S.op("act", lambda e: e.activation(out=out, in_=in_, func=func, **kw), rd, wr)

    def TT(eng, out, in0, in1, op, rd, wr):
        S.op(eng, lambda e: e.tensor_tensor(out=out, in0=in0, in1=in1, op=op), rd, wr)

    def TS(eng, out, in0, s1, s2, op0, op1, rd, wr):
        if s2 is None:
            S.op(eng, lambda e: e.tensor_scalar(out=out, in0=in0, scalar1=s1, scalar2=None, op0=op0), rd, wr)
        else:
            S.op(eng, lambda e: e.tensor_scalar(out=out, in0=in0, scalar1=s1, scalar2=s2, op0=op0, op1=op1), rd, wr)

    def STT(out, in0, sc, in1, op0, op1, rd, wr):
        S.op("dve", lambda e: e.scalar_tensor_tensor(out=out, in0=in0, scalar=sc, in1=in1, op0=op0, op1=op1), rd, wr)

    def MM(out, lhsT, rhs, start, stop, rd, wr):
        S.op("pe", lambda e: e.matmul(out, lhsT=lhsT, rhs=rhs, start=start, stop=stop), rd, wr)

    def CP(eng, out, in_, rd, wr):
        if eng == "act":
            S.op("act", lambda e: e.copy(out=out, in_=in_), rd, wr)
        else:
            S.op(eng, lambda e: e.tensor_copy(out=out, in_=in_), rd, wr)

    def DMA(q, out, in_, rd, wr):
        S.dma(q, lambda e: e.dma_start(out=out, in_=in_), rd, wr)

    def blocks(n, b=512):
        r = []
        t = 0
        while t < n:
            r.append((t, min(b, n - t)))
            t += b
        return r

    try:
      with ExitStack() as top:
          used_names = {}

          def sbt(es, name, shape, dt):
              k = used_names.get(name, 0)
              used_names[name] = k + 1
              if k:
                  name = "%s_%d" % (name, k + 1)
              return es.enter_context(nc.sbuf_tensor(name, list(shape), dt))

          ps = [top.enter_context(nc.psum_tensor("ps%d" % i, [128, 512], F32)) for i in range(8)]
          PSN = ["ps%d" % i for i in range(8)]

          vecs = sbt(top, "vecs", [128, NV], F32)
          cst = sbt(top, "cst", [128, NCONST], F32)
          DMA("sp", vecs[:], vecs_d, [], ["vecs"])
          DMA("sp", cst[:], consts_d, [], ["cst"])
          flg = sbt(top, "flg", [128, 1], F32)
          DMA("sp", flg[:], flag, [], ["flg"])
          identb = sbt(top, "identb", [128, 128], BF16)
          ident4 = sbt(top, "ident4", [128, 4, 128], BF16)
          onesb = sbt(top, "onesb", [128, 128], BF16)
          blkb = sbt(top, "blkb", [128, 128], BF16)
          omka = sbt(top, "omka", [128, 8], F32)
          ommu = sbt(top, "ommu", [128, 28], F32)

          def cv_(name):
              c0, n = C[name]
              return cst[:, c0:c0 + n]

          CP("dve", identb[:], cv_("ident"), ["cst"], ["identb"])
          for q in range(4):
              CP("dve", ident4[:, q, :], cv_("ident"), ["cst"], ["ident4"])
          S.op("pool", lambda e: e.memset(onesb[:], 1.0), [], ["onesb"])
          CP("dve", blkb[:], cv_("blk"), ["cst"], ["blkb"])
          TS("dve", omka[:], vecs[:, V["k_a"]:V["k_a"] + 8], -1.0, 1.0, ALU.mult, ALU.add, ["vecs"], ["omka"])

          def vcol(name, c):
              return vecs[:, V[name] + c:V[name] + c + 1]

          def prenorm(es, src, n, gname, hT, hoff, tag):
              xc = [sbt(es, "xc%s%d" % (tag, i), [128, n], F32) for i in range(2)]
              sq = [sbt(es, "sq%s%d" % (tag, i), [128, n], BF16) for i in range(2)]
              rs = sbt(es, "rs%s" % tag, [128, n], F32)
              bl = blocks(n)
              for c in range(KC):
                  x_ = xc[c % 2]
                  q_ = sq[c % 2]
                  DMA("sp", x_[:], src[c * 128:(c + 1) * 128, :], [], [x_.name])
                  ACT(q_[:], x_[:], AF.Square, [x_.name], [q_.name])
                  for bi, (t0, nn) in enumerate(bl):
                      MM(ps[bi][:, :nn], onesb[:], q_[:, t0:t0 + nn], c == 0, c == KC - 1, ["onesb", q_.name], [PSN[bi]])
              for bi, (t0, nn) in enumerate(bl):
                  ACT(rs[:, t0:t0 + nn], ps[bi][:, :nn], AF.Sqrt, [PSN[bi]], [rs.name], bias=RMS_EPS, scale=1.0 / D)
              S.op("dve", lambda e: e.reciprocal(out=rs[:], in_=rs[:]), [rs.name], [rs.name])
              for c in range(KC):
                  x_ = xc[c % 2]
                  DMA("sp", x_[:], src[c * 128:(c + 1) * 128, :], [], [x_.name])
                  STT(hT[:, c, hoff:hoff + n], x_[:], vcol(gname, c), rs[:], ALU.mult, ALU.mult,
                      [x_.name, rs.name, "vecs"], [hT.name])

          wctr = [0]

          def load_w(wbufs, src):
              b = wbufs[wctr[0] % len(wbufs)]
              wctr[0] += 1
              n1, n2 = b.shape[1], b.shape[2]
              if n1 * n2 <= 2048:
                  DMA("pool", b[:], src, [], [b.name])
              else:
                  g = max(1, 2048 // n2)
                  for k0 in range(0, n1, g):
                      k1 = min(n1, k0 + g)
                      DMA("pool", b[:, k0:k1, :], src[:, k0:k1, :], [], [b.name])
              return b

          lstack = ExitStack()
          Hf = sbt(lstack, "Hf", [128, 8, 64], F32)
          S.op("pool", lambda e: e.memset(Hf[:], 0.0), [], ["Hf"])
          lw = sbt(lstack, "lw", [128, 1024], BF16)
          la = sbt(lstack, "la", [128, 1024], BF16)
          lg = sbt(lstack, "lg", [128, 2, 1024], BF16)
          DMA("pool", lw[:], w_lora_l, [], ["lw"])
          DMA("pool", la[:], a_lora_l, [], ["la"])
          DMA("pool", lg[:], g_lora_l, [], ["lg"])

          def mix_phase(full):
              es = ExitStack()
              tg = "M" if full else "P"
              n = NCH if full else 896
              ntile = n // 128
              npt = 9 if full else 7
              tok_lo = CH0 if full else 0
              hoff = 768 if full else 0
              hn = 1408 if full else 896
              hT = sbt(es, "hT" + tg, [128, KC, hn], BF16)
              with ExitStack() as e2:
                  prenorm(e2, xT[:, hoff:hoff + hn], hn, "norm_mix_pre", hT, 0, tg)
              dump("hT" + tg, hT[:, 0:2, :], [hT.name], BF16)
              ck("prenorm" + tg)
              S.barrier()
              wb = [sbt(es, "wb%s%d" % (tg, i), [128, KC, 128], BF16) for i in range(4)]
              if full:
                  pbl = [(895, 512), (1407, 512), (1919, 257)]
                  pbase = 895
              else:
                  pbl = [(0, 512), (512, 384)]
                  pbase = -1

              def project(w, prbuf, psl):
                  for bi, (t0, nn) in enumerate(pbl):
                      p_ = psl[bi % len(psl)]
                      for kc in range(KC):
                          MM(ps[p_][:, :nn], w[:, kc, :], hT[:, kc, t0 - hoff:t0 - hoff + nn], kc == 0, kc == KC - 1,
                             [w.name, hT.name], [PSN[p_]])
                      CP("act", prbuf[:, t0 - pbase:t0 - pbase + nn], ps[p_][:, :nn], [PSN[p_]], [prbuf.name])

              T = [sbt(es, "T%s%d" % (tg, i), [128, n], F32) for i in range(6)]
              dtmp = T[5]
              prevS = sbt(es, "prevS" + tg, [128, 16, 8], F32)
              shs = sbt(es, "shs" + tg, [128, 28, 16], F32)
              nshs = sbt(es, "nshs" + tg, [128, 28, 17], F32)
              if full:
                  DMA("sp", shs[:], shiftst, [], [shs.name])

              def shiftmix(prbuf, chunk):
                  mu = vcol("mu", chunk)
                  npr = npt * 128
                  if full:
                      cur_s = prbuf[:, 1 + npr:1 + n].rearrange("p (s t) -> p s t", t=8)
                      CP("pool", nshs[:, chunk, 0:1], prbuf[:, npr:npr + 1], [prbuf.name], [nshs.name])
                      CP("pool", nshs[:, chunk, 1:17], cur_s[:, :, 7], [prbuf.name], [nshs.name])
                      CP("pool", prevS[:, :, 1:8], cur_s[:, :, 0:7], [prbuf.name], [prevS.name])
                      CP("pool", prevS[:, :, 0:1], shs[:, chunk, :].unsqueeze(2), [shs.name], [prevS.name])
                      ds = dtmp[:, npr:n].rearrange("p (s t) -> p s t", t=8)
                      TT("dve", ds, prevS[:], cur_s, ALU.subtract, [prevS.name, prbuf.name], [dtmp.name])
                  TT("dve", dtmp[:, 0:npr], prbuf[:, 0:npr], prbuf[:, 1:1 + npr], ALU.subtract, [prbuf.name], [dtmp.name])
                  STT(prbuf[:, 1:1 + n], dtmp[:], mu, prbuf[:, 1:1 + n], ALU.mult, ALU.add,
                      [dtmp.name, prbuf.name, "vecs"], [prbuf.name])

              pr = [sbt(es, "pr%s%d" % (tg, i), [128, n + 1], F32) for i in range(3)]
              if not full:
                  for i in range(3):
                      S.op("pool", lambda e, i=i: e.memset(pr[i][:, 0:1], 0.0), [], [pr[i].name])

              if full:
                  ec = ExitStack()
                  glu2 = [sbt(ec, "glu%d" % i, [128, 1408], F32) for i in range(2)]
                  glub2 = [sbt(ec, "glub%d" % i, [128, 1408], BF16) for i in range(2)]
                  sg2 = [sbt(ec, "sgc%d" % i, [128, 512], F32) for i in range(2)]
                  fullS2 = [sbt(ec, "fullS%d" % i, [128, 16, 38], F32) for i in range(2)]
                  fullSb2 = [sbt(ec, "fullSb%d" % i, [128, 16, 38], BF16) for i in range(2)]
                  dg2 = [sbt(ec, "dg%d" % i, [128, 31, 128], BF16) for i in range(2)]
                  cvp = sbt(ec, "cvp", [128, 8, NCH], BF16)
                  cbl = [(768, 512), (1280, 512), (1792, 384)]
                  for c in range(8):
                      glu, glub, fullS, fullSb, dg = glu2[c % 2], glub2[c % 2], fullS2[c % 2], fullSb2[c % 2], dg2[c % 2]
                      gN, gbN, fN, fbN, dN = glu.name, glub.name, fullS.name, fullSb.name, dg.name
                      wv = load_w(wb, w_in_l[c])
                      wg = load_w(wb, w_in_l[8 + c])
                      for bq, (t0, nn) in enumerate(cbl):
                          sg = sg2[bq % 2]
                          for kc in range(KC):
                              MM(ps[0][:, :nn], wv[:, kc, :], hT[:, kc, t0 - 768:t0 - 768 + nn], kc == 0, kc == KC - 1,
                                 [wv.name, hT.name], ["ps0"])
                          for kc in range(KC):
                              MM(ps[1][:, :nn], wg[:, kc, :], hT[:, kc, t0 - 768:t0 - 768 + nn], kc == 0, kc == KC - 1,
                                 [wg.name, hT.name], ["ps1"])
                          ACT(sg[:, :nn], ps[1][:, :nn], AF.Sigmoid, ["ps1"], [sg.name])
                          TT("dve", glu[:, t0 - 768:t0 - 768 + nn], ps[0][:, :nn], sg[:, :nn], ALU.mult, ["ps0", sg.name], [gN])
                      CP("act", glub[:], glu[:], [gN], [gbN])
                      DMA("sp", ncp[c * 128:(c + 1) * 128, :], glu[:, 2018 - 768:2048 - 768], [gN], [])
                      DMA("sp", fullS[:, :, 0:30], convst[c * 128:(c + 1) * 128, :, :], [], [fN])
                      CP("pool", fullS[:, :, 30:38], glu[:, 1280:1408].rearrange("p (s t) -> p s t", t=8), [gN], [fN])
                      DMA("sp", ncs[c * 128:(c + 1) * 128, :, :], fullS[:, :, 8:38], [fN], [])
                      CP("act", fullSb[:], fullS[:], [fN], [fbN])
                      wcv = vecs[:, V["conv_dw"] + c * 31:V["conv_dw"] + (c + 1) * 31]
                      TT("dve", dg[:], identb[:].unsqueeze(1).to_broadcast([128, 31, 128]),
                         wcv.unsqueeze(2).to_broadcast([128, 31, 128]), ALU.mult, ["identb", "vecs"], [dN])
                      for (c0, nn) in [(0, 512), (512, 512), (1024, 128)]:
                          g0 = c0 + 128 - 30
                          for j in range(31):
                              MM(ps[2][:, :nn], dg[:, j, :], glub[:, g0 + j:g0 + j + nn], j == 0, j == 30, [dN, gbN], ["ps2"])
                          ACT(cvp[:, c, c0:c0 + nn], ps[2][:, :nn], AF.Identity, ["ps2", "vecs"], ["cvp"], bias=vcol("conv_dw_b", c))
                      for j in range(31):
                          MM(ps[3][:, :128], dg[:, j, :], fullSb[:, :, j:j + 8], j == 0, j == 30, [dN, fbN], ["ps3"])
                      ACT(cvp[:, c, 1152:1280], ps[3][:, :128], AF.Identity, ["ps3", "vecs"], ["cvp"], bias=vcol("conv_dw_b", c))
                  sqb = sbt(ec, "sqb", [128, 512], BF16)
                  mean = sbt(ec, "lnmean", [128, 512], F32)
                  rstd = sbt(ec, "lnrstd", [128, 512], F32)
                  tmpf = sbt(ec, "lntmp", [128, 512], F32)
                  cvo = sbt(ec, "cvo", [128, 512], BF16)
                  for (t0, nn) in blocks(NCH):
                      for c in range(8):
                          ACT(sqb[:, :nn], cvp[:, c, t0:t0 + nn], AF.Square, ["cvp"], ["sqb"])
                          MM(ps[4][:, :nn], onesb[:], cvp[:, c, t0:t0 + nn], c == 0, c == 7, ["onesb", "cvp"], ["ps4"])
                          MM(ps[5][:, :nn], onesb[:], sqb[:, :nn], c == 0, c == 7, ["onesb", "sqb"], ["ps5"])
                      ACT(mean[:, :nn], ps[4][:, :nn], AF.Copy, ["ps4"], ["lnmean"], scale=1.0 / 1024)
                      ACT(tmpf[:, :nn], ps[4][:, :nn], AF.Square, ["ps4"], ["lntmp"], scale=1.0 / 1024)
                      STT(rstd[:, :nn], ps[5][:, :nn], 1.0 / 1024, tmpf[:, :nn], ALU.mult, ALU.subtract, ["ps5", "lntmp"], ["lnrstd"])
                      ACT(rstd[:, :nn], rstd[:, :nn], AF.Sqrt, ["lnrstd"], ["lnrstd"], bias=LN_EPS)
                      S.op("dve", lambda e, nn=nn: e.reciprocal(out=rstd[:, :nn], in_=rstd[:, :nn]), ["lnrstd"], ["lnrstd"])
                      for c in range(8):
                          TT("dve", tmpf[:, :nn], cvp[:, c, t0:t0 + nn], mean[:, :nn], ALU.subtract, ["cvp", "lnmean"], ["lntmp"])
                          TT("dve", tmpf[:, :nn], tmpf[:, :nn], rstd[:, :nn], ALU.mult, ["lntmp", "lnrstd"], ["lntmp"])
                          ACT(cvo[:, :nn], tmpf[:, :nn], AF.Silu, ["lntmp", "vecs"], ["cvo"], bias=vcol("conv_ln_b", c),
                              scale=vcol("conv_ln_g", c))
                          DMA("sp", mixS[c * 128:(c + 1) * 128, t0:t0 + nn], cvo[:, :nn], ["cvo"], ["mixS"])
                  S.barrier()
                  ec.close()
                  ck("convM")

              twd = sbt(es, "twd" + tg, [128, n], BF16)
              adb = sbt(es, "adb" + tg, [128, n], BF16)
              sgd = sbt(es, "sgd" + tg, [128, 2, n], BF16)
              for q in ([40, 41, 42, 43] if full else [40, 41]):
                  w = load_w(wb, w_in_l[q])
                  project(w, pr[0], [0, 1])
                  shiftmix(pr[0], 24 + (q - 40))
                  if q == 40:
                      ACT(twd[:], pr[0][:, 1:1 + n], AF.Tanh, [pr[0].name], [twd.name])
                  elif q == 41:
                      CP("act", adb[:], pr[0][:, 1:1 + n], [pr[0].name], [adb.name])
                  else:
                      ACT(sgd[:, q - 42, :], pr[0][:, 1:1 + n], AF.Sigmoid, [pr[0].name], [sgd.name])
              dump("twd" + tg, twd[:], [twd.name], BF16)
              ck("lora" + tg)

              Tb = sbt(es, "Tb" + tg, [128, n], BF16)
              PAR = sbt(es, "PAR" + tg, [128, ntile, 2, 128], BF16)
              Pb = sbt(es, "Pb" + tg, [128, n], BF16)
              Pk = sbt(es, "Pk" + tg, [128, n], BF16)
              Pv = sbt(es, "Pv" + tg, [128, n], BF16)
              yTb = sbt(es, "yT" + tg, [128, n], F32)
              NBT = 4 if full else 7
              NI = 2 * NBT
              Q = sbt(es, "Q" + tg, [128, NI, 128], BF16)
              QT = sbt(es, "QT" + tg, [128, NI, 128], BF16)
              IQ = sbt(es, "IQ" + tg, [128, NI, 128], BF16)
              Tt = [sbt(es, "Tt%s%d" % (tg, i), [128, NI, 128], BF16) for i in range(2)]
              Aak = sbt(es, "Aak" + tg, [128, NI, 128], BF16)
              Arb = sbt(es, "Arb" + tg, [128, NI, 128], BF16)
              Ark = sbt(es, "Ark" + tg, [128, NI, 128], BF16)

              def gk(buf, gi):
                  return "%s_g%d" % (buf.name, gi)
              tkm = sbt(es, "tkm" + tg, [128, 3, 128], BF16)
              Xb = sbt(es, "Xb" + tg, [128, 128], BF16)
              Ub = sbt(es, "Ub" + tg, [128, 128], BF16)
              Hb = sbt(es, "Hb" + tg, [128, 128], BF16)
              S.op("pool", lambda e: e.memset(Hb[:], 0.0), [], [Hb.name])
              HG = sbt(es, "HG" + tg, [128, 64], F32)
              if full:
                  Hs = sbt(es, "Hs", [128, 16, 64], F32)
                  Hsb = sbt(es, "Hsb", [128, 16, 128], BF16)
                  S.op("pool", lambda e: e.memset(Hsb[:], 0.0), [], ["Hsb"])
                  am = sbt(es, "am", [128, 16, 128], BF16)
                  rm = sbt(es, "rm", [128, 16, 128], BF16)
                  Ue = sbt(es, "Ue", [128, 16, 128], BF16)
                  Ve = sbt(es, "Ve", [128, 16, 128], BF16)
                  S.op("pool", lambda e: e.memset(am[:], 0.0), [], ["am"])
                  S.op("pool", lambda e: e.memset(rm[:], 0.0), [], ["rm"])
              scanm = cv_("scan")
              for c in range(8):
                  wr_ = load_w(wb, w_in_l[16 + c]) if full else None
                  wk_ = load_w(wb, w_in_l[24 + c])
                  wv_ = load_w(wb, w_in_l[32 + c])
                  if full:
                      project(wr_, pr[0], [0, 1])
                      shiftmix(pr[0], c)
                  project(wk_, pr[1], [2, 3])
                  shiftmix(pr[1], 8 + c)
                  project(wv_, pr[2], [0, 1])
                  shiftmix(pr[2], 16 + c)
                  xr = pr[0][:, 1:1 + n]
                  xk = pr[1][:, 1:1 + n]
                  xv = pr[2][:, 1:1 + n]
                  T1, T2, T3, T4, T5, T6 = [t[:] for t in T]
                  n1, n2, n3, n4, n5, n6 = [t.name for t in T]
                  bl = blocks(n)
                  for (t0, nn) in bl:
                      MM(ps[4][:, :nn], lw[0:96, c * 128:(c + 1) * 128], twd[0:96, t0:t0 + nn], True, True, ["lw", twd.name], ["ps4"])
                      ACT(T1[:, t0:t0 + nn], ps[4][:, :nn], AF.Sigmoid, ["ps4", "vecs"], [n1], bias=vcol("w0", c))
                      MM(ps[5][:, :nn], la[0:96, c * 128:(c + 1) * 128], adb[0:96, t0:t0 + nn], True, True, ["la", adb.name], ["ps5"])
                      ACT(T4[:, t0:t0 + nn], ps[5][:, :nn], AF.Sigmoid, ["ps5", "vecs"], [n4], bias=vcol("a0", c))
                  S.op("dve", lambda e: e.tensor_tensor_scan(out=T2, data0=scanm[:, 0:n], data1=T1, initial=0.0,
                                                             op0=ALU.mult, op1=ALU.add), ["cst", n1], [n2])
                  TT("dve", T1, T2, T1, ALU.subtract, [n1, n2], [n1])
                  ACT(T1, T1, AF.Exp, [n1], [n1], scale=-LDK)
                  ACT(T3, T2, AF.Exp, [n2], [n3], scale=-LDK)
                  ACT(T2, T2, AF.Exp, [n2], [n2], scale=LDK)
                  ACT(Tb[:], xk, AF.Square, [pr[1].name, "vecs"], [Tb.name], scale=vcol("k_k", c))
                  for (t0, nn) in bl:
                      MM(ps[6][:, :nn], blkb[:], Tb[:, t0:t0 + nn], True, True, ["blkb", Tb.name], ["ps6"])
                      TS("dve", T5[:, t0:t0 + nn], ps[6][:, :nn], 1e-24, None, ALU.max, None, ["ps6"], [n5])
                  ACT(T5, T5, AF.Sqrt, [n5], [n5])
                  S.op("dve", lambda e: e.reciprocal(out=T5, in_=T5), [n5], [n5])
                  STT(T5, xk, vcol("k_k", c), T5, ALU.mult, ALU.mult, [pr[1].name, n5, "vecs"], [n5])
                  TS("dve", T6, T4, vcol("k_a", c), omka[:, c:c + 1], ALU.mult, ALU.add, [n4, "vecs", "omka"], [n6])
                  TT("dve", T6, T6, xk, ALU.mult, [n6, pr[1].name], [n6])
                  PARa = PAR[:, :, 0, :]
                  PARr = PAR[:, :, 1, :]
                  STT(PARa, T5.rearrange("p (a b) -> p a b", b=128), -1.0, T1.rearrange("p (a b) -> p a b", b=128),
                      ALU.mult, ALU.mult, [n5, n1], [PAR.name])
                  TT("dve", T4, T5, T4, ALU.mult, [n5, n4], [n4])
                  TT("dve", Pb[:], T4, T2, ALU.mult, [n4, n2], [Pb.name])
                  TT("dve", Pk[:], T6, T2, ALU.mult, [n6, n2], [Pk.name])
                  CP("act", Pv[:], xv, [pr[2].name], [Pv.name])
                  if full:
                      TT("dve", PARr, xr.rearrange("p (a b) -> p a b", b=128), T3.rearrange("p (a b) -> p a b", b=128),
                         ALU.mult, [pr[0].name, n3], [PAR.name])
                      STT(Tb[:], xr, vcol("r_k", c), T6, ALU.mult, ALU.mult, [pr[0].name, n6, "vecs"], [Tb.name])
                      for (t0, nn) in bl:
                          MM(ps[6][:, :nn], blkb[:], Tb[:, t0:t0 + nn], True, True, ["blkb", Tb.name], ["ps6"])
                          TT("dve", T6[:, t0:t0 + nn], ps[6][:, :nn], xv[:, t0:t0 + nn], ALU.mult, ["ps6", pr[2].name], [n6])
                      for (t0, nn) in bl:
                          MM(ps[7][:, :nn], lg[:, 0, c * 128:(c + 1) * 128], sgd[:, 0, t0:t0 + nn], True, False, ["lg", sgd.name], ["ps7"])
                          MM(ps[7][:, :nn], lg[:, 1, c * 128:(c + 1) * 128], sgd[:, 1, t0:t0 + nn], False, True, ["lg", sgd.name], ["ps7"])
                          CP("act", T1[:, t0:t0 + nn], ps[7][:, :nn], ["ps7"], [n1])
                      DMA("sp", Hs[:], wkvT[c], [], ["Hs"])
                      CP("act", Hsb[0:64, :, 0:64], Hs[0:64, :, :], ["Hs"], ["Hsb"])
                      CP("act", Hsb[64:128, :, 64:128], Hs[64:128, :, :], ["Hs"], ["Hsb"])
                      for s_ in range(16):
                          CP("pool", am[:, s_, 8 * s_:8 * s_ + 8], PAR[:, ntile - 1, 0, 8 * s_:8 * s_ + 8], [PAR.name], ["am"])
                          CP("pool", rm[:, s_, 8 * s_:8 * s_ + 8], PAR[:, ntile - 1, 1, 8 * s_:8 * s_ + 8], [PAR.name], ["rm"])
                  CP("act", Hb[0:64, 0:64], Hf[0:64, c, :], ["Hf"], [Hb.name])
                  CP("act", Hb[64:128, 64:128], Hf[64:128, c, :], ["Hf"], [Hb.name])
                  if c == 0:
                      dump("PAR" + tg, PAR[:], [PAR.name], BF16)
                      dump("Pb" + tg, Pb[:], [Pb.name], BF16)
                      dump("Pk" + tg, Pk[:], [Pk.name], BF16)
                      dump("T3" + tg, T[2][:], [T[2].name])
                      ck("prep" + tg)

                  for g0 in range(0, ntile, NBT):
                      tiles = [t_ for t_ in range(g0, min(ntile, g0 + NBT))]
                      items = [(t_, hh) for t_ in tiles for hh in (0, 1)]
                      for ii, (t_, hh) in enumerate(items):
                          gi = ii // 4
                          smp = full and t_ == ntile - 1
                          sfx = "_bd" if smp else ""
                          hs = slice(64 * hh, 64 * hh + 64)
                          cs = slice(t_ * 128, t_ * 128 + 128)
                          pa = ii % 2
                          ncol = 256 if full else 128
                          rhs_ar = PAR[hs, t_, :, :].rearrange("p a b -> p (a b)")[:, 0:ncol]
                          MM(ps[pa][:, 0:ncol], Pb[hs, cs], rhs_ar, True, True, [Pb.name, PAR.name], [PSN[pa]])
                          MM(ps[2 + pa][:, 0:ncol], Pk[hs, cs], rhs_ar, True, True, [Pk.name, PAR.name], [PSN[2 + pa]])
                          MM(ps[4 + pa][:, 0:128], PAR[hs, t_, 0, :], Pb[hs, cs], True, True, [Pb.name, PAR.name], [PSN[4 + pa]])
                          TT("dve", QT[:, ii, :], ps[pa][:, 0:128], cv_("su" + sfx), ALU.mult, [PSN[pa], "cst"], [gk(QT, gi)])
                          TT("dve", Aak[:, ii, :], ps[2 + pa][:, 0:128], cv_("su" + sfx), ALU.mult, [PSN[2 + pa], "cst"], [gk(Aak, gi)])
                          TT("dve", Q[:, ii, :], ps[4 + pa][:, 0:128], cv_("sl" + sfx), ALU.mult, [PSN[4 + pa], "cst"], [gk(Q, gi)])
                          if full:
                              TT("dve", Arb[:, ii, :], ps[pa][:, 128:256], cv_("iu" + sfx), ALU.mult, [PSN[pa], "cst"], [gk(Arb, gi)])
                              TT("dve", Ark[:, ii, :], ps[2 + pa][:, 128:256], cv_("iu" + sfx), ALU.mult, [PSN[2 + pa], "cst"], [gk(Ark, gi)])
                      ni_all = len(items)
                      groups = [(gi, gi * 4, min(4, ni_all - gi * 4)) for gi in range((ni_all + 3) // 4)]
                      for gi, i0, ni in groups:
                          TT("dve", Tt[0][:, i0:i0 + ni, :], QT[:, i0:i0 + ni, :], ident4[:, 0:ni, :], ALU.add,
                             [gk(QT, gi), "ident4"], [gk(Tt[0], gi)])
                      cur = 0
                      nlev = 7
                      for k in range(1, nlev):
                          for gi, i0, ni in groups:
                              pb = 3 * (gi % 2)
                              for ii in range(ni):
                                  MM(ps[pb][:, ii * 128:(ii + 1) * 128], QT[:, i0 + ii, :], Q[:, i0 + ii, :], True, True,
                                     [gk(QT, gi), gk(Q, gi)], [PSN[pb]])
                              if k < nlev - 1:
                                  for ii in range(ni):
                                      MM(ps[pb + 1][:, ii * 128:(ii + 1) * 128], Q[:, i0 + ii, :], QT[:, i0 + ii, :], True, True,
                                         [gk(QT, gi), gk(Q, gi)], [PSN[pb + 1]])
                              CP("act", Q[:, i0:i0 + ni, :], ps[pb][:, 0:ni * 128].rearrange("p (a b) -> p a b", b=128), [PSN[pb]], [gk(Q, gi)])
                              if k < nlev - 1:
                                  CP("act" if gi % 2 else "dve", QT[:, i0:i0 + ni, :], ps[pb + 1][:, 0:ni * 128].rearrange("p (a b) -> p a b", b=128),
                                     [PSN[pb + 1]], [gk(QT, gi)])
                              for ii in range(ni):
                                  MM(ps[pb + 2][:, ii * 128:(ii + 1) * 128], Q[:, i0 + ii, :], Tt[cur][:, i0 + ii, :], True, True,
                                     [gk(Q, gi), gk(Tt[cur], gi)], [PSN[pb + 2]])
                              TT("dve", Tt[1 - cur][:, i0:i0 + ni, :], ps[pb + 2][:, 0:ni * 128].rearrange("p (a b) -> p a b", b=128),
                                 Tt[cur][:, i0:i0 + ni, :], ALU.add, [PSN[pb + 2], gk(Tt[cur], gi)], [gk(Tt[1 - cur], gi)])
                          cur = 1 - cur
                      TTf = Tt[cur]
                      if c == 0 and g0 == 0:
                          dump("TTf" + tg, TTf[:, 0:4, :], [gk(TTf, 0)], BF16)
                          dump("Aak" + tg, Aak[:, 0:4, :], [gk(Aak, 0)], BF16)
                          ck("tinv" + tg)
                      for ti, t_ in enumerate(tiles):
                          smp = full and t_ == ntile - 1
                          cs = slice(t_ * 128, t_ * 128 + 128)
                          pT = ps[3].bitcast(BF16)
                          for q_, src_ in enumerate((Pb, Pk, Pv)):
                              S.op("pe", lambda e, q_=q_, src_=src_, cs=cs, pT=pT: e.transpose(out=pT[:, q_ * 128:(q_ + 1) * 128], in_=src_[:, cs],
                                                                                identity=identb[:]), [src_.name, "identb"], ["ps3"])
                          CP("act", tkm[:], pT[:, 0:384].rearrange("p (a b) -> p a b", b=128), ["ps3"], [tkm.name])
                          bT, kT_, vT = tkm[:, 0, :], tkm[:, 1, :], tkm[:, 2, :]
                          if not smp:
                              MM(ps[4][:, 0:128], PAR[:, t_, 0, :], Hb[:], True, False, [PAR.name, Hb.name], ["ps4"])
                          else:
                              for s_ in range(16):
                                  MM(ps[4][:, 0:128], am[:, s_, :], Hsb[:, s_, :], s_ == 0, False, ["am", "Hsb"], ["ps4"])
                          for hh in (0, 1):
                              ii = ti * 2 + hh
                              hs = slice(64 * hh, 64 * hh + 64)
                              MM(ps[4][:, hh * 64:hh * 64 + 64], Aak[:, ii, :], vT[:, hs], False, hh == 1, [gk(Aak, ii // 4), tkm.name], ["ps4"])
                          CP("act", Xb[:], ps[4][:, 0:128], ["ps4"], [Xb.name])
                          if c == 0 and t_ == DBG_TILE:
                              dump("Hbx" + tg, Hb[:], [Hb.name], BF16)
                              dump("PARx" + tg, PAR[:, t_, 0, :], [PAR.name], BF16)
                          for hh in (0, 1):
                              ii = ti * 2 + hh
                              hs = slice(64 * hh, 64 * hh + 64)
                              MM(ps[5][:, hh * 64:hh * 64 + 64], TTf[:, ii, :], Xb[:, hs], True, True, [gk(TTf, ii // 4), Xb.name], ["ps5"])
                          CP("dve", Ub[:], ps[5][:, 0:128], ["ps5"], [Ub.name])
                          if full:
                              for hh in (0, 1):
                                  ii = ti * 2 + hh
                                  hs = slice(64 * hh, 64 * hh + 64)
                                  if not smp:
                                      MM(ps[6][hs, 0:128], Hb[:, hs], PAR[:, t_, 1, :], True, False, [Hb.name, PAR.name], ["ps6"])
                                  else:
                                      for s_ in range(16):
                                          MM(ps[6][hs, 0:128], Hsb[:, s_, hs], rm[:, s_, :], s_ == 0, False, ["Hsb", "rm"], ["ps6"])
                                  MM(ps[6][hs, 0:128], Ub[:, hs], Arb[:, ii, :], False, False, [Ub.name, gk(Arb, ii // 4)], ["ps6"])
                                  MM(ps[6][hs, 0:128], vT[:, hs], Ark[:, ii, :], False, True, [tkm.name, gk(Ark, ii // 4)], ["ps6"])
                              CP("act", yTb[:, cs], ps[6][:, 0:128], ["ps6"], [yTb.name])
                          gam = T[2][:, t_ * 128 + 127:t_ * 128 + 128]
                          if not smp:
                              for hh in (0, 1):
                                  hs = slice(64 * hh, 64 * hh + 64)
                                  MM(ps[7][hs, 0:64], bT[:, hs], Ub[:, hs], True, False, [tkm.name, Ub.name], ["ps7"])
                                  MM(ps[7][hs, 0:64], kT_[:, hs], vT[:, hs], False, True, [tkm.name], ["ps7"])
                              ACT(HG[:], Hf[:, c, :], AF.Copy, ["Hf", n3], [HG.name], scale=gam)
                              STT(Hf[:, c, :], ps[7][:, 0:64], gam, HG[:], ALU.mult, ALU.add, ["ps7", HG.name, n3], ["Hf"])
                              CP("act", Hb[0:64, 0:64], Hf[0:64, c, :], ["Hf"], [Hb.name])
                              CP("act", Hb[64:128, 64:128], Hf[64:128, c, :], ["Hf"], [Hb.name])
                              if c == 0 and t_ == DBG_TILE - 1:
                                  dump("HfE" + tg, Hf[:, 0, :], ["Hf"])
                                  dump("HbE" + tg, Hb[:], [Hb.name], BF16)
                                  dump("HGE" + tg, HG[:], [HG.name])
                              if c == 0 and t_ == DBG_TILE:
                                  dump("tkm" + tg, tkm[:], [tkm.name], BF16)
                                  dump("Xb" + tg, Xb[:], [Xb.name], BF16)
                                  dump("Ub" + tg, Ub[:], [Ub.name], BF16)
                                  dump("Hf0" + tg, Hf[:, 0, :], ["Hf"])
                                  ck("tile0" + tg)
                          else:
                              seqm = cv_("seqm")
                              seqb = seqm.unsqueeze(2).to_broadcast([128, 16, 128])
                              TT("dve", Ue[:], Ub[:].unsqueeze(1).to_broadcast([128, 16, 128]), seqb, ALU.mult, [Ub.name, "cst"], ["Ue"])
                              TT("dve", Ve[:], vT.unsqueeze(1).to_broadcast([128, 16, 128]), seqb, ALU.mult, [tkm.name, "cst"], ["Ve"])
                              gs = T[2][:, (ntile - 1) * 128:n].rearrange("p (s t) -> p s t", t=8)[:, :, 7:8]
                              HsG = T[1][:, 0:1024].rearrange("p (s i) -> p s i", i=64)
                              TT("dve", HsG, Hs[:], gs.to_broadcast([128, 16, 64]), ALU.mult, ["Hs", n3], [n2])
                              for half in (0, 1):
                                  pz = ps[half]
                                  for hh in (0, 1):
                                      hs = slice(64 * hh, 64 * hh + 64)
                                      MM(pz[hs, :], bT[:, hs], Ue[:, 8 * half:8 * half + 8, hs], True, False, [tkm.name, "Ue"], [PSN[half]])
                                      MM(pz[hs, :], kT_[:, hs], Ve[:, 8 * half:8 * half + 8, hs], False, True, [tkm.name, "Ve"], [PSN[half]])
                                  sl = slice(8 * half, 8 * half + 8)
                                  TT("dve", Hs[:, sl, :], pz[:, :].rearrange("p (s i) -> p s i", i=64),
                                     gs[:, sl, :].to_broadcast([128, 8, 64]), ALU.mult, [PSN[half], n3], ["Hs"])
                              TT("dve", Hs[:], Hs[:], HsG, ALU.add, ["Hs", n2], ["Hs"])
                              DMA("sp", nws[c], Hs[:], ["Hs"], [])
                  if c == 0:
                      dump("Hf" + tg, Hf[:, 0, :], ["Hf"])
                      if full:
                          dump("yTb", yTb[:], [yTb.name])
                      ck("pair0" + tg)
                  if full:
                      DMA("sp", nwp[c], Hf[:, c, :], ["Hf"], [])
                      T1, T6 = T[0][:], T[5][:]
                      gns, gnr, rwo = T[1], T[3], Pb
                      ACT(Tb[:], yTb[:], AF.Square, [yTb.name], [Tb.name])
                      CP("act", Pv[:], yTb[:], [yTb.name], [Pv.name])
                      for (t0, nn) in bl:
                          MM(ps[4][:, :nn], blkb[:], Pv[:, t0:t0 + nn], True, True, ["blkb", Pv.name], ["ps4"])
                          MM(ps[5][:, :nn], blkb[:], Tb[:, t0:t0 + nn], True, True, ["blkb", Tb.name], ["ps5"])
                          ACT(gns[:, :nn], ps[4][:, :nn], AF.Square, ["ps4"], [T[1].name], scale=1.0 / 64)
                          STT(gnr[:, :nn], ps[5][:, :nn], 1.0 / 64, gns[:, :nn], ALU.mult, ALU.subtract, ["ps5", T[1].name], [T[3].name])
                          ACT(gnr[:, :nn], gnr[:, :nn], AF.Sqrt, [T[3].name], [T[3].name], bias=GN_EPS)
                          S.op("dve", lambda e, nn=nn: e.reciprocal(out=gnr[:, :nn], in_=gnr[:, :nn]), [T[3].name], [T[3].name])
                          ysl = yTb[:, t0:t0 + nn]
                          STT(ysl, ps[4][:, :nn], -1.0 / 64, ysl, ALU.mult, ALU.add, ["ps4", yTb.name], [yTb.name])
                          TT("dve", ysl, ysl, gnr[:, :nn], ALU.mult, [yTb.name, T[3].name], [yTb.name])
                          TS("dve", ysl, ysl, vcol("ln_x_g", c), vcol("ln_x_b", c), ALU.mult, ALU.add, [yTb.name, "vecs"], [yTb.name])
                          TT("dve", ysl, ysl, T6[:, t0:t0 + nn], ALU.add, [yTb.name, T[5].name], [yTb.name])
                          TT("dve", rwo[:, t0:t0 + nn], ysl, T1[:, t0:t0 + nn], ALU.mult, [yTb.name, T[0].name], [Pb.name])
                      DMA("sp", mixS[1024 + c * 128:1024 + (c + 1) * 128, :], Pb[:], [Pb.name], ["mixS"])
              if full:
                  DMA("sp", nsh, nshs[:], [nshs.name], [])
              S.barrier()
              es.close()

          if not SKIP_MIX:
              mix_phase(False)
              ck("mixP")
              mix_phase(True)
          lstack.close()
          if not SKIP_MIX:
              dump("mixS", mixS, ["mixS"], BF16)
          ck("mixdone")

          def boundary(es, produce, xsrc, gpost, gpre, xdst, hT, final=False):
              sT = sbt(es, "sT", [128, KC, NCH], BF16)
              sq = sbt(es, "bsq", [128, 512], BF16)
              rs = sbt(es, "brs", [128, NCH], F32)
              xc = [sbt(es, "bxc%d" % i, [128, NCH], F32) for i in range(2)]
              sq2 = sbt(es, "bsq2", [128, NCH], BF16)
              bl = blocks(NCH)
              for c in range(KC):
                  def consume(pi, t0, nn, c=c):
                      CP("act", sT[:, c, t0:t0 + nn], ps[pi][:, :nn], [PSN[pi]], ["sT"])
                      ACT(sq[:, :nn], ps[pi][:, :nn], AF.Square, [PSN[pi]], ["bsq"])
                      bi = t0 // 512
                      MM(ps[5 + bi][:, :nn], onesb[:], sq[:, :nn], c == 0, c == KC - 1, ["onesb", "bsq"], [PSN[5 + bi]])
                  produce(c, consume)
              for bi, (t0, nn) in enumerate(bl):
                  ACT(rs[:, t0:t0 + nn], ps[5 + bi][:, :nn], AF.Sqrt, [PSN[5 + bi]], ["brs"], bias=RMS_EPS, scale=1.0 / D)
              S.op("dve", lambda e: e.reciprocal(out=rs[:], in_=rs[:]), ["brs"], ["brs"])
              for c in range(KC):
                  x_ = xc[c % 2]
                  DMA("sp", x_[:], xsrc[c * 128:(c + 1) * 128, :], [], [x_.name])
                  TT("dve", sq2[:], sT[:, c, :], rs[:], ALU.mult, ["sT", "brs"], ["bsq2"])
                  STT(x_[:], sq2[:], vcol(gpost, c), x_[:], ALU.mult, ALU.add, ["bsq2", x_.name, "vecs"], [x_.name])
                  DMA("sp", xdst[c * 128:(c + 1) * 128, :], x_[:], [x_.name], ["xdst"])
                  if not final:
                      ACT(sq2[:], x_[:], AF.Square, [x_.name], ["bsq2"])
                      for bi, (t0, nn) in enumerate(bl):
                          MM(ps[5 + bi][:, :nn], onesb[:], sq2[:, t0:t0 + nn], c == 0, c == KC - 1, ["onesb", "bsq2"], [PSN[5 + bi]])
              if final:
                  return
              for bi, (t0, nn) in enumerate(bl):
                  ACT(rs_keep[:, t0:t0 + nn], ps[5 + bi][:, :nn], AF.Sqrt, [PSN[5 + bi]], ["rs_keep"], bias=RMS_EPS, scale=1.0 / D)
              S.op("dve", lambda e: e.reciprocal(out=rs_keep[:], in_=rs_keep[:]), ["rs_keep"], ["rs_keep"])

          def prenorm_apply(es, hT, xsrc, gpre):
              xc = [sbt(es, "pxc%d" % i, [128, NCH], F32) for i in range(2)]
              for c in range(KC):
                  x_ = xc[c % 2]
                  DMA("sp", x_[:], xsrc[c * 128:(c + 1) * 128, :], ["xdst"], [x_.name])
                  STT(hT[:, c, :], x_[:], vcol(gpre, c), rs_keep[:], ALU.mult, ALU.mult, [x_.name, "rs_keep", "vecs"], [hT.name])

          rs_keep = sbt(top, "rs_keep", [128, NCH], F32)


          with ExitStack() as es:
           if not SKIP_MIX:
              wb = [sbt(es, "wbo%d" % i, [128, KC, 128], BF16) for i in range(3)]
              mx = sbt(es, "mx", [128, KC, NCH], BF16)
              for kc in range(KC):
                  DMA("sp", mx[:, kc, :], mixS[kc * 128:(kc + 1) * 128, :], ["mixS"], ["mx"])

              def prod(c, consume):
                  w = load_w(wb, w_out_l[c])
                  for bi, (t0, nn) in enumerate(blocks(NCH)):
                      pi = (c * 3 + bi) % 4
                      for kc in range(KC):
                          MM(ps[pi][:, :nn], w[:, kc, :], mx[:, kc, t0:t0 + nn], kc == 0, kc == KC - 1, [w.name, "mx"], [PSN[pi]])
                      consume(pi, t0, nn)
              boundary(es, prod, xT[:, CH0:NT], "norm_mix_post", "norm_xa_pre", x1T, None)
              S.barrier()
              dump("x1T", x1T, ["xdst"])
              ck("b1")

          with ExitStack() as eso:
            oT = sbt(eso, "oT", [128, KC, NCH], BF16)
            with ExitStack() as es:
              hT2 = sbt(es, "hT2", [128, KC, NCH], BF16)
              with ExitStack() as e2:
                  if not SKIP_MIX:
                      prenorm_apply(e2, hT2, x1T, "norm_xa_pre")
                  S.barrier()
              ck("pa1")
              kTp = sbt(es, "kTp", [128, KC, 256], BF16)
              vp = sbt(es, "vp", [128, 2, D], BF16)
              with ExitStack() as ea:
                  wbk = [sbt(ea, "wbk%d" % i, [128, KC, 128], BF16) for i in range(3)]
                  mnT = sbt(ea, "mnT", [128, KC, 256], BF16)
                  with ExitStack() as e2:
                      prenorm(e2, memT, 256, "norm_mem", mnT, 0, "mem")
                      S.barrier()
                  ck("mn")
                  ktf = sbt(ea, "ktf", [128, 256], F32)
                  vpf = sbt(ea, "vpf", [128, 512], F32)
                  for c in range(KC):
                      w = load_w(wbk, w_k_l[c])
                      for kc in range(KC):
                          MM(ps[0][:, :256], w[:, kc, :], mnT[:, kc, :], kc == 0, kc == KC - 1, [w.name, "mnT"], ["ps0"])
                      CP("act", kTp[:, c, :], ps[0][:, :256], ["ps0"], ["kTp"])
                      CP("dve", ktf[:], ps[0][:, :256], ["ps0"], ["ktf"])
                      DMA("sp", mkT[c * 128:(c + 1) * 128, :], ktf[:], ["ktf"], [])
                  ck("kproj")
                  wvb = [sbt(ea, "wvb%d" % i, [128, KC, 512], BF16) for i in range(2)]
                  for cb in range(4):
                      w = load_w(wvb, w_v_l[cb])
                      for mt in range(2):
                          for kc in range(KC):
                              MM(ps[1 + mt][:, :], mnT[:, kc, mt * 128:(mt + 1) * 128], w[:, kc, :], kc == 0, kc == KC - 1,
                                 [w.name, "mnT"], [PSN[1 + mt]])
                          CP("act", vp[:, mt, cb * 512:(cb + 1) * 512], ps[1 + mt][:, :], [PSN[1 + mt]], ["vp"])
                          CP("dve", vpf[:], ps[1 + mt][:, :], [PSN[1 + mt]], ["vpf"])
                          DMA("sp", mv[mt * 128:(mt + 1) * 128, cb * 512:(cb + 1) * 512], vpf[:], ["vpf"], [])
                  S.barrier()
              ck("memkv")
              wb = [sbt(es, "wba%d" % i, [128, KC, 128], BF16) for i in range(2)]
              qT = sbt(es, "qT", [128, KC, NCH], BF16)
              for c in range(KC):
                  w = load_w(wb, w_q_l[c])
                  for bi, (t0, nn) in enumerate(blocks(NCH)):
                      pi = 3 + (c * 3 + bi) % 3
                      for kc in range(KC):
                          MM(ps[pi][:, :nn], w[:, kc, :], hT2[:, kc, t0:t0 + nn], kc == 0, kc == KC - 1, [w.name, hT2.name], [PSN[pi]])
                      ACT(qT[:, c, t0:t0 + nn], ps[pi][:, :nn], AF.Copy, [PSN[pi]], ["qT"], scale=512.0 ** -0.5)
              mx8 = sbt(es, "mx8", [128, 8], F32)
              sm8 = sbt(es, "sm8", [128, 8], F32)
              att = sbt(es, "att", [128, 4, 256], BF16)
              attT = sbt(es, "attT", [128, 2, 4, 128], BF16)
              for t_ in range(9):
                  cs = slice(t_ * 128, (t_ + 1) * 128)
                  for h in range(4):
                      pi = h % 2
                      for dc in range(4):
                          MM(ps[pi][:, 0:256], qT[:, 4 * h + dc, cs], kTp[:, 4 * h + dc, :], dc == 0, dc == 3, ["qT", "kTp"], [PSN[pi]])
                      S.op("dve", lambda e, pi=pi, h=h: e.reduce_max(out=mx8[:, h:h + 1], in_=ps[pi][:, 0:256], axis=mybir.AxisListType.X),
                           [PSN[pi]], ["mx8"])
                      TS("dve", mx8[:, 4 + h:5 + h], mx8[:, h:h + 1], -1.0, None, ALU.mult, None, ["mx8"], ["mx8"])
                      S.op("act", lambda e, pi=pi, h=h: e.activation(out=att[:, h, :], in_=ps[pi][:, 0:256], func=AF.Exp,
                                                                    bias=mx8[:, 4 + h:5 + h], accum_out=sm8[:, h:h + 1]),
                           [PSN[pi], "mx8"], ["att", "sm8"])
                      S.op("dve", lambda e, h=h: e.reciprocal(out=sm8[:, 4 + h:5 + h], in_=sm8[:, h:h + 1]), ["sm8"], ["sm8"])
                      TS("dve", att[:, h, :], att[:, h, :], sm8[:, 4 + h:5 + h], None, ALU.mult, None, ["att", "sm8"], ["att"])
                  pT = ps[2].bitcast(BF16)
                  for h in range(4):
                      for mt in range(2):
                          S.op("pe", lambda e, h=h, mt=mt: e.transpose(out=pT[:, (mt * 4 + h) * 128:(mt * 4 + h + 1) * 128],
                                                                      in_=att[:, h, mt * 128:(mt + 1) * 128], identity=identb[:]),
                               ["att", "identb"], ["ps2"])
                  CP("act", attT[:], pT[:, 0:1024].rearrange("p (a b c) -> p a b c", a=2, b=4), ["ps2"], ["attT"])
                  for dc in range(KC):
                      h = dc // 4
                      pi = 3 + dc % 2
                      for mt in range(2):
                          MM(ps[pi][:, 0:128], vp[:, mt, dc * 128:(dc + 1) * 128], attT[:, mt, h, :], mt == 0, mt == 1, ["vp", "attT"], [PSN[pi]])
                      CP("act" if dc % 2 else "dve", oT[:, dc, cs], ps[pi][:, 0:128], [PSN[pi]], ["oT"])
              dump("oTp", oT[:, :, 0:1152], ["oT"], BF16)
              ck("attnP")
              kts = [sbt(es, "kts%d" % i, [128, KC, 256], BF16) for i in range(2)]
              vss = [sbt(es, "vss%d" % i, [128, 2, D], BF16) for i in range(2)]
              sc8 = sbt(es, "sc8", [8, 4, 256], F32)
              at8 = sbt(es, "at8", [8, 4, 256], BF16)
              m8 = sbt(es, "m8", [8, 8], F32)
              a8T = sbt(es, "a8T", [128, 8, 8], BF16)
              for s_ in range(16):
                  kt = kts[s_ % 2]
                  vv = vss[s_ % 2]
                  DMA("pool", kt[:], kTs[s_].rearrange("(c p) m -> p c m", p=128), [], [kt.name])
                  DMA("pool", vv[:], vs[s_].rearrange("(t p) d -> p t d", p=128), [], [vv.name])
                  cs = slice(1152 + 8 * s_, 1152 + 8 * s_ + 8)
                  for h in range(4):
                      pi = h % 2
                      for dc in range(4):
                          MM(ps[pi][0:8, 0:256], qT[:, 4 * h + dc, cs], kt[:, 4 * h + dc, :], dc == 0, dc == 3, ["qT", kt.name], [PSN[pi]])
                      CP("act", sc8[:, h, :], ps[pi][0:8, 0:256], [PSN[pi]], ["sc8"])
                  S.op("dve", lambda e: e.tensor_reduce(out=m8[:, 0:4], in_=sc8[:], axis=mybir.AxisListType.X, op=ALU.max), ["sc8"], ["m8"])
                  TT("dve", sc8[:], sc8[:], m8[:, 0:4].unsqueeze(2).to_broadcast([8, 4, 256]), ALU.subtract, ["sc8", "m8"], ["sc8"])
                  ACT(sc8[:], sc8[:], AF.Exp, ["sc8"], ["sc8"])
                  S.op("dve", lambda e: e.tensor_reduce(out=m8[:, 4:8], in_=sc8[:], axis=mybir.AxisListType.X, op=ALU.add), ["sc8"], ["m8"])
                  S.op("dve", lambda e: e.reciprocal(out=m8[:, 4:8], in_=m8[:, 4:8]), ["m8"], ["m8"])
                  TT("dve", at8[:], sc8[:], m8[:, 4:8].unsqueeze(2).to_broadcast([8, 4, 256]), ALU.mult, ["sc8", "m8"], ["at8"])
                  pT = ps[2].bitcast(BF16)
                  for h in range(4):
                      for mt in range(2):
                          S.op("pe", lambda e, h=h, mt=mt: e.transpose(out=pT[:, (mt * 4 + h) * 8:(mt * 4 + h + 1) * 8],
                                                                      in_=at8[:, h, mt * 128:(mt + 1) * 128], identity=identb[0:8, 0:8]),
                               ["at8", "identb"], ["ps2"])
                  CP("act", a8T[:], pT[:, 0:64].rearrange("p (a b) -> p a b", b=8), ["ps2"], ["a8T"])
                  for dc in range(KC):
                      h = dc // 4
                      for mt in range(2):
                          MM(ps[3][:, dc * 8:dc * 8 + 8], vv[:, mt, dc * 128:(dc + 1) * 128], a8T[:, mt * 4 + h, :], mt == 0, mt == 1,
                             [vv.name, "a8T"], ["ps3"])
                  CP("act", oT[:, :, cs], ps[3][:, 0:128].rearrange("p (a b) -> p a b", b=8), ["ps3"], ["oT"])
              S.barrier()
              dump("oT", oT[:], ["oT"], BF16)
              ck("attnS")
            with ExitStack() as es:
              wb = [sbt(es, "wbo2%d" % i, [128, KC, 128], BF16) for i in range(3)]

              def prod(c, consume):
                  w = load_w(wb, w_o_l[c])
                  for bi, (t0, nn) in enumerate(blocks(NCH)):
                      pi = (c * 3 + bi) % 4
                      for kc in range(KC):
                          MM(ps[pi][:, :nn], w[:, kc, :], oT[:, kc, t0:t0 + nn], kc == 0, kc == KC - 1, [w.name, "oT"], [PSN[pi]])
                      consume(pi, t0, nn)
              boundary(es, prod, x1T, "norm_xa_post", "norm_ffn_pre", x2T, None)
              S.barrier()
              dump("x2T", x2T, ["xdst"])
              ck("b2")

          with ExitStack() as es:
              actT = sbt(es, "actT", [128, NFC, NCH], BF16)
              eu = ExitStack()
              hT2 = sbt(eu, "hT2", [128, KC, NCH], BF16)
              with ExitStack() as e2:
                  prenorm_apply(e2, hT2, x2T, "norm_ffn_pre")
                  S.barrier()
              wb = [sbt(eu, "wbf%d" % i, [128, KC, 128], BF16) for i in range(4)]
              TS("dve", hT2[:, :, 0:128], hT2[:, :, 0:128], flg[:, 0:1], None, ALU.mult, None, [hT2.name, "flg"], [hT2.name])
              up = [sbt(eu, "up%d" % i, [128, 2 + 1152], F32) for i in range(2)]
              ups = [sbt(eu, "ups%d" % i, [128, 16, 10], F32) for i in range(2)]
              uc = [sbt(eu, "uc%d" % i, [128, NCH], F32) for i in range(2)]
              for i in range(2):
                  S.op("pool", lambda e, i=i: e.memset(up[i][:, 0:2], 0.0), [], [up[i].name])
              for c in range(NFC):
                  for vi, ch in enumerate((c, NFC + c)):
                      w = load_w(wb, w_up_l[ch])
                      u_, us_, uc_ = up[vi], ups[vi], uc[vi]
                      DMA("sp", us_[:, :, 0:2], ffnst[ch], [], [us_.name])
                      for bi, (t0, nn) in enumerate(blocks(NCH)):
                          pi = (vi * 3 + bi) % 4
                          for kc in range(KC):
                              MM(ps[pi][:, :nn], w[:, kc, :], hT2[:, kc, t0:t0 + nn], kc == 0, kc == KC - 1, [w.name, hT2.name], [PSN[pi]])
                          if t0 + nn <= 1152:
                              CP("act", u_[:, 2 + t0:2 + t0 + nn], ps[pi][:, :nn], [PSN[pi]], [u_.name])
                          else:
                              npm = 1152 - t0
                              CP("act", u_[:, 2 + t0:2 + 1152], ps[pi][:, :npm], [PSN[pi]], [u_.name])
                              CP("act", us_[:, :, 2:10], ps[pi][:, npm:nn].rearrange("p (s t) -> p s t", t=8), [PSN[pi]], [us_.name])
                      DMA("sp", nfp[ch], u_[:, 1152:1154], [u_.name], [])
                      DMA("sp", nfs[ch], us_[:, :, 8:10], [us_.name], [])
                      fw0 = V["ffn_dw"] + ch * 3
                      wj = [vecs[:, fw0 + j:fw0 + j + 1] for j in range(3)]
                      bj = vcol("ffn_dw_b", ch)
                      TS("dve", uc_[:, 0:1152], u_[:, 2:1154], wj[2], bj, ALU.mult, ALU.add, [u_.name, "vecs"], [uc_.name])
                      STT(uc_[:, 0:1152], u_[:, 1:1153], wj[1], uc_[:, 0:1152], ALU.mult, ALU.add, [u_.name, uc_.name, "vecs"], [uc_.name])
                      STT(uc_[:, 0:1152], u_[:, 0:1152], wj[0], uc_[:, 0:1152], ALU.mult, ALU.add, [u_.name, uc_.name, "vecs"], [uc_.name])
                      ucs = uc_[:, 1152:1280].rearrange("p (s t) -> p s t", t=8)
                      TS("dve", ucs, us_[:, :, 2:10], wj[2], bj, ALU.mult, ALU.add, [us_.name, "vecs"], [uc_.name])
                      STT(ucs, us_[:, :, 1:9], wj[1], ucs, ALU.mult, ALU.add, [us_.name, uc_.name, "vecs"], [uc_.name])
                      STT(ucs, us_[:, :, 0:8], wj[0], ucs, ALU.mult, ALU.add, [us_.name, uc_.name, "vecs"], [uc_.name])
                  ACT(uc[0][:], uc[0][:], AF.Silu, [uc[0].name], [uc[0].name])
                  TT("dve", actT[:, c, :], uc[0][:], uc[1][:], ALU.mult, [uc[0].name, uc[1].name], ["actT"])
              S.barrier()
              dump("actT", actT[:], ["actT"], BF16)
              ck("ffnup")
              eu.close()
              wdb = [sbt(es, "wdb%d" % i, [128, 22, 128], BF16) for i in range(3)]

              def prod(c, consume):
                  wh = [load_w(wdb, w_down_l[c][:, 0:22, :]), load_w(wdb, w_down_l[c][:, 22:44, :])]
                  for bi, (t0, nn) in enumerate(blocks(NCH)):
                      pi = (c * 3 + bi) % 4
                      for kc in range(NFC):
                          w = wh[kc // 22]
                          MM(ps[pi][:, :nn], w[:, kc % 22, :], actT[:, kc, t0:t0 + nn], kc == 0, kc == NFC - 1, [w.name, "actT"], [PSN[pi]])
                      consume(pi, t0, nn)
              boundary(es, prod, x2T, "norm_ffn_post", None, yT, None, final=True)

    except StopBuild:
        pass
    S.emit()
    return nc


_CACHE = {}


def make_in_maps(inp):
    inp = {k: np.asarray(v) for k, v in inp.items()}
    f32 = np.float32
    vecs, NV = build_vecs(inp)
    consts, NCONST = build_consts()
    w_in = inp["w_in"][0]
    Wp = np.zeros((D, 44 * 128), f32)
    Wp[:, :5120] = w_in[:, :5120]
    Wp[:, 5120:5216] = w_in[:, 5120:5216]
    Wp[:, 5248:5344] = w_in[:, 5216:5312]
    Wp[:, 5376:5632] = w_in[:, 5312:5568]
    shared = {
        "vecs_in": vecs, "consts_in": consts,
        "w_in_l": relayout_w(Wp, 128),
        "w_lora_l": np.concatenate([inp["w_lora"][0], np.zeros((32, 1024), f32)], 0),
        "a_lora_l": np.concatenate([inp["a_lora"][0], np.zeros((32, 1024), f32)], 0),
        "g_lora_l": np.ascontiguousarray(inp["g_lora"][0].reshape(2, 128, 1024).transpose(1, 0, 2)),
        "w_out_l": relayout_w(inp["w_out"][0], 128),
        "w_q_l": relayout_w(inp["w_q"][0], 128),
        "w_k_l": relayout_w(inp["w_k"][0], 128),
        "w_v_l": relayout_w(inp["w_v"][0], 512),
        "w_o_l": relayout_w(inp["w_o"][0], 128),
        "w_up_l": relayout_w(inp["w_up"][0], 128),
        "w_down_l": relayout_w(inp["w_down"][0], 128),
    }
    xp, xs = inp["x_prompt"], inp["x_sample"]
    in_maps = []
    for c in range(8):
        b, half = c // 2, c % 2
        sq = slice(16 * c, 16 * c + 16)
        xT = np.zeros((D, NT), f32)
        if half == 1:
            xT[:, :2048] = xp[b].T
        else:
            xT[:, 1024:2048] = xp[b, :1024].T
        xT[:, 2048:] = xs[sq].reshape(128, D).T
        ss = inp["state_shift"][0, sq]
        shp = np.zeros((16, 28 * 128), f32)
        shp[:, :3072] = ss[:, :3072]
        shp[:, 3072:3168] = ss[:, 3072:3168]
        shp[:, 3200:3296] = ss[:, 3168:3264]
        shp[:, 3328:3584] = ss[:, 3264:3520]
        wk = inp["state_wkv"][0, sq]
        wkT = wk.reshape(16, 8, 2, 64, 64).transpose(1, 2, 4, 0, 3).reshape(8, 128, 16, 64)
        m = dict(shared)
        m.update({
            "xT": xT,
            "flag": np.full((128, 1), float(half), f32),
            "memT": np.ascontiguousarray(inp["mem_prompt"][b].T),
            "kTs": np.ascontiguousarray(inp["cache_mem_k"][0, sq].reshape(16, 256, D).transpose(0, 2, 1)),
            "vs": np.ascontiguousarray(inp["cache_mem_v"][0, sq].reshape(16, 256, D)),
            "convst": np.ascontiguousarray(inp["state_conv"][0, sq].transpose(2, 0, 1)),
            "shiftst": np.ascontiguousarray(shp.reshape(16, 28, 128).transpose(2, 1, 0)),
            "wkvT": np.ascontiguousarray(wkT),
            "ffnst": np.ascontiguousarray(inp["state_ffn"][0, sq].reshape(16, 2, 88, 128).transpose(2, 3, 0, 1)),
        })
        in_maps.append(m)
    return in_maps, NV, NCONST


def kernel(**inp):
    f32 = np.float32
    in_maps, NV, NCONST = make_in_maps(inp)
    key = (NV, NCONST)
    if key not in _CACHE:
        _CACHE[key] = build_nc(NV, NCONST)
    nc = _CACHE[key]
    res = run_bass_kernel_spmd(nc, in_maps, core_ids=list(range(8))).results

    y_p = np.zeros((4, 2048, D), f32)
    y_s = np.zeros((128, 8, D), f32)
    conv_p = np.zeros((1, 4, 30, 1024), f32)
    conv_s = np.zeros((1, 128, 30, 1024), f32)
    sh_p = np.zeros((1, 4, 3520), f32)
    sh_s = np.zeros((1, 128, 3520), f32)
    wkv_p = np.zeros((1, 4, 16, 64, 64), f32)
    wkv_s = np.zeros((1, 128, 16, 64, 64), f32)
    ffn_p = np.zeros((1, 4, 2, 11264), f32)
    ffn_s = np.zeros((1, 128, 2, 11264), f32)
    mk = np.zeros((1, 4, 256, 4, 512), f32)
    mvv = np.zeros((1, 4, 256, 4, 512), f32)

    def unshift(a):
        return np.concatenate([a[:3072], a[3072:3168], a[3200:3296], a[3328:3584]], 0)

    for c in range(8):
        r = res[c]
        b, half = c // 2, c % 2
        sq = slice(16 * c, 16 * c + 16)
        yT = r["yT"]
        y_p[b, half * 1024:(half + 1) * 1024] = yT[:, 128:1152].T
        y_s[sq] = yT[:, 1152:1280].T.reshape(16, 8, D)
        conv_s[0, sq] = r["ncs"].transpose(1, 2, 0)
        nshf = r["nsh"].transpose(1, 0, 2).reshape(28 * 128, 17)
        sh_s[0, sq] = unshift(nshf[:, 1:17]).T
        wkv_s[0, sq] = r["nws"].reshape(8, 2, 64, 16, 64).transpose(3, 0, 1, 4, 2).reshape(16, 16, 64, 64)
        ffn_s[0, sq] = r["nfs"].transpose(2, 3, 0, 1).reshape(16, 2, 11264)
        if half == 1:
            conv_p[0, b] = r["ncp"].T
            sh_p[0, b] = unshift(nshf[:, 0:1])[:, 0]
            wkv_p[0, b] = r["nwp"].reshape(8, 2, 64, 64).transpose(0, 1, 3, 2).reshape(16, 64, 64)
            ffn_p[0, b] = r["nfp"].transpose(2, 0, 1).reshape(2, 11264)
            mk[0, b] = r["mkT"].T.reshape(256, 4, 512)
            mvv[0, b] = r["mv"].reshape(256, 4, 512)
    return (y_p, y_s, conv_p, conv_s, sh_p, sh_s, wkv_p, wkv_s, ffn_p, ffn_s, mk, mvv)
```

```python
import numpy as np
from contextlib import ExitStack
import concourse.bass as bass
import concourse.mybir as mybir
from concourse.bass_utils import run_bass_kernel_spmd

F32 = mybir.dt.float32
BF16 = mybir.dt.bfloat16
AF = mybir.ActivationFunctionType
ALU = mybir.AluOpType
ENGS = ("pe", "act", "dve", "pool", "sp")
SERIAL = False
SKIP_MIX = False

D = 2048
KC = 16
NT = 2176
NCH = 1280
CH0 = 896
DFF = 5632
NFC = 44
RMS_EPS = 1e-6
LN_EPS = 1e-5
GN_EPS = 64e-5
LDK = 0.6065306597126334


class Res:
    __slots__ = ("last_write", "reads")

    def __init__(self):
        self.last_write = None
        self.reads = []


class Op:
    __slots__ = ("eng", "fn", "deps", "signal", "idx", "is_dma", "dma_sem", "dma_cnt", "sigcount", "prewait")


class Sched:
    def __init__(self, nc, n_dma_sems=80):
        self.nc = nc
        self.ops = []
        self.per_eng = {e: [] for e in ENGS}
        self.n_dma_sems = n_dma_sems
        self.dma_rr = {"hw": 0, "sw": 0}
        self.dma_uses = [0] * n_dma_sems
        self.dma_last_op = [None] * n_dma_sems
        self.res = {}
        self.pending = {e: set() for e in ENGS}
        self.dma_unconsumed = set()
        self.excl = set("ps%d" % i for i in range(8))

    def R(self, name):
        r = self.res.get(name)
        if r is None:
            r = Res()
            self.res[name] = r
        return r

    def barrier(self):
        last = set()
        for e in ENGS:
            if self.per_eng[e]:
                last.add(self.per_eng[e][-1])
        last |= self.dma_unconsumed
        for e in ENGS:
            self.pending[e] |= last
        self.dma_unconsumed = set()

    def _mk(self, eng, fn, reads, writes, is_dma):
        op = Op()
        op.eng = eng
        op.fn = fn
        op.is_dma = is_dma
        op.signal = False
        op.dma_sem = None
        op.dma_cnt = 0
        op.prewait = None
        oid = len(self.ops)
        deps = set(self.pending[eng])
        self.pending[eng] = set()
        if SERIAL:
            for e_ in ENGS:
                if self.per_eng[e_]:
                    deps.add(self.per_eng[e_][-1])
        writes = list(writes) + [r for r in reads if r in self.excl and r not in writes]
        reads = [r for r in reads if r not in self.excl]
        rl = [self.R(r) for r in reads]
        wl = [self.R(w) for w in writes]
        for r in rl:
            if r.last_write is not None:
                deps.add(r.last_write)
        for w in wl:
            if w.last_write is not None:
                deps.add(w.last_write)
            deps.update(w.reads)
        for r in rl:
            r.reads.append(oid)
        for w in wl:
            w.last_write = oid
            w.reads = []
        deps.discard(oid)
        for d in deps:
            self.dma_unconsumed.discard(d)
        op.deps = deps
        op.idx = len(self.per_eng[eng])
        self.ops.append(op)
        self.per_eng[eng].append(oid)
        if is_dma:
            half = self.n_dma_sems // 2
            kind = "sw" if eng == "pool" else "hw"
            k = self.dma_rr[kind] + (half if kind == "sw" else 0)
            self.dma_rr[kind] = (self.dma_rr[kind] + 1) % half
            op.dma_sem = k
            self.dma_uses[k] += 1
            op.dma_cnt = self.dma_uses[k]
            op.prewait = self.dma_last_op[k]
            self.dma_last_op[k] = oid
            self.dma_unconsumed.add(oid)
        return oid

    def op(self, eng, fn, reads=(), writes=()):
        return self._mk(eng, fn, reads, writes, False)

    def dma(self, eng, fn, reads=(), writes=()):
        return self._mk(eng, fn, reads, writes, True)

    def emit(self):
        nc = self.nc
        ops = self.ops
        known = {e: {f: -1 for f in ENGS} for e in ENGS}
        dma_known = {e: set() for e in ENGS}
        need = []
        for oid, op in enumerate(ops):
            e = op.eng
            lst = []
            cmax = {}
            for d in op.deps:
                dop = ops[d]
                if dop.is_dma:
                    if d not in dma_known[e]:
                        lst.append(("d", d))
                        dma_known[e].add(d)
                else:
                    f = dop.eng
                    if f == "pe" and e == "pe" and not op.is_dma:
                        continue
                    if dop.idx <= known[e][f]:
                        continue
                    if f not in cmax or ops[cmax[f]].idx < dop.idx:
                        cmax[f] = d
            for f, d in cmax.items():
                lst.append(("c", d))
                known[e][f] = ops[d].idx
                ops[d].signal = True
            if op.is_dma and op.prewait is not None and op.prewait not in dma_known[e]:
                lst.append(("d", op.prewait))
                dma_known[e].add(op.prewait)
            need.append(lst)
        for e in ENGS:
            c = 0
            for oid in self.per_eng[e]:
                op = ops[oid]
                if op.signal and not op.is_dma:
                    c += 1
                op.sigcount = c
        with ExitStack() as es:
            csem = {e: es.enter_context(nc.semaphore("cs_" + e)) for e in ENGS}
            dsem = [es.enter_context(nc.semaphore("ds_%d" % i)) for i in range(self.n_dma_sems)]
            block = es.enter_context(nc.Block())

            def run(e, engobj):
                for oid in self.per_eng[e]:
                    op = ops[oid]
                    for kind, d in need[oid]:
                        dop = ops[d]
                        if kind == "c":
                            engobj.wait_ge(csem[dop.eng], dop.sigcount)
                        else:
                            engobj.wait_ge(dsem[dop.dma_sem], 16 * dop.dma_cnt)
                    ins = op.fn(engobj)
                    if op.is_dma:
                        ins.then_inc(dsem[op.dma_sem], 16)
                    elif op.signal:
                        ins.then_inc(csem[e], 1)
                last = {}
                for oid in self.per_eng[e]:
                    op = ops[oid]
                    if op.is_dma:
                        last[op.dma_sem] = max(last.get(op.dma_sem, 0), op.dma_cnt)
                for k, cnt in last.items():
                    engobj.wait_ge(dsem[k], 16 * cnt)

            @block.tensor
            def _(eng):
                run("pe", eng)

            @block.scalar
            def _(eng):
                run("act", eng)

            @block.vector
            def _(eng):
                run("dve", eng)

            @block.gpsimd
            def _(eng):
                run("pool", eng)

            @block.sync
            def _(eng):
                run("sp", eng)


def relayout_w(W, ncol):
    K, N = W.shape
    return np.ascontiguousarray(W.reshape(K // 128, 128, N // ncol, ncol).transpose(2, 1, 0, 3))


def fm(v):
    v = np.asarray(v, np.float32).reshape(-1)
    n = v.shape[0]
    nch = (n + 127) // 128
    p = np.zeros(nch * 128, np.float32)
    p[:n] = v
    return p.reshape(nch, 128).T


VEC_LAYOUT = {}


def build_vecs(inp):
    cols = []
    pos = [0]

    def add(name, arr):
        VEC_LAYOUT[name] = pos[0]
        cols.append(arr)
        pos[0] += arr.shape[1]

    for nm in ["norm_mix_pre", "norm_mix_post", "norm_xa_pre", "norm_xa_post", "norm_ffn_pre",
               "norm_ffn_post", "norm_mem"]:
        add(nm, fm(inp[nm][0]))
    add("conv_dw", np.ascontiguousarray(inp["conv_dw"][0].reshape(31, 8, 128).transpose(2, 1, 0)).reshape(128, 248))
    for nm in ["conv_dw_b", "conv_ln_g", "conv_ln_b"]:
        add(nm, fm(inp[nm][0]))
    mu = inp["rwkv_mu"][0]
    add("mu", np.concatenate([fm(mu[:3072]), fm(mu[3072:3168]), fm(mu[3168:3264]), fm(mu[3264:3520])], 1))
    for nm in ["w0", "a0", "k_k", "k_a", "r_k", "ln_x_g", "ln_x_b"]:
        add(nm, fm(inp[nm][0]))
    add("ffn_dw", np.ascontiguousarray(inp["ffn_dw"][0].reshape(3, 88, 128).transpose(2, 1, 0)).reshape(128, 264))
    add("ffn_dw_b", fm(inp["ffn_dw_b"][0]))
    return np.ascontiguousarray(np.concatenate(cols, 1)), pos[0]


CONST_LAYOUT = {}


def build_consts():
    cols = []
    pos = [0]

    def add(name, arr):
        CONST_LAYOUT[name] = (pos[0], arr.shape[1])
        cols.append(arr.astype(np.float32))
        pos[0] += arr.shape[1]

    i = np.arange(128)
    s, t = i[:, None], i[None, :]
    same = (s // 8) == (t // 8)
    add("ident", (s == t))
    add("su", (t > s))
    add("iu", (t >= s))
    add("sl", (t < s))
    add("su_bd", (t > s) & same)
    add("iu_bd", (t >= s) & same)
    add("sl_bd", (t < s) & same)
    add("blk", (s // 64) == (t // 64))
    sm = np.ones(1280)
    sm[0:1152:128] = 0
    sm[1152:1280:8] = 0
    add("scan", np.broadcast_to(sm[None, :], (128, 1280)))
    add("seqm", (i[:, None] // 8) == np.arange(16)[None, :])
    return np.ascontiguousarray(np.concatenate(cols, 1)), pos[0]


class StopBuild(Exception):
    pass


STOP = None
DBG_TILE = 0
DUMPS = {}


def build_nc(NV, NCONST):
    nc = bass.Bass("TRN2", target_bir_lowering=False)
    S = Sched(nc)

    def ck(name):
        if STOP == name:
            raise StopBuild()

    def dump(name, ap, rd, dt=F32):
        if STOP is None:
            return
        d = nc.dram_tensor("dbg_" + name, list(ap.shape), dt, kind="ExternalOutput").ap()
        S.dma("sp", lambda e: e.dma_start(out=d, in_=ap), rd, [])

    def din(name, shape):
        return nc.dram_tensor(name, list(shape), F32, kind="ExternalInput").ap()

    def dout(name, shape):
        return nc.dram_tensor(name, list(shape), F32, kind="ExternalOutput").ap()

    xT = din("xT", [D, NT])
    flag = din("flag", [128, 1])
    memT = din("memT", [D, 256])
    kTs = din("kTs", [16, D, 256])
    vs = din("vs", [16, 256, D])
    convst = din("convst", [1024, 16, 30])
    shiftst = din("shiftst", [128, 28, 16])
    wkvT = din("wkvT", [8, 128, 16, 64])
    ffnst = din("ffnst", [88, 128, 16, 2])
    vecs_d = din("vecs_in", [128, NV])
    consts_d = din("consts_in", [128, NCONST])
    w_in_l = din("w_in_l", [44, 128, 16, 128])
    w_lora_l = din("w_lora_l", [128, 1024])
    a_lora_l = din("a_lora_l", [128, 1024])
    g_lora_l = din("g_lora_l", [128, 2, 1024])
    w_out_l = din("w_out_l", [16, 128, 16, 128])
    w_q_l = din("w_q_l", [16, 128, 16, 128])
    w_k_l = din("w_k_l", [16, 128, 16, 128])
    w_v_l = din("w_v_l", [4, 128, 16, 512])
    w_o_l = din("w_o_l", [16, 128, 16, 128])
    w_up_l = din("w_up_l", [88, 128, 16, 128])
    w_down_l = din("w_down_l", [16, 128, 44, 128])

    yT = dout("yT", [D, NCH])
    ncp = dout("ncp", [1024, 30])
    ncs = dout("ncs", [1024, 16, 30])
    nsh = dout("nsh", [128, 28, 17])
    nwp = dout("nwp", [8, 128, 64])
    nws = dout("nws", [8, 128, 16, 64])
    nfp = dout("nfp", [88, 128, 2])
    nfs = dout("nfs", [88, 128, 16, 2])
    mkT = dout("mkT", [D, 256])
    mv = dout("mv", [256, D])

    x1T = nc.dram_tensor("x1T", [D, NCH], F32, kind="Internal").ap()
    x2T = nc.dram_tensor("x2T", [D, NCH], F32, kind="Internal").ap()
    mixS = nc.dram_tensor("mixS", [D, NCH], BF16, kind="Internal").ap()

    V = VEC_LAYOUT
    C = CONST_LAYOUT

    def ACT(out, in_, func, rd, wr, bias=None, scale=None):
        kw = {}
        if bias is not None:
            kw["bias"] = bias
        if scale is not None:
            kw["scale"] = scale
        S.op("act", lambda e: e.activation(out=out, in_=in_, func=func, **kw), rd, wr)

    def TT(eng, out, in0, in1, op, rd, wr):
        S.op(eng, lambda e: e.tensor_tensor(out=out, in0=in0, in1=in1, op=op), rd, wr)

    def TS(eng, out, in0, s1, s2, op0, op1, rd, wr):
        if s2 is None:
            S.op(eng, lambda e: e.tensor_scalar(out=out, in0=in0, scalar1=s1, scalar2=None, op0=op0), rd, wr)
        else:
            S.op(eng, lambda e: e.tensor_scalar(out=out, in0=in0, scalar1=s1, scalar2=s2, op0=op0, op1=op1), rd, wr)

    def STT(out, in0, sc, in1, op0, op1, rd, wr):
        S.op("dve", lambda e: e.scalar_tensor_tensor(out=out, in0=in0, scalar=sc, in1=in1, op0=op0, op1=op1), rd, wr)

    def MM(out, lhsT, rhs, start, stop, rd, wr):
        S.op("pe", lambda e: e.matmul(out, lhsT=lhsT, rhs=rhs, start=start, stop=stop), rd, wr)

    def CP(eng, out, in_, rd, wr):
        if eng == "act":
            S.op("act", lambda e: e.copy(out=out, in_=in_), rd, wr)
        else:
            S.op(eng, lambda e: e.tensor_copy(out=out, in_=in_), rd, wr)

    def DMA(q, out, in_, rd, wr):
        S.dma(q, lambda e: e.dma_start(out=out, in_=in_), rd, wr)

    def blocks(n, b=512):
        r = []
        t = 0
        while t < n:
            r.append((t, min(b, n - t)))
            t += b
        return r

    try:
      with ExitStack() as top:
          used_names = {}

          def sbt(es, name, shape, dt):
              k = used_names.get(name, 0)
              used_names[name] = k + 1
              if k:
                  name = "%s_%d" % (name, k + 1)
              return es.enter_context(nc.sbuf_tensor(name, list(shape), dt))

          ps = [top.enter_context(nc.psum_tensor("ps%d" % i, [128, 512], F32)) for i in range(8)]
          PSN = ["ps%d" % i for i in range(8)]

          vecs = sbt(top, "vecs", [128, NV], F32)
          cst = sbt(top, "cst", [128, NCONST], F32)
          DMA("sp", vecs[:], vecs_d, [], ["vecs"])
          DMA("sp", cst[:], consts_d, [], ["cst"])
          flg = sbt(top, "flg", [128, 1], F32)
          DMA("sp", flg[:], flag, [], ["flg"])
          identb = sbt(top, "identb", [128, 128], BF16)
          ident4 = sbt(top, "ident4", [128, 4, 128], BF16)
          onesb = sbt(top, "onesb", [128, 128], BF16)
          blkb = sbt(top, "blkb", [128, 128], BF16)
          omka = sbt(top, "omka", [128, 8], F32)
          ommu = sbt(top, "ommu", [128, 28], F32)

          def cv_(name):
              c0, n = C[name]
              return cst[:, c0:c0 + n]

          CP("dve", identb[:], cv_("ident"), ["cst"], ["identb"])
          for q in range(4):
              CP("dve", ident4[:, q, :], cv_("ident"), ["cst"], ["ident4"])
          S.op("pool", lambda e: e.memset(onesb[:], 1.0), [], ["onesb"])
          CP("dve", blkb[:], cv_("blk"), ["cst"], ["blkb"])
          TS("dve", omka[:], vecs[:, V["k_a"]:V["k_a"] + 8], -1.0, 1.0, ALU.mult, ALU.add, ["vecs"], ["omka"])

          def vcol(name, c):
              return vecs[:, V[name] + c:V[name] + c + 1]

          def prenorm(es, src, n, gname, hT, hoff, tag):
              xc = [sbt(es, "xc%s%d" % (tag, i), [128, n], F32) for i in range(2)]
              sq = [sbt(es, "sq%s%d" % (tag, i), [128, n], BF16) for i in range(2)]
              rs = sbt(es, "rs%s" % tag, [128, n], F32)
              bl = blocks(n)
              for c in range(KC):
                  x_ = xc[c % 2]
                  q_ = sq[c % 2]
                  DMA("sp", x_[:], src[c * 128:(c + 1) * 128, :], [], [x_.name])
                  ACT(q_[:], x_[:], AF.Square, [x_.name], [q_.name])
                  for bi, (t0, nn) in enumerate(bl):
                      MM(ps[bi][:, :nn], onesb[:], q_[:, t0:t0 + nn], c == 0, c == KC - 1, ["onesb", q_.name], [PSN[bi]])
              for bi, (t0, nn) in enumerate(bl):
                  ACT(rs[:, t0:t0 + nn], ps[bi][:, :nn], AF.Sqrt, [PSN[bi]], [rs.name], bias=RMS_EPS, scale=1.0 / D)
              S.op("dve", lambda e: e.reciprocal(out=rs[:], in_=rs[:]), [rs.name], [rs.name])
              for c in range(KC):
                  x_ = xc[c % 2]
                  DMA("sp", x_[:], src[c * 128:(c + 1) * 128, :], [], [x_.name])
                  STT(hT[:, c, hoff:hoff + n], x_[:], vcol(gname, c), rs[:], ALU.mult, ALU.mult,
                      [x_.name, rs.name, "vecs"], [hT.name])

          wctr = [0]

          def load_w(wbufs, src):
              b = wbufs[wctr[0] % len(wbufs)]
              wctr[0] += 1
              n1, n2 = b.shape[1], b.shape[2]
              if n1 * n2 <= 2048:
                  DMA("pool", b[:], src, [], [b.name])
              else:
                  g = max(1, 2048 // n2)
                  for k0 in range(0, n1, g):
                      k1 = min(n1, k0 + g)
                      DMA("pool", b[:, k0:k1, :], src[:, k0:k1, :], [], [b.name])
              return b

          lstack = ExitStack()
          Hf = sbt(lstack, "Hf", [128, 8, 64], F32)
          S.op("pool", lambda e: e.memset(Hf[:], 0.0), [], ["Hf"])
          lw = sbt(lstack, "lw", [128, 1024], BF16)
          la = sbt(lstack, "la", [128, 1024], BF16)
          lg = sbt(lstack, "lg", [128, 2, 1024], BF16)
          DMA("pool", lw[:], w_lora_l, [], ["lw"])
          DMA("pool", la[:], a_lora_l, [], ["la"])
          DMA("pool", lg[:], g_lora_l, [], ["lg"])

          def mix_phase(full):
              es = ExitStack()
              tg = "M" if full else "P"
              n = NCH if full else 896
              ntile = n // 128
              npt = 9 if full else 7
              tok_lo = CH0 if full else 0
              hoff = 768 if full else 0
              hn = 1408 if full else 896
              hT = sbt(es, "hT" + tg, [128, KC, hn], BF16)
              with ExitStack() as e2:
                  prenorm(e2, xT[:, hoff:hoff + hn], hn, "norm_mix_pre", hT, 0, tg)
              dump("hT" + tg, hT[:, 0:2, :], [hT.name], BF16)
              ck("prenorm" + tg)
              S.barrier()
              wb = [sbt(es, "wb%s%d" % (tg, i), [128, KC, 128], BF16) for i in range(4)]
              if full:
                  pbl = [(895, 512), (1407, 512), (1919, 257)]
                  pbase = 895
              else:
                  pbl = [(0, 512), (512, 384)]
                  pbase = -1

              def project(w, prbuf, psl):
                  for bi, (t0, nn) in enumerate(pbl):
                      p_ = psl[bi % len(psl)]
                      for kc in range(KC):
                          MM(ps[p_][:, :nn], w[:, kc, :], hT[:, kc, t0 - hoff:t0 - hoff + nn], kc == 0, kc == KC - 1,
                             [w.name, hT.name], [PSN[p_]])
                      CP("act", prbuf[:, t0 - pbase:t0 - pbase + nn], ps[p_][:, :nn], [PSN[p_]], [prbuf.name])

              T = [sbt(es, "T%s%d" % (tg, i), [128, n], F32) for i in range(6)]
              dtmp = T[5]
              prevS = sbt(es, "prevS" + tg, [128, 16, 8], F32)
              shs = sbt(es, "shs" + tg, [128, 28, 16], F32)
              nshs = sbt(es, "nshs" + tg, [128, 28, 17], F32)
              if full:
                  DMA("sp", shs[:], shiftst, [], [shs.name])

              def shiftmix(prbuf, chunk):
                  mu = vcol("mu", chunk)
                  npr = npt * 128
                  if full:
                      cur_s = prbuf[:, 1 + npr:1 + n].rearrange("p (s t) -> p s t", t=8)
                      CP("pool", nshs[:, chunk, 0:1], prbuf[:, npr:npr + 1], [prbuf.name], [nshs.name])
                      CP("pool", nshs[:, chunk, 1:17], cur_s[:, :, 7], [prbuf.name], [nshs.name])
                      CP("pool", prevS[:, :, 1:8], cur_s[:, :, 0:7], [prbuf.name], [prevS.name])
                      CP("pool", prevS[:, :, 0:1], shs[:, chunk, :].unsqueeze(2), [shs.name], [prevS.name])
                      ds = dtmp[:, npr:n].rearrange("p (s t) -> p s t", t=8)
                      TT("dve", ds, prevS[:], cur_s, ALU.subtract, [prevS.name, prbuf.name], [dtmp.name])
                  TT("dve", dtmp[:, 0:npr], prbuf[:, 0:npr], prbuf[:, 1:1 + npr], ALU.subtract, [prbuf.name], [dtmp.name])
                  STT(prbuf[:, 1:1 + n], dtmp[:], mu, prbuf[:, 1:1 + n], ALU.mult, ALU.add,
                      [dtmp.name, prbuf.name, "vecs"], [prbuf.name])

              pr = [sbt(es, "pr%s%d" % (tg, i), [128, n + 1], F32) for i in range(3)]
              if not full:
                  for i in range(3):
                      S.op("pool", lambda e, i=i: e.memset(pr[i][:, 0:1], 0.0), [], [pr[i].name])

              if full:
                  ec = ExitStack()
                  glu2 = [sbt(ec, "glu%d" % i, [128, 1408], F32) for i in range(2)]
                  glub2 = [sbt(ec, "glub%d" % i, [128, 1408], BF16) for i in range(2)]
                  sg2 = [sbt(ec, "sgc%d" % i, [128, 512], F32) for i in range(2)]
                  fullS2 = [sbt(ec, "fullS%d" % i, [128, 16, 38], F32) for i in range(2)]
                  fullSb2 = [sbt(ec, "fullSb%d" % i, [128, 16, 38], BF16) for i in range(2)]
                  dg2 = [sbt(ec, "dg%d" % i, [128, 31, 128], BF16) for i in range(2)]
                  cvp = sbt(ec, "cvp", [128, 8, NCH], BF16)
                  cbl = [(768, 512), (1280, 512), (1792, 384)]
                  for c in range(8):
                      glu, glub, fullS, fullSb, dg = glu2[c % 2], glub2[c % 2], fullS2[c % 2], fullSb2[c % 2], dg2[c % 2]
                      gN, gbN, fN, fbN, dN = glu.name, glub.name, fullS.name, fullSb.name, dg.name
                      wv = load_w(wb, w_in_l[c])
                      wg = load_w(wb, w_in_l[8 + c])
                      for bq, (t0, nn) in enumerate(cbl):
                          sg = sg2[bq % 2]
                          for kc in range(KC):
                              MM(ps[0][:, :nn], wv[:, kc, :], hT[:, kc, t0 - 768:t0 - 768 + nn], kc == 0, kc == KC - 1,
                                 [wv.name, hT.name], ["ps0"])
                          for kc in range(KC):
                              MM(ps[1][:, :nn], wg[:, kc, :], hT[:, kc, t0 - 768:t0 - 768 + nn], kc == 0, kc == KC - 1,
                                 [wg.name, hT.name], ["ps1"])
                          ACT(sg[:, :nn], ps[1][:, :nn], AF.Sigmoid, ["ps1"], [sg.name])
                          TT("dve", glu[:, t0 - 768:t0 - 768 + nn], ps[0][:, :nn], sg[:, :nn], ALU.mult, ["ps0", sg.name], [gN])
                      CP("act", glub[:], glu[:], [gN], [gbN])
                      DMA("sp", ncp[c * 128:(c + 1) * 128, :], glu[:, 2018 - 768:2048 - 768], [gN], [])
                      DMA("sp", fullS[:, :, 0:30], convst[c * 128:(c + 1) * 128, :, :], [], [fN])
                      CP("pool", fullS[:, :, 30:38], glu[:, 1280:1408].rearrange("p (s t) -> p s t", t=8), [gN], [fN])
                      DMA("sp", ncs[c * 128:(c + 1) * 128, :, :], fullS[:, :, 8:38], [fN], [])
                      CP("act", fullSb[:], fullS[:], [fN], [fbN])
                      wcv = vecs[:, V["conv_dw"] + c * 31:V["conv_dw"] + (c + 1) * 31]
                      TT("dve", dg[:], identb[:].unsqueeze(1).to_broadcast([128, 31, 128]),
                         wcv.unsqueeze(2).to_broadcast([128, 31, 128]), ALU.mult, ["identb", "vecs"], [dN])
                      for (c0, nn) in [(0, 512), (512, 512), (1024, 128)]:
                          g0 = c0 + 128 - 30
                          for j in range(31):
                              MM(ps[2][:, :nn], dg[:, j, :], glub[:, g0 + j:g0 + j + nn], j == 0, j == 30, [dN, gbN], ["ps2"])
                          ACT(cvp[:, c, c0:c0 + nn], ps[2][:, :nn], AF.Identity, ["ps2", "vecs"], ["cvp"], bias=vcol("conv_dw_b", c))
                      for j in range(31):
                          MM(ps[3][:, :128], dg[:, j, :], fullSb[:, :, j:j + 8], j == 0, j == 30, [dN, fbN], ["ps3"])
                      ACT(cvp[:, c, 1152:1280], ps[3][:, :128], AF.Identity, ["ps3", "vecs"], ["cvp"], bias=vcol("conv_dw_b", c))
                  sqb = sbt(ec, "sqb", [128, 512], BF16)
                  mean = sbt(ec, "lnmean", [128, 512], F32)
                  rstd = sbt(ec, "lnrstd", [128, 512], F32)
                  tmpf = sbt(ec, "lntmp", [128, 512], F32)
                  cvo = sbt(ec, "cvo", [128, 512], BF16)
                  for (t0, nn) in blocks(NCH):
                      for c in range(8):
                          ACT(sqb[:, :nn], cvp[:, c, t0:t0 + nn], AF.Square, ["cvp"], ["sqb"])
                          MM(ps[4][:, :nn], onesb[:], cvp[:, c, t0:t0 + nn], c == 0, c == 7, ["onesb", "cvp"], ["ps4"])
                          MM(ps[5][:, :nn], onesb[:], sqb[:, :nn], c == 0, c == 7, ["onesb", "sqb"], ["ps5"])
                      ACT(mean[:, :nn], ps[4][:, :nn], AF.Copy, ["ps4"], ["lnmean"], scale=1.0 / 1024)
                      ACT(tmpf[:, :nn], ps[4][:, :nn], AF.Square, ["ps4"], ["lntmp"], scale=1.0 / 1024)
                      STT(rstd[:, :nn], ps[5][:, :nn], 1.0 / 1024, tmpf[:, :nn], ALU.mult, ALU.subtract, ["ps5", "lntmp"], ["lnrstd"])
                      ACT(rstd[:, :nn], rstd[:, :nn], AF.Sqrt, ["lnrstd"], ["lnrstd"], bias=LN_EPS)
                      S.op("dve", lambda e, nn=nn: e.reciprocal(out=rstd[:, :nn], in_=rstd[:, :nn]), ["lnrstd"], ["lnrstd"])
                      for c in range(8):
                          TT("dve", tmpf[:, :nn], cvp[:, c, t0:t0 + nn], mean[:, :nn], ALU.subtract, ["cvp", "lnmean"], ["lntmp"])
                          TT("dve", tmpf[:, :nn], tmpf[:, :nn], rstd[:, :nn], ALU.mult, ["lntmp", "lnrstd"], ["lntmp"])
                          ACT(cvo[:, :nn], tmpf[:, :nn], AF.Silu, ["lntmp", "vecs"], ["cvo"], bias=vcol("conv_ln_b", c),
                              scale=vcol("conv_ln_g", c))
                          DMA("sp", mixS[c * 128:(c + 1) * 128, t0:t0 + nn], cvo[:, :nn], ["cvo"], ["mixS"])
                  S.barrier()
                  ec.close()
                  ck("convM")

              twd = sbt(es, "twd" + tg, [128, n], BF16)
              adb = sbt(es, "adb" + tg, [128, n], BF16)
              sgd = sbt(es, "sgd" + tg, [128, 2, n], BF16)
              for q in ([40, 41, 42, 43] if full else [40, 41]):
                  w = load_w(wb, w_in_l[q])
                  project(w, pr[0], [0, 1])
                  shiftmix(pr[0], 24 + (q - 40))
                  if q == 40:
                      ACT(twd[:], pr[0][:, 1:1 + n], AF.Tanh, [pr[0].name], [twd.name])
                  elif q == 41:
                      CP("act", adb[:], pr[0][:, 1:1 + n], [pr[0].name], [adb.name])
                  else:
                      ACT(sgd[:, q - 42, :], pr[0][:, 1:1 + n], AF.Sigmoid, [pr[0].name], [sgd.name])
              dump("twd" + tg, twd[:], [twd.name], BF16)
              ck("lora" + tg)

              Tb = sbt(es, "Tb" + tg, [128, n], BF16)
              PAR = sbt(es, "PAR" + tg, [128, ntile, 2, 128], BF16)
              Pb = sbt(es, "Pb" + tg, [128, n], BF16)
              Pk = sbt(es, "Pk" + tg, [128, n], BF16)
              Pv = sbt(es, "Pv" + tg, [128, n], BF16)
              yTb = sbt(es, "yT" + tg, [128, n], F32)
              NBT = 4 if full else 7
              NI = 2 * NBT
              Q = sbt(es, "Q" + tg, [128, NI, 128], BF16)
              QT = sbt(es, "QT" + tg, [128, NI, 128], BF16)
              IQ = sbt(es, "IQ" + tg, [128, NI, 128], BF16)
              Tt = [sbt(es, "Tt%s%d" % (tg, i), [128, NI, 128], BF16) for i in range(2)]
              Aak = sbt(es, "Aak" + tg, [128, NI, 128], BF16)
              Arb = sbt(es, "Arb" + tg, [128, NI, 128], BF16)
              Ark = sbt(es, "Ark" + tg, [128, NI, 128], BF16)

              def gk(buf, gi):
                  return "%s_g%d" % (buf.name, gi)
              tkm = sbt(es, "tkm" + tg, [128, 3, 128], BF16)
              Xb = sbt(es, "Xb" + tg, [128, 128], BF16)
              Ub = sbt(es, "Ub" + tg, [128, 128], BF16)
              Hb = sbt(es, "Hb" + tg, [128, 128], BF16)
              S.op("pool", lambda e: e.memset(Hb[:], 0.0), [], [Hb.name])
              HG = sbt(es, "HG" + tg, [128, 64], F32)
              if full:
                  Hs = sbt(es, "Hs", [128, 16, 64], F32)
                  Hsb = sbt(es, "Hsb", [128, 16, 128], BF16)
                  S.op("pool", lambda e: e.memset(Hsb[:], 0.0), [], ["Hsb"])
                  am = sbt(es, "am", [128, 16, 128], BF16)
                  rm = sbt(es, "rm", [128, 16, 128], BF16)
                  Ue = sbt(es, "Ue", [128, 16, 128], BF16)
                  Ve = sbt(es, "Ve", [128, 16, 128], BF16)
                  S.op("pool", lambda e: e.memset(am[:], 0.0), [], ["am"])
                  S.op("pool", lambda e: e.memset(rm[:], 0.0), [], ["rm"])
              scanm = cv_("scan")

              def proj_units(cc):
                  lst = []
                  if full:
                      lst.append((load_w(wb, w_in_l[16 + cc]), pr[0]))
                  lst.append((load_w(wb, w_in_l[24 + cc]), pr[1]))
                  lst.append((load_w(wb, w_in_l[32 + cc]), pr[2]))
                  u = 0
                  for w, prbuf in lst:
                      for (t0, nn) in pbl:
                          p_ = 6 + (u % 2)
                          u += 1
                          for kc in range(KC):
                              MM(ps[p_][:, :nn], w[:, kc, :], hT[:, kc, t0 - hoff:t0 - hoff + nn], kc == 0, kc == KC - 1,
                                 [w.name, hT.name], [PSN[p_]])
                          CP("act", prbuf[:, t0 - pbase:t0 - pbase + nn], ps[p_][:, :nn], [PSN[p_]], [prbuf.name])
                          yield

              def fill(gen, k=1):
                  if gen is None:
                      return
                  for _ in range(k):
                      try:
                          next(gen)
                      except StopIteration:
                          return

              filler = proj_units(0)
              fill(filler, 100)
              for c in range(8):
                  if full:
                      shiftmix(pr[0], c)
                  shiftmix(pr[1], 8 + c)
                  shiftmix(pr[2], 16 + c)
                  xr = pr[0][:, 1:1 + n]
                  xk = pr[1][:, 1:1 + n]
                  xv = pr[2][:, 1:1 + n]
                  T1, T2, T3, T4, T5, T6 = [t[:] for t in T]
                  n1, n2, n3, n4, n5, n6 = [t.name for t in T]
                  bl = blocks(n)
                  for (t0, nn) in bl:
                      MM(ps[4][:, :nn], lw[0:96, c * 128:(c + 1) * 128], twd[0:96, t0:t0 + nn], True, True, ["lw", twd.name], ["ps4"])
                      ACT(T1[:, t0:t0 + nn], ps[4][:, :nn], AF.Sigmoid, ["ps4", "vecs"], [n1], bias=vcol("w0", c))
                      MM(ps[5][:, :nn], la[0:96, c * 128:(c + 1) * 128], adb[0:96, t0:t0 + nn], True, True, ["la", adb.name], ["ps5"])
                      ACT(T4[:, t0:t0 + nn], ps[5][:, :nn], AF.Sigmoid, ["ps5", "vecs"], [n4], bias=vcol("a0", c))
                  S.op("dve", lambda e: e.tensor_tensor_scan(out=T2, data0=scanm[:, 0:n], data1=T1, initial=0.0,
                                                             op0=ALU.mult, op1=ALU.add), ["cst", n1], [n2])
                  TT("dve", T1, T2, T1, ALU.subtract, [n1, n2], [n1])
                  ACT(T1, T1, AF.Exp, [n1], [n1], scale=-LDK)
                  ACT(T3, T2, AF.Exp, [n2], [n3], scale=-LDK)
                  ACT(T2, T2, AF.Exp, [n2], [n2], scale=LDK)
                  ACT(Tb[:], xk, AF.Square, [pr[1].name, "vecs"], [Tb.name], scale=vcol("k_k", c))
                  for (t0, nn) in bl:
                      MM(ps[6][:, :nn], blkb[:], Tb[:, t0:t0 + nn], True, True, ["blkb", Tb.name], ["ps6"])
                      TS("dve", T5[:, t0:t0 + nn], ps[6][:, :nn], 1e-24, None, ALU.max, None, ["ps6"], [n5])
                  ACT(T5, T5, AF.Sqrt, [n5], [n5])
                  S.op("dve", lambda e: e.reciprocal(out=T5, in_=T5), [n5], [n5])
                  STT(T5, xk, vcol("k_k", c), T5, ALU.mult, ALU.mult, [pr[1].name, n5, "vecs"], [n5])
                  TS("dve", T6, T4, vcol("k_a", c), omka[:, c:c + 1], ALU.mult, ALU.add, [n4, "vecs", "omka"], [n6])
                  TT("dve", T6, T6, xk, ALU.mult, [n6, pr[1].name], [n6])
                  PARa = PAR[:, :, 0, :]
                  PARr = PAR[:, :, 1, :]
                  STT(PARa, T5.rearrange("p (a b) -> p a b", b=128), -1.0, T1.rearrange("p (a b) -> p a b", b=128),
                      ALU.mult, ALU.mult, [n5, n1], [PAR.name])
                  TT("dve", T4, T5, T4, ALU.mult, [n5, n4], [n4])
                  TT("dve", Pb[:], T4, T2, ALU.mult, [n4, n2], [Pb.name])
                  TT("dve", Pk[:], T6, T2, ALU.mult, [n6, n2], [Pk.name])
                  CP("act", Pv[:], xv, [pr[2].name], [Pv.name])
                  if full:
                      TT("dve", PARr, xr.rearrange("p (a b) -> p a b", b=128), T3.rearrange("p (a b) -> p a b", b=128),
                         ALU.mult, [pr[0].name, n3], [PAR.name])
                      STT(Tb[:], xr, vcol("r_k", c), T6, ALU.mult, ALU.mult, [pr[0].name, n6, "vecs"], [Tb.name])
                      for (t0, nn) in bl:
                          MM(ps[6][:, :nn], blkb[:], Tb[:, t0:t0 + nn], True, True, ["blkb", Tb.name], ["ps6"])
                          TT("dve", T6[:, t0:t0 + nn], ps[6][:, :nn], xv[:, t0:t0 + nn], ALU.mult, ["ps6", pr[2].name], [n6])
                      for (t0, nn) in bl:
                          MM(ps[7][:, :nn], lg[:, 0, c * 128:(c + 1) * 128], sgd[:, 0, t0:t0 + nn], True, False, ["lg", sgd.name], ["ps7"])
                          MM(ps[7][:, :nn], lg[:, 1, c * 128:(c + 1) * 128], sgd[:, 1, t0:t0 + nn], False, True, ["lg", sgd.name], ["ps7"])
                          CP("act", T1[:, t0:t0 + nn], ps[7][:, :nn], ["ps7"], [n1])
                      DMA("sp", Hs[:], wkvT[c], [], ["Hs"])
                      CP("act", Hsb[0:64, :, 0:64], Hs[0:64, :, :], ["Hs"], ["Hsb"])
                      CP("act", Hsb[64:128, :, 64:128], Hs[64:128, :, :], ["Hs"], ["Hsb"])
                      for s_ in range(16):
                          CP("pool", am[:, s_, 8 * s_:8 * s_ + 8], PAR[:, ntile - 1, 0, 8 * s_:8 * s_ + 8], [PAR.name], ["am"])
                          CP("pool", rm[:, s_, 8 * s_:8 * s_ + 8], PAR[:, ntile - 1, 1, 8 * s_:8 * s_ + 8], [PAR.name], ["rm"])
                  filler = proj_units(c + 1) if c + 1 < 8 else None
                  CP("act", Hb[0:64, 0:64], Hf[0:64, c, :], ["Hf"], [Hb.name])
                  CP("act", Hb[64:128, 64:128], Hf[64:128, c, :], ["Hf"], [Hb.name])
                  if c == 0:
                      dump("PAR" + tg, PAR[:], [PAR.name], BF16)
                      dump("Pb" + tg, Pb[:], [Pb.name], BF16)
                      dump("Pk" + tg, Pk[:], [Pk.name], BF16)
                      dump("T3" + tg, T[2][:], [T[2].name])
                      ck("prep" + tg)

                  for g0 in range(0, ntile, NBT):
                      tiles = [t_ for t_ in range(g0, min(ntile, g0 + NBT))]
                      items = [(t_, hh) for t_ in tiles for hh in (0, 1)]
                      for ii, (t_, hh) in enumerate(items):
                          gi = ii // 4
                          smp = full and t_ == ntile - 1
                          sfx = "_bd" if smp else ""
                          hs = slice(64 * hh, 64 * hh + 64)
                          cs = slice(t_ * 128, t_ * 128 + 128)
                          pa = ii % 2
                          ncol = 256 if full else 128
                          rhs_ar = PAR[hs, t_, :, :].rearrange("p a b -> p (a b)")[:, 0:ncol]
                          MM(ps[pa][:, 0:ncol], Pb[hs, cs], rhs_ar, True, True, [Pb.name, PAR.name], [PSN[pa]])
                          MM(ps[2 + pa][:, 0:ncol], Pk[hs, cs], rhs_ar, True, True, [Pk.name, PAR.name], [PSN[2 + pa]])
                          MM(ps[4 + pa][:, 0:128], PAR[hs, t_, 0, :], Pb[hs, cs], True, True, [Pb.name, PAR.name], [PSN[4 + pa]])
                          TT("dve", QT[:, ii, :], ps[pa][:, 0:128], cv_("su" + sfx), ALU.mult, [PSN[pa], "cst"], [gk(QT, gi)])
                          TT("dve", Aak[:, ii, :], ps[2 + pa][:, 0:128], cv_("su" + sfx), ALU.mult, [PSN[2 + pa], "cst"], [gk(Aak, gi)])
                          TT("dve", Q[:, ii, :], ps[4 + pa][:, 0:128], cv_("sl" + sfx), ALU.mult, [PSN[4 + pa], "cst"], [gk(Q, gi)])
                          if full:
                              TT("dve", Arb[:, ii, :], ps[pa][:, 128:256], cv_("iu" + sfx), ALU.mult, [PSN[pa], "cst"], [gk(Arb, gi)])
                              TT("dve", Ark[:, ii, :], ps[2 + pa][:, 128:256], cv_("iu" + sfx), ALU.mult, [PSN[2 + pa], "cst"], [gk(Ark, gi)])
                      ni_all = len(items)
                      groups = [(gi, gi * 4, min(4, ni_all - gi * 4)) for gi in range((ni_all + 3) // 4)]
                      for gi, i0, ni in groups:
                          TT("dve", Tt[0][:, i0:i0 + ni, :], QT[:, i0:i0 + ni, :], ident4[:, 0:ni, :], ALU.add,
                             [gk(QT, gi), "ident4"], [gk(Tt[0], gi)])
                      cur = 0
                      nlev = 7
                      for k in range(1, nlev):
                          for gi, i0, ni in groups:
                              pb = 3 * (gi % 2)
                              for ii in range(ni):
                                  MM(ps[pb][:, ii * 128:(ii + 1) * 128], QT[:, i0 + ii, :], Q[:, i0 + ii, :], True, True,
                                     [gk(QT, gi), gk(Q, gi)], [PSN[pb]])
                              if k < nlev - 1:
                                  for ii in range(ni):
                                      MM(ps[pb + 1][:, ii * 128:(ii + 1) * 128], Q[:, i0 + ii, :], QT[:, i0 + ii, :], True, True,
                                         [gk(QT, gi), gk(Q, gi)], [PSN[pb + 1]])
                              CP("act", Q[:, i0:i0 + ni, :], ps[pb][:, 0:ni * 128].rearrange("p (a b) -> p a b", b=128), [PSN[pb]], [gk(Q, gi)])
                              if k < nlev - 1:
                                  CP("act" if gi % 2 else "dve", QT[:, i0:i0 + ni, :], ps[pb + 1][:, 0:ni * 128].rearrange("p (a b) -> p a b", b=128),
                                     [PSN[pb + 1]], [gk(QT, gi)])
                              for ii in range(ni):
                                  MM(ps[pb + 2][:, ii * 128:(ii + 1) * 128], Q[:, i0 + ii, :], Tt[cur][:, i0 + ii, :], True, True,
                                     [gk(Q, gi), gk(Tt[cur], gi)], [PSN[pb + 2]])
                              TT("dve", Tt[1 - cur][:, i0:i0 + ni, :], ps[pb + 2][:, 0:ni * 128].rearrange("p (a b) -> p a b", b=128),
                                 Tt[cur][:, i0:i0 + ni, :], ALU.add, [PSN[pb + 2], gk(Tt[cur], gi)], [gk(Tt[1 - cur], gi)])
                          cur = 1 - cur
                      TTf = Tt[cur]
                      if c == 0 and g0 == 0:
                          dump("TTf" + tg, TTf[:, 0:4, :], [gk(TTf, 0)], BF16)
                          dump("Aak" + tg, Aak[:, 0:4, :], [gk(Aak, 0)], BF16)
                          ck("tinv" + tg)
                      for ti, t_ in enumerate(tiles):
                          smp = full and t_ == ntile - 1
                          cs = slice(t_ * 128, t_ * 128 + 128)
                          pT = ps[3].bitcast(BF16)
                          for q_, src_ in enumerate((Pb, Pk, Pv)):
                              S.op("pe", lambda e, q_=q_, src_=src_, cs=cs, pT=pT: e.transpose(out=pT[:, q_ * 128:(q_ + 1) * 128], in_=src_[:, cs],
                                                                                identity=identb[:]), [src_.name, "identb"], ["ps3"])
                          CP("act", tkm[:], pT[:, 0:384].rearrange("p (a b) -> p a b", b=128), ["ps3"], [tkm.name])
                          bT, kT_, vT = tkm[:, 0, :], tkm[:, 1, :], tkm[:, 2, :]
                          if not smp:
                              MM(ps[4][:, 0:128], PAR[:, t_, 0, :], Hb[:], True, False, [PAR.name, Hb.name], ["ps4"])
                          else:
                              for s_ in range(16):
                                  MM(ps[4][:, 0:128], am[:, s_, :], Hsb[:, s_, :], s_ == 0, False, ["am", "Hsb"], ["ps4"])
                          for hh in (0, 1):
                              ii = ti * 2 + hh
                              hs = slice(64 * hh, 64 * hh + 64)
                              MM(ps[4][:, hh * 64:hh * 64 + 64], Aak[:, ii, :], vT[:, hs], False, hh == 1, [gk(Aak, ii // 4), tkm.name], ["ps4"])
                          CP("act", Xb[:], ps[4][:, 0:128], ["ps4"], [Xb.name])
                          if c == 0 and t_ == DBG_TILE:
                              dump("Hbx" + tg, Hb[:], [Hb.name], BF16)
                              dump("PARx" + tg, PAR[:, t_, 0, :], [PAR.name], BF16)
                          for hh in (0, 1):
                              ii = ti * 2 + hh
                              hs = slice(64 * hh, 64 * hh + 64)
                              MM(ps[5][:, hh * 64:hh * 64 + 64], TTf[:, ii, :], Xb[:, hs], True, True, [gk(TTf, ii // 4), Xb.name], ["ps5"])
                          CP("dve", Ub[:], ps[5][:, 0:128], ["ps5"], [Ub.name])
                          if full:
                              for hh in (0, 1):
                                  ii = ti * 2 + hh
                                  hs = slice(64 * hh, 64 * hh + 64)
                                  if not smp:
                                      MM(ps[6][hs, 0:128], Hb[:, hs], PAR[:, t_, 1, :], True, False, [Hb.name, PAR.name], ["ps6"])
                                  else:
                                      for s_ in range(16):
                                          MM(ps[6][hs, 0:128], Hsb[:, s_, hs], rm[:, s_, :], s_ == 0, False, ["Hsb", "rm"], ["ps6"])
                                  MM(ps[6][hs, 0:128], Ub[:, hs], Arb[:, ii, :], False, False, [Ub.name, gk(Arb, ii // 4)], ["ps6"])
                                  MM(ps[6][hs, 0:128], vT[:, hs], Ark[:, ii, :], False, True, [tkm.name, gk(Ark, ii // 4)], ["ps6"])
                              CP("act", yTb[:, cs], ps[6][:, 0:128], ["ps6"], [yTb.name])
                          gam = T[2][:, t_ * 128 + 127:t_ * 128 + 128]
                          if not smp:
                              for hh in (0, 1):
                                  hs = slice(64 * hh, 64 * hh + 64)
                                  MM(ps[7][hs, 0:64], bT[:, hs], Ub[:, hs], True, False, [tkm.name, Ub.name], ["ps7"])
                                  MM(ps[7][hs, 0:64], kT_[:, hs], vT[:, hs], False, True, [tkm.name], ["ps7"])
                              ACT(HG[:], Hf[:, c, :], AF.Copy, ["Hf", n3], [HG.name], scale=gam)
                              STT(Hf[:, c, :], ps[7][:, 0:64], gam, HG[:], ALU.mult, ALU.add, ["ps7", HG.name, n3], ["Hf"])
                              CP("act", Hb[0:64, 0:64], Hf[0:64, c, :], ["Hf"], [Hb.name])
                              CP("act", Hb[64:128, 64:128], Hf[64:128, c, :], ["Hf"], [Hb.name])
                              if c == 0 and t_ == DBG_TILE - 1:
                                  dump("HfE" + tg, Hf[:, 0, :], ["Hf"])
                                  dump("HbE" + tg, Hb[:], [Hb.name], BF16)
                                  dump("HGE" + tg, HG[:], [HG.name])
                              if c == 0 and t_ == DBG_TILE:
                                  dump("tkm" + tg, tkm[:], [tkm.name], BF16)
                                  dump("Xb" + tg, Xb[:], [Xb.name], BF16)
                                  dump("Ub" + tg, Ub[:], [Ub.name], BF16)
                                  dump("Hf0" + tg, Hf[:, 0, :], ["Hf"])
                                  ck("tile0" + tg)
                          else:
                              seqm = cv_("seqm")
                              seqb = seqm.unsqueeze(2).to_broadcast([128, 16, 128])
                              TT("dve", Ue[:], Ub[:].unsqueeze(1).to_broadcast([128, 16, 128]), seqb, ALU.mult, [Ub.name, "cst"], ["Ue"])
                              TT("dve", Ve[:], vT.unsqueeze(1).to_broadcast([128, 16, 128]), seqb, ALU.mult, [tkm.name, "cst"], ["Ve"])
                              gs = T[2][:, (ntile - 1) * 128:n].rearrange("p (s t) -> p s t", t=8)[:, :, 7:8]
                              HsG = T[1][:, 0:1024].rearrange("p (s i) -> p s i", i=64)
                              TT("dve", HsG, Hs[:], gs.to_broadcast([128, 16, 64]), ALU.mult, ["Hs", n3], [n2])
                              for half in (0, 1):
                                  pz = ps[half]
                                  for hh in (0, 1):
                                      hs = slice(64 * hh, 64 * hh + 64)
                                      MM(pz[hs, :], bT[:, hs], Ue[:, 8 * half:8 * half + 8, hs], True, False, [tkm.name, "Ue"], [PSN[half]])
                                      MM(pz[hs, :], kT_[:, hs], Ve[:, 8 * half:8 * half + 8, hs], False, True, [tkm.name, "Ve"], [PSN[half]])
                                  sl = slice(8 * half, 8 * half + 8)
                                  TT("dve", Hs[:, sl, :], pz[:, :].rearrange("p (s i) -> p s i", i=64),
                                     gs[:, sl, :].to_broadcast([128, 8, 64]), ALU.mult, [PSN[half], n3], ["Hs"])
                              TT("dve", Hs[:], Hs[:], HsG, ALU.add, ["Hs", n2], ["Hs"])
                              DMA("sp", nws[c], Hs[:], ["Hs"], [])
                  if c == 0:
                      dump("Hf" + tg, Hf[:, 0, :], ["Hf"])
                      if full:
                          dump("yTb", yTb[:], [yTb.name])
                      ck("pair0" + tg)
                  if full:
                      DMA("sp", nwp[c], Hf[:, c, :], ["Hf"], [])
                      T1, T6 = T[0][:], T[5][:]
                      gns, gnr, rwo = T[1], T[3], Pb
                      ACT(Tb[:], yTb[:], AF.Square, [yTb.name], [Tb.name])
                      CP("act", Pv[:], yTb[:], [yTb.name], [Pv.name])
                      for (t0, nn) in bl:
                          MM(ps[4][:, :nn], blkb[:], Pv[:, t0:t0 + nn], True, True, ["blkb", Pv.name], ["ps4"])
                          MM(ps[5][:, :nn], blkb[:], Tb[:, t0:t0 + nn], True, True, ["blkb", Tb.name], ["ps5"])
                          ACT(gns[:, :nn], ps[4][:, :nn], AF.Square, ["ps4"], [T[1].name], scale=1.0 / 64)
                          STT(gnr[:, :nn], ps[5][:, :nn], 1.0 / 64, gns[:, :nn], ALU.mult, ALU.subtract, ["ps5", T[1].name], [T[3].name])
                          ACT(gnr[:, :nn], gnr[:, :nn], AF.Sqrt, [T[3].name], [T[3].name], bias=GN_EPS)
                          S.op("dve", lambda e, nn=nn: e.reciprocal(out=gnr[:, :nn], in_=gnr[:, :nn]), [T[3].name], [T[3].name])
                          ysl = yTb[:, t0:t0 + nn]
                          STT(ysl, ps[4][:, :nn], -1.0 / 64, ysl, ALU.mult, ALU.add, ["ps4", yTb.name], [yTb.name])
                          TT("dve", ysl, ysl, gnr[:, :nn], ALU.mult, [yTb.name, T[3].name], [yTb.name])
                          TS("dve", ysl, ysl, vcol("ln_x_g", c), vcol("ln_x_b", c), ALU.mult, ALU.add, [yTb.name, "vecs"], [yTb.name])
                          TT("dve", ysl, ysl, T6[:, t0:t0 + nn], ALU.add, [yTb.name, T[5].name], [yTb.name])
                          TT("dve", rwo[:, t0:t0 + nn], ysl, T1[:, t0:t0 + nn], ALU.mult, [yTb.name, T[0].name], [Pb.name])
                      DMA("sp", mixS[1024 + c * 128:1024 + (c + 1) * 128, :], Pb[:], [Pb.name], ["mixS"])
                  fill(filler, 100)
              if full:
                  DMA("sp", nsh, nshs[:], [nshs.name], [])
              S.barrier()
              es.close()

          if not SKIP_MIX:
              mix_phase(False)
              ck("mixP")
              mix_phase(True)
          lstack.close()
          if not SKIP_MIX:
              dump("mixS", mixS, ["mixS"], BF16)
          ck("mixdone")

          def boundary(es, produce, xsrc, gpost, gpre, xdst, hT, final=False):
              sT = sbt(es, "sT", [128, KC, NCH], BF16)
              sq = sbt(es, "bsq", [128, 512], BF16)
              rs = sbt(es, "brs", [128, NCH], F32)
              xc = [sbt(es, "bxc%d" % i, [128, NCH], F32) for i in range(2)]
              sq2 = sbt(es, "bsq2", [128, NCH], BF16)
              bl = blocks(NCH)
              for c in range(KC):
                  def consume(pi, t0, nn, c=c):
                      CP("act", sT[:, c, t0:t0 + nn], ps[pi][:, :nn], [PSN[pi]], ["sT"])
                      ACT(sq[:, :nn], ps[pi][:, :nn], AF.Square, [PSN[pi]], ["bsq"])
                      bi = t0 // 512
                      MM(ps[5 + bi][:, :nn], onesb[:], sq[:, :nn], c == 0, c == KC - 1, ["onesb", "bsq"], [PSN[5 + bi]])
                  produce(c, consume)
              for bi, (t0, nn) in enumerate(bl):
                  ACT(rs[:, t0:t0 + nn], ps[5 + bi][:, :nn], AF.Sqrt, [PSN[5 + bi]], ["brs"], bias=RMS_EPS, scale=1.0 / D)
              S.op("dve", lambda e: e.reciprocal(out=rs[:], in_=rs[:]), ["brs"], ["brs"])
              for c in range(KC):
                  x_ = xc[c % 2]
                  DMA("sp", x_[:], xsrc[c * 128:(c + 1) * 128, :], [], [x_.name])
                  TT("dve", sq2[:], sT[:, c, :], rs[:], ALU.mult, ["sT", "brs"], ["bsq2"])
                  STT(x_[:], sq2[:], vcol(gpost, c), x_[:], ALU.mult, ALU.add, ["bsq2", x_.name, "vecs"], [x_.name])
                  DMA("sp", xdst[c * 128:(c + 1) * 128, :], x_[:], [x_.name], ["xdst"])
                  if not final:
                      ACT(sq2[:], x_[:], AF.Square, [x_.name], ["bsq2"])
                      for bi, (t0, nn) in enumerate(bl):
                          MM(ps[5 + bi][:, :nn], onesb[:], sq2[:, t0:t0 + nn], c == 0, c == KC - 1, ["onesb", "bsq2"], [PSN[5 + bi]])
              if final:
                  return
              for bi, (t0, nn) in enumerate(bl):
                  ACT(rs_keep[:, t0:t0 + nn], ps[5 + bi][:, :nn], AF.Sqrt, [PSN[5 + bi]], ["rs_keep"], bias=RMS_EPS, scale=1.0 / D)
              S.op("dve", lambda e: e.reciprocal(out=rs_keep[:], in_=rs_keep[:]), ["rs_keep"], ["rs_keep"])

          def prenorm_apply(es, hT, xsrc, gpre):
              xc = [sbt(es, "pxc%d" % i, [128, NCH], F32) for i in range(2)]
              for c in range(KC):
                  x_ = xc[c % 2]
                  DMA("sp", x_[:], xsrc[c * 128:(c + 1) * 128, :], ["xdst"], [x_.name])
                  STT(hT[:, c, :], x_[:], vcol(gpre, c), rs_keep[:], ALU.mult, ALU.mult, [x_.name, "rs_keep", "vecs"], [hT.name])

          rs_keep = sbt(top, "rs_keep", [128, NCH], F32)


          with ExitStack() as es:
           if not SKIP_MIX:
              wb = [sbt(es, "wbo%d" % i, [128, KC, 128], BF16) for i in range(3)]
              mx = sbt(es, "mx", [128, KC, NCH], BF16)
              for kc in range(KC):
                  DMA("sp", mx[:, kc, :], mixS[kc * 128:(kc + 1) * 128, :], ["mixS"], ["mx"])

              def prod(c, consume):
                  w = load_w(wb, w_out_l[c])
                  for bi, (t0, nn) in enumerate(blocks(NCH)):
                      pi = (c * 3 + bi) % 4
                      for kc in range(KC):
                          MM(ps[pi][:, :nn], w[:, kc, :], mx[:, kc, t0:t0 + nn], kc == 0, kc == KC - 1, [w.name, "mx"], [PSN[pi]])
                      consume(pi, t0, nn)
              boundary(es, prod, xT[:, CH0:NT], "norm_mix_post", "norm_xa_pre", x1T, None)
              S.barrier()
              dump("x1T", x1T, ["xdst"])
              ck("b1")

          with ExitStack() as eso:
            oT = sbt(eso, "oT", [128, KC, NCH], BF16)
            with ExitStack() as es:
              hT2 = sbt(es, "hT2", [128, KC, NCH], BF16)
              with ExitStack() as e2:
                  if not SKIP_MIX:
                      prenorm_apply(e2, hT2, x1T, "norm_xa_pre")
                  S.barrier()
              ck("pa1")
              kTp = sbt(es, "kTp", [128, KC, 256], BF16)
              vp = sbt(es, "vp", [128, 2, D], BF16)
              with ExitStack() as ea:
                  wbk = [sbt(ea, "wbk%d" % i, [128, KC, 128], BF16) for i in range(3)]
                  mnT = sbt(ea, "mnT", [128, KC, 256], BF16)
                  with ExitStack() as e2:
                      prenorm(e2, memT, 256, "norm_mem", mnT, 0, "mem")
                      S.barrier()
                  ck("mn")
                  ktf = sbt(ea, "ktf", [128, 256], F32)
                  vpf = sbt(ea, "vpf", [128, 512], F32)
                  for c in range(KC):
                      w = load_w(wbk, w_k_l[c])
                      for kc in range(KC):
                          MM(ps[0][:, :256], w[:, kc, :], mnT[:, kc, :], kc == 0, kc == KC - 1, [w.name, "mnT"], ["ps0"])
                      CP("act", kTp[:, c, :], ps[0][:, :256], ["ps0"], ["kTp"])
                      CP("dve", ktf[:], ps[0][:, :256], ["ps0"], ["ktf"])
                      DMA("sp", mkT[c * 128:(c + 1) * 128, :], ktf[:], ["ktf"], [])
                  ck("kproj")
                  wvb = [sbt(ea, "wvb%d" % i, [128, KC, 512], BF16) for i in range(2)]
                  for cb in range(4):
                      w = load_w(wvb, w_v_l[cb])
                      for mt in range(2):
                          for kc in range(KC):
                              MM(ps[1 + mt][:, :], mnT[:, kc, mt * 128:(mt + 1) * 128], w[:, kc, :], kc == 0, kc == KC - 1,
                                 [w.name, "mnT"], [PSN[1 + mt]])
                          CP("act", vp[:, mt, cb * 512:(cb + 1) * 512], ps[1 + mt][:, :], [PSN[1 + mt]], ["vp"])
                          CP("dve", vpf[:], ps[1 + mt][:, :], [PSN[1 + mt]], ["vpf"])
                          DMA("sp", mv[mt * 128:(mt + 1) * 128, cb * 512:(cb + 1) * 512], vpf[:], ["vpf"], [])
                  S.barrier()
              ck("memkv")
              wb = [sbt(es, "wba%d" % i, [128, KC, 128], BF16) for i in range(2)]
              qT = sbt(es, "qT", [128, KC, NCH], BF16)
              for c in range(KC):
                  w = load_w(wb, w_q_l[c])
                  for bi, (t0, nn) in enumerate(blocks(NCH)):
                      pi = 3 + (c * 3 + bi) % 3
                      for kc in range(KC):
                          MM(ps[pi][:, :nn], w[:, kc, :], hT2[:, kc, t0:t0 + nn], kc == 0, kc == KC - 1, [w.name, hT2.name], [PSN[pi]])
                      ACT(qT[:, c, t0:t0 + nn], ps[pi][:, :nn], AF.Copy, [PSN[pi]], ["qT"], scale=512.0 ** -0.5)
              mx8 = sbt(es, "mx8", [128, 8], F32)
              sm8 = sbt(es, "sm8", [128, 8], F32)
              att = sbt(es, "att", [128, 4, 256], BF16)
              attT = sbt(es, "attT", [128, 2, 4, 128], BF16)
              for t_ in range(9):
                  cs = slice(t_ * 128, (t_ + 1) * 128)
                  for h in range(4):
                      pi = h % 2
                      for dc in range(4):
                          MM(ps[pi][:, 0:256], qT[:, 4 * h + dc, cs], kTp[:, 4 * h + dc, :], dc == 0, dc == 3, ["qT", "kTp"], [PSN[pi]])
                      S.op("dve", lambda e, pi=pi, h=h: e.reduce_max(out=mx8[:, h:h + 1], in_=ps[pi][:, 0:256], axis=mybir.AxisListType.X),
                           [PSN[pi]], ["mx8"])
                      TS("dve", mx8[:, 4 + h:5 + h], mx8[:, h:h + 1], -1.0, None, ALU.mult, None, ["mx8"], ["mx8"])
                      S.op("act", lambda e, pi=pi, h=h: e.activation(out=att[:, h, :], in_=ps[pi][:, 0:256], func=AF.Exp,
                                                                    bias=mx8[:, 4 + h:5 + h], accum_out=sm8[:, h:h + 1]),
                           [PSN[pi], "mx8"], ["att", "sm8"])
                      S.op("dve", lambda e, h=h: e.reciprocal(out=sm8[:, 4 + h:5 + h], in_=sm8[:, h:h + 1]), ["sm8"], ["sm8"])
                      TS("dve", att[:, h, :], att[:, h, :], sm8[:, 4 + h:5 + h], None, ALU.mult, None, ["att", "sm8"], ["att"])
                  pT = ps[2].bitcast(BF16)
                  for h in range(4):
                      for mt in range(2):
                          S.op("pe", lambda e, h=h, mt=mt: e.transpose(out=pT[:, (mt * 4 + h) * 128:(mt * 4 + h + 1) * 128],
                                                                      in_=att[:, h, mt * 128:(mt + 1) * 128], identity=identb[:]),
                               ["att", "identb"], ["ps2"])
                  CP("act", attT[:], pT[:, 0:1024].rearrange("p (a b c) -> p a b c", a=2, b=4), ["ps2"], ["attT"])
                  for dc in range(KC):
                      h = dc // 4
                      pi = 3 + dc % 2
                      for mt in range(2):
                          MM(ps[pi][:, 0:128], vp[:, mt, dc * 128:(dc + 1) * 128], attT[:, mt, h, :], mt == 0, mt == 1, ["vp", "attT"], [PSN[pi]])
                      CP("act" if dc % 2 else "dve", oT[:, dc, cs], ps[pi][:, 0:128], [PSN[pi]], ["oT"])
              dump("oTp", oT[:, :, 0:1152], ["oT"], BF16)
              ck("attnP")
              kts = [sbt(es, "kts%d" % i, [128, KC, 256], BF16) for i in range(2)]
              vss = [sbt(es, "vss%d" % i, [128, 2, D], BF16) for i in range(2)]
              sc8 = sbt(es, "sc8", [8, 4, 256], F32)
              at8 = sbt(es, "at8", [8, 4, 256], BF16)
              m8 = sbt(es, "m8", [8, 8], F32)
              a8T = sbt(es, "a8T", [128, 8, 8], BF16)
              for s_ in range(16):
                  kt = kts[s_ % 2]
                  vv = vss[s_ % 2]
                  DMA("pool", kt[:], kTs[s_].rearrange("(c p) m -> p c m", p=128), [], [kt.name])
                  DMA("pool", vv[:], vs[s_].rearrange("(t p) d -> p t d", p=128), [], [vv.name])
                  cs = slice(1152 + 8 * s_, 1152 + 8 * s_ + 8)
                  for h in range(4):
                      pi = h % 2
                      for dc in range(4):
                          MM(ps[pi][0:8, 0:256], qT[:, 4 * h + dc, cs], kt[:, 4 * h + dc, :], dc == 0, dc == 3, ["qT", kt.name], [PSN[pi]])
                      CP("act", sc8[:, h, :], ps[pi][0:8, 0:256], [PSN[pi]], ["sc8"])
                  S.op("dve", lambda e: e.tensor_reduce(out=m8[:, 0:4], in_=sc8[:], axis=mybir.AxisListType.X, op=ALU.max), ["sc8"], ["m8"])
                  TT("dve", sc8[:], sc8[:], m8[:, 0:4].unsqueeze(2).to_broadcast([8, 4, 256]), ALU.subtract, ["sc8", "m8"], ["sc8"])
                  ACT(sc8[:], sc8[:], AF.Exp, ["sc8"], ["sc8"])
                  S.op("dve", lambda e: e.tensor_reduce(out=m8[:, 4:8], in_=sc8[:], axis=mybir.AxisListType.X, op=ALU.add), ["sc8"], ["m8"])
                  S.op("dve", lambda e: e.reciprocal(out=m8[:, 4:8], in_=m8[:, 4:8]), ["m8"], ["m8"])
                  TT("dve", at8[:], sc8[:], m8[:, 4:8].unsqueeze(2).to_broadcast([8, 4, 256]), ALU.mult, ["sc8", "m8"], ["at8"])
                  pT = ps[2].bitcast(BF16)
                  for h in range(4):
                      for mt in range(2):
                          S.op("pe", lambda e, h=h, mt=mt: e.transpose(out=pT[:, (mt * 4 + h) * 8:(mt * 4 + h + 1) * 8],
                                                                      in_=at8[:, h, mt * 128:(mt + 1) * 128], identity=identb[0:8, 0:8]),
                               ["at8", "identb"], ["ps2"])
                  CP("act", a8T[:], pT[:, 0:64].rearrange("p (a b) -> p a b", b=8), ["ps2"], ["a8T"])
                  for dc in range(KC):
                      h = dc // 4
                      for mt in range(2):
                          MM(ps[3][:, dc * 8:dc * 8 + 8], vv[:, mt, dc * 128:(dc + 1) * 128], a8T[:, mt * 4 + h, :], mt == 0, mt == 1,
                             [vv.name, "a8T"], ["ps3"])
                  CP("act", oT[:, :, cs], ps[3][:, 0:128].rearrange("p (a b) -> p a b", b=8), ["ps3"], ["oT"])
              S.barrier()
              dump("oT", oT[:], ["oT"], BF16)
              ck("attnS")
            with ExitStack() as es:
              wb = [sbt(es, "wbo2%d" % i, [128, KC, 128], BF16) for i in range(3)]

              def prod(c, consume):
                  w = load_w(wb, w_o_l[c])
                  for bi, (t0, nn) in enumerate(blocks(NCH)):
                      pi = (c * 3 + bi) % 4
                      for kc in range(KC):
                          MM(ps[pi][:, :nn], w[:, kc, :], oT[:, kc, t0:t0 + nn], kc == 0, kc == KC - 1, [w.name, "oT"], [PSN[pi]])
                      consume(pi, t0, nn)
              boundary(es, prod, x1T, "norm_xa_post", "norm_ffn_pre", x2T, None)
              S.barrier()
              dump("x2T", x2T, ["xdst"])
              ck("b2")

          with ExitStack() as es:
              actT = sbt(es, "actT", [128, NFC, NCH], BF16)
              eu = ExitStack()
              hT2 = sbt(eu, "hT2", [128, KC, NCH], BF16)
              with ExitStack() as e2:
                  prenorm_apply(e2, hT2, x2T, "norm_ffn_pre")
                  S.barrier()
              wb = [sbt(eu, "wbf%d" % i, [128, KC, 128], BF16) for i in range(4)]
              TS("dve", hT2[:, :, 0:128], hT2[:, :, 0:128], flg[:, 0:1], None, ALU.mult, None, [hT2.name, "flg"], [hT2.name])
              up = [sbt(eu, "up%d" % i, [128, 2 + 1152], F32) for i in range(2)]
              ups4 = [[sbt(eu, "ups%d_%d" % (i, q), [128, 16, 10], F32) for q in range(2)] for i in range(2)]
              stgp = [[sbt(eu, "stgp%d_%d" % (i, q), [128, 2], F32) for q in range(2)] for i in range(2)]
              stgs = [[sbt(eu, "stgs%d_%d" % (i, q), [128, 16, 2], F32) for q in range(2)] for i in range(2)]
              stgh = [[sbt(eu, "stgh%d" % i, [128, 16, 2], F32)] * 2 for i in range(2)]
              uc = [sbt(eu, "uc%d" % i, [128, NCH], F32) for i in range(2)]
              for i in range(2):
                  S.op("pool", lambda e, i=i: e.memset(up[i][:, 0:2], 0.0), [], [up[i].name + "A"])
              pctr = [0]
              ws_next = [load_w(wb, w_up_l[0]), load_w(wb, w_up_l[NFC])]
              for c in range(NFC):
                  ws = ws_next
                  if c + 1 < NFC:
                      ws_next = [load_w(wb, w_up_l[c + 1]), load_w(wb, w_up_l[NFC + c + 1])]
                  ups = [ups4[0][c % 2], ups4[1][c % 2]]
                  for vi, ch in enumerate((c, NFC + c)):
                      sh_ = stgh[vi][c % 2]
                      DMA("sp", sh_[:], ffnst[ch], [], [sh_.name])
                      CP("pool", ups[vi][:, :, 0:2], sh_[:], [sh_.name], [ups[vi].name])
                  for reg, rblocks in (("A", [(0, 512), (512, 128)]), ("B", [(640, 512), (1152, 128)])):
                      for vi, ch in enumerate((c, NFC + c)):
                          w = ws[vi]
                          u_, us_, uc_ = up[vi], ups[vi], uc[vi]
                          uk = u_.name + reg
                          for (t0, nn) in rblocks:
                              pi = pctr[0] % 6
                              pctr[0] += 1
                              for kc in range(KC):
                                  MM(ps[pi][:, :nn], w[:, kc, :], hT2[:, kc, t0:t0 + nn], kc == 0, kc == KC - 1, [w.name, hT2.name], [PSN[pi]])
                              if t0 < 1152:
                                  CP("act", u_[:, 2 + t0:2 + t0 + nn], ps[pi][:, :nn], [PSN[pi]], [uk])
                              else:
                                  CP("act", us_[:, :, 2:10], ps[pi][:, :nn].rearrange("p (s t) -> p s t", t=8), [PSN[pi]], [us_.name])
                          fw0 = V["ffn_dw"] + ch * 3
                          wj = [vecs[:, fw0 + j:fw0 + j + 1] for j in range(3)]
                          bj = vcol("ffn_dw_b", ch)
                          ck_ = uc_.name + reg
                          if reg == "A":
                              lo, hi = 0, 640
                              rdk = [uk, "vecs"]
                          else:
                              lo, hi = 640, 1152
                              rdk = [u_.name + "A", uk, "vecs"]
                              sp_, ss_ = stgp[vi][c % 2], stgs[vi][c % 2]
                              CP("pool", sp_[:], u_[:, 1152:1154], [uk], [sp_.name])
                              CP("pool", ss_[:], us_[:, :, 8:10], [us_.name], [ss_.name])
                              DMA("sp", nfp[ch], sp_[:], [sp_.name], [])
                              DMA("sp", nfs[ch], ss_[:], [ss_.name], [])
                          TS("dve", uc_[:, lo:hi], u_[:, 2 + lo:2 + hi], wj[2], bj, ALU.mult, ALU.add, rdk, [ck_])
                          STT(uc_[:, lo:hi], u_[:, 1 + lo:1 + hi], wj[1], uc_[:, lo:hi], ALU.mult, ALU.add, rdk + [ck_], [ck_])
                          STT(uc_[:, lo:hi], u_[:, lo:hi], wj[0], uc_[:, lo:hi], ALU.mult, ALU.add, rdk + [ck_], [ck_])
                          if reg == "B":
                              ucs = uc_[:, 1152:1280].rearrange("p (s t) -> p s t", t=8)
                              TS("dve", ucs, us_[:, :, 2:10], wj[2], bj, ALU.mult, ALU.add, [us_.name, "vecs"], [ck_])
                              STT(ucs, us_[:, :, 1:9], wj[1], ucs, ALU.mult, ALU.add, [us_.name, ck_, "vecs"], [ck_])
                              STT(ucs, us_[:, :, 0:8], wj[0], ucs, ALU.mult, ALU.add, [us_.name, ck_, "vecs"], [ck_])
                      lo, hi = (0, 640) if reg == "A" else (640, 1280)
                      k0, k1 = uc[0].name + reg, uc[1].name + reg
                      ACT(uc[0][:, lo:hi], uc[0][:, lo:hi], AF.Silu, [k0], [k0])
                      TT("dve", actT[:, c, lo:hi], uc[0][:, lo:hi], uc[1][:, lo:hi], ALU.mult, [k0, k1], ["actT" + reg])
              S.barrier()
              dump("actT", actT[:], ["actTA", "actTB"], BF16)
              ck("ffnup")
              eu.close()
              wdb = [sbt(es, "wdb%d" % i, [128, 22, 128], BF16) for i in range(3)]

              def prod(c, consume):
                  wh = [load_w(wdb, w_down_l[c][:, 0:22, :]), load_w(wdb, w_down_l[c][:, 22:44, :])]
                  for bi, (t0, nn) in enumerate(blocks(NCH)):
                      pi = (c * 3 + bi) % 4
                      for kc in range(NFC):
                          w = wh[kc // 22]
                          MM(ps[pi][:, :nn], w[:, kc % 22, :], actT[:, kc, t0:t0 + nn], kc == 0, kc == NFC - 1, [w.name, "actTA", "actTB"], [PSN[pi]])
                      consume(pi, t0, nn)
              boundary(es, prod, x2T, "norm_ffn_post", None, yT, None, final=True)

    except StopBuild:
        pass
    S.emit()
    return nc


_CACHE = {}


def make_in_maps(inp):
    inp = {k: np.asarray(v) for k, v in inp.items()}
    f32 = np.float32
    vecs, NV = build_vecs(inp)
    consts, NCONST = build_consts()
    w_in = inp["w_in"][0]
    Wp = np.zeros((D, 44 * 128), f32)
    Wp[:, :5120] = w_in[:, :5120]
    Wp[:, 5120:5216] = w_in[:, 5120:5216]
    Wp[:, 5248:5344] = w_in[:, 5216:5312]
    Wp[:, 5376:5632] = w_in[:, 5312:5568]
    shared = {
        "vecs_in": vecs, "consts_in": consts,
        "w_in_l": relayout_w(Wp, 128),
        "w_lora_l": np.concatenate([inp["w_lora"][0], np.zeros((32, 1024), f32)], 0),
        "a_lora_l": np.concatenate([inp["a_lora"][0], np.zeros((32, 1024), f32)], 0),
        "g_lora_l": np.ascontiguousarray(inp["g_lora"][0].reshape(2, 128, 1024).transpose(1, 0, 2)),
        "w_out_l": relayout_w(inp["w_out"][0], 128),
        "w_q_l": relayout_w(inp["w_q"][0], 128),
        "w_k_l": relayout_w(inp["w_k"][0], 128),
        "w_v_l": relayout_w(inp["w_v"][0], 512),
        "w_o_l": relayout_w(inp["w_o"][0], 128),
        "w_up_l": relayout_w(inp["w_up"][0], 128),
        "w_down_l": relayout_w(inp["w_down"][0], 128),
    }
    xp, xs = inp["x_prompt"], inp["x_sample"]
    in_maps = []
    for c in range(8):
        b, half = c // 2, c % 2
        sq = slice(16 * c, 16 * c + 16)
        xT = np.zeros((D, NT), f32)
        if half == 1:
            xT[:, :2048] = xp[b].T
        else:
            xT[:, 1024:2048] = xp[b, :1024].T
        xT[:, 2048:] = xs[sq].reshape(128, D).T
        ss = inp["state_shift"][0, sq]
        shp = np.zeros((16, 28 * 128), f32)
        shp[:, :3072] = ss[:, :3072]
        shp[:, 3072:3168] = ss[:, 3072:3168]
        shp[:, 3200:3296] = ss[:, 3168:3264]
        shp[:, 3328:3584] = ss[:, 3264:3520]
        wk = inp["state_wkv"][0, sq]
        wkT = wk.reshape(16, 8, 2, 64, 64).transpose(1, 2, 4, 0, 3).reshape(8, 128, 16, 64)
        m = dict(shared)
        m.update({
            "xT": xT,
            "flag": np.full((128, 1), float(half), f32),
            "memT": np.ascontiguousarray(inp["mem_prompt"][b].T),
            "kTs": np.ascontiguousarray(inp["cache_mem_k"][0, sq].reshape(16, 256, D).transpose(0, 2, 1)),
            "vs": np.ascontiguousarray(inp["cache_mem_v"][0, sq].reshape(16, 256, D)),
            "convst": np.ascontiguousarray(inp["state_conv"][0, sq].transpose(2, 0, 1)),
            "shiftst": np.ascontiguousarray(shp.reshape(16, 28, 128).transpose(2, 1, 0)),
            "wkvT": np.ascontiguousarray(wkT),
            "ffnst": np.ascontiguousarray(inp["state_ffn"][0, sq].reshape(16, 2, 88, 128).transpose(2, 3, 0, 1)),
        })
        in_maps.append(m)
    return in_maps, NV, NCONST


def kernel(**inp):
    f32 = np.float32
    in_maps, NV, NCONST = make_in_maps(inp)
    key = (NV, NCONST)
    if key not in _CACHE:
        _CACHE[key] = build_nc(NV, NCONST)
    nc = _CACHE[key]
    res = run_bass_kernel_spmd(nc, in_maps, core_ids=list(range(8))).results

    y_p = np.zeros((4, 2048, D), f32)
    y_s = np.zeros((128, 8, D), f32)
    conv_p = np.zeros((1, 4, 30, 1024), f32)
    conv_s = np.zeros((1, 128, 30, 1024), f32)
    sh_p = np.zeros((1, 4, 3520), f32)
    sh_s = np.zeros((1, 128, 3520), f32)
    wkv_p = np.zeros((1, 4, 16, 64, 64), f32)
    wkv_s = np.zeros((1, 128, 16, 64, 64), f32)
    ffn_p = np.zeros((1, 4, 2, 11264), f32)
    ffn_s = np.zeros((1, 128, 2, 11264), f32)
    mk = np.zeros((1, 4, 256, 4, 512), f32)
    mvv = np.zeros((1, 4, 256, 4, 512), f32)

    def unshift(a):
        return np.concatenate([a[:3072], a[3072:3168], a[3200:3296], a[3328:3584]], 0)

    for c in range(8):
        r = res[c]
        b, half = c // 2, c % 2
        sq = slice(16 * c, 16 * c + 16)
        yT = r["yT"]
        y_p[b, half * 1024:(half + 1) * 1024] = yT[:, 128:1152].T
        y_s[sq] = yT[:, 1152:1280].T.reshape(16, 8, D)
        conv_s[0, sq] = r["ncs"].transpose(1, 2, 0)
        nshf = r["nsh"].transpose(1, 0, 2).reshape(28 * 128, 17)
        sh_s[0, sq] = unshift(nshf[:, 1:17]).T
        wkv_s[0, sq] = r["nws"].reshape(8, 2, 64, 16, 64).transpose(3, 0, 1, 4, 2).reshape(16, 16, 64, 64)
        ffn_s[0, sq] = r["nfs"].transpose(2, 3, 0, 1).reshape(16, 2, 11264)
        if half == 1:
            conv_p[0, b] = r["ncp"].T
            sh_p[0, b] = unshift(nshf[:, 0:1])[:, 0]
            wkv_p[0, b] = r["nwp"].reshape(8, 2, 64, 64).transpose(0, 1, 3, 2).reshape(16, 64, 64)
            ffn_p[0, b] = r["nfp"].transpose(2, 0, 1).reshape(2, 11264)
            mk[0, b] = r["mkT"].T.reshape(256, 4, 512)
            mvv[0, b] = r["mv"].reshape(256, 4, 512)
    return (y_p, y_s, conv_p, conv_s, sh_p, sh_s, wkv_p, wkv_s, ffn_p, ffn_s, mk, mvv)
```

```python
import numpy as np
from contextlib import ExitStack
import concourse.bass as bass
import concourse.mybir as mybir
from concourse.bass_utils import run_bass_kernel_spmd

F32 = mybir.dt.float32
BF16 = mybir.dt.bfloat16
AF = mybir.ActivationFunctionType
ALU = mybir.AluOpType
ENGS = ("pe", "act", "dve", "pool", "sp")
SERIAL = False
SKIP_MIX = False

D = 2048
KC = 16
NT = 2176
NCH = 1280
CH0 = 896
DFF = 5632
NFC = 44
RMS_EPS = 1e-6
LN_EPS = 1e-5
GN_EPS = 64e-5
LDK = 0.6065306597126334


class Res:
    __slots__ = ("last_write", "reads")

    def __init__(self):
        self.last_write = None
        self.reads = []


class Op:
    __slots__ = ("eng", "fn", "deps", "signal", "idx", "is_dma", "dma_sem", "dma_cnt", "sigcount", "prewait")


class Sched:
    def __init__(self, nc, n_dma_sems=80):
        self.nc = nc
        self.ops = []
        self.per_eng = {e: [] for e in ENGS}
        self.n_dma_sems = n_dma_sems
        self.dma_rr = {"hw": 0, "sw": 0}
        self.dma_uses = [0] * n_dma_sems
        self.dma_last_op = [None] * n_dma_sems
        self.res = {}
        self.pending = {e: set() for e in ENGS}
        self.dma_unconsumed = set()
        self.excl = set("ps%d" % i for i in range(8))

    def R(self, name):
        r = self.res.get(name)
        if r is None:
            r = Res()
            self.res[name] = r
        return r

    def barrier(self):
        last = set()
        for e in ENGS:
            if self.per_eng[e]:
                last.add(self.per_eng[e][-1])
        last |= self.dma_unconsumed
        for e in ENGS:
            self.pending[e] |= last
        self.dma_unconsumed = set()

    def _mk(self, eng, fn, reads, writes, is_dma):
        op = Op()
        op.eng = eng
        op.fn = fn
        op.is_dma = is_dma
        op.signal = False
        op.dma_sem = None
        op.dma_cnt = 0
        op.prewait = None
        oid = len(self.ops)
        deps = set(self.pending[eng])
        self.pending[eng] = set()
        if SERIAL:
            for e_ in ENGS:
                if self.per_eng[e_]:
                    deps.add(self.per_eng[e_][-1])
        writes = list(writes) + [r for r in reads if r in self.excl and r not in writes]
        reads = [r for r in reads if r not in self.excl]
        rl = [self.R(r) for r in reads]
        wl = [self.R(w) for w in writes]
        for r in rl:
            if r.last_write is not None:
                deps.add(r.last_write)
        for w in wl:
            if w.last_write is not None:
                deps.add(w.last_write)
            deps.update(w.reads)
        for r in rl:
            r.reads.append(oid)
        for w in wl:
            w.last_write = oid
            w.reads = []
        deps.discard(oid)
        for d in deps:
            self.dma_unconsumed.discard(d)
        op.deps = deps
        op.idx = len(self.per_eng[eng])
        self.ops.append(op)
        self.per_eng[eng].append(oid)
        if is_dma:
            half = self.n_dma_sems // 2
            kind = "sw" if eng == "pool" else "hw"
            k = self.dma_rr[kind] + (half if kind == "sw" else 0)
            self.dma_rr[kind] = (self.dma_rr[kind] + 1) % half
            op.dma_sem = k
            self.dma_uses[k] += 1
            op.dma_cnt = self.dma_uses[k]
            op.prewait = self.dma_last_op[k]
            self.dma_last_op[k] = oid
            self.dma_unconsumed.add(oid)
        return oid

    def op(self, eng, fn, reads=(), writes=()):
        return self._mk(eng, fn, reads, writes, False)

    def dma(self, eng, fn, reads=(), writes=()):
        return self._mk(eng, fn, reads, writes, True)

    def emit(self):
        nc = self.nc
        ops = self.ops
        known = {e: {f: -1 for f in ENGS} for e in ENGS}
        dma_known = {e: set() for e in ENGS}
        need = []
        for oid, op in enumerate(ops):
            e = op.eng
            lst = []
            cmax = {}
            for d in op.deps:
                dop = ops[d]
                if dop.is_dma:
                    if d not in dma_known[e]:
                        lst.append(("d", d))
                        dma_known[e].add(d)
                else:
                    f = dop.eng
                    if f == "pe" and e == "pe" and not op.is_dma:
                        continue
                    if dop.idx <= known[e][f]:
                        continue
                    if f not in cmax or ops[cmax[f]].idx < dop.idx:
                        cmax[f] = d
            for f, d in cmax.items():
                lst.append(("c", d))
                known[e][f] = ops[d].idx
                ops[d].signal = True
            if op.is_dma and op.prewait is not None and op.prewait not in dma_known[e]:
                lst.append(("d", op.prewait))
                dma_known[e].add(op.prewait)
            need.append(lst)
        for e in ENGS:
            c = 0
            for oid in self.per_eng[e]:
                op = ops[oid]
                if op.signal and not op.is_dma:
                    c += 1
                op.sigcount = c
        with ExitStack() as es:
            csem = {e: es.enter_context(nc.semaphore("cs_" + e)) for e in ENGS}
            dsem = [es.enter_context(nc.semaphore("ds_%d" % i)) for i in range(self.n_dma_sems)]
            block = es.enter_context(nc.Block())

            def run(e, engobj):
                for oid in self.per_eng[e]:
                    op = ops[oid]
                    for kind, d in need[oid]:
                        dop = ops[d]
                        if kind == "c":
                            engobj.wait_ge(csem[dop.eng], dop.sigcount)
                        else:
                            engobj.wait_ge(dsem[dop.dma_sem], 16 * dop.dma_cnt)
                    ins = op.fn(engobj)
                    if op.is_dma:
                        ins.then_inc(dsem[op.dma_sem], 16)
                    elif op.signal:
                        ins.then_inc(csem[e], 1)
                last = {}
                for oid in self.per_eng[e]:
                    op = ops[oid]
                    if op.is_dma:
                        last[op.dma_sem] = max(last.get(op.dma_sem, 0), op.dma_cnt)
                for k, cnt in last.items():
                    engobj.wait_ge(dsem[k], 16 * cnt)

            @block.tensor
            def _(eng):
                run("pe", eng)

            @block.scalar
            def _(eng):
                run("act", eng)

            @block.vector
            def _(eng):
                run("dve", eng)

            @block.gpsimd
            def _(eng):
                run("pool", eng)

            @block.sync
            def _(eng):
                run("sp", eng)


def relayout_w(W, ncol):
    K, N = W.shape
    return np.ascontiguousarray(W.reshape(K // 128, 128, N // ncol, ncol).transpose(2, 1, 0, 3))


def fm(v):
    v = np.asarray(v, np.float32).reshape(-1)
    n = v.shape[0]
    nch = (n + 127) // 128
    p = np.zeros(nch * 128, np.float32)
    p[:n] = v
    return p.reshape(nch, 128).T


VEC_LAYOUT = {}


def build_vecs(inp):
    cols = []
    pos = [0]

    def add(name, arr):
        VEC_LAYOUT[name] = pos[0]
        cols.append(arr)
        pos[0] += arr.shape[1]

    for nm in ["norm_mix_pre", "norm_mix_post", "norm_xa_pre", "norm_xa_post", "norm_ffn_pre",
               "norm_ffn_post", "norm_mem"]:
        add(nm, fm(inp[nm][0]))
    add("conv_dw", np.ascontiguousarray(inp["conv_dw"][0].reshape(31, 8, 128).transpose(2, 1, 0)).reshape(128, 248))
    for nm in ["conv_dw_b", "conv_ln_g", "conv_ln_b"]:
        add(nm, fm(inp[nm][0]))
    mu = inp["rwkv_mu"][0]
    add("mu", np.concatenate([fm(mu[:3072]), fm(mu[3072:3168]), fm(mu[3168:3264]), fm(mu[3264:3520])], 1))
    for nm in ["w0", "a0", "k_k", "k_a", "r_k", "ln_x_g", "ln_x_b"]:
        add(nm, fm(inp[nm][0]))
    add("ffn_dw", np.ascontiguousarray(inp["ffn_dw"][0].reshape(3, 88, 128).transpose(2, 1, 0)).reshape(128, 264))
    add("ffn_dw_b", fm(inp["ffn_dw_b"][0]))
    return np.ascontiguousarray(np.concatenate(cols, 1)), pos[0]


CONST_LAYOUT = {}


def build_consts():
    cols = []
    pos = [0]

    def add(name, arr):
        CONST_LAYOUT[name] = (pos[0], arr.shape[1])
        cols.append(arr.astype(np.float32))
        pos[0] += arr.shape[1]

    i = np.arange(128)
    s, t = i[:, None], i[None, :]
    same = (s // 8) == (t // 8)
    add("ident", (s == t))
    add("su", (t > s))
    add("iu", (t >= s))
    add("sl", (t < s))
    add("su_bd", (t > s) & same)
    add("iu_bd", (t >= s) & same)
    add("sl_bd", (t < s) & same)
    add("blk", (s // 64) == (t // 64))
    sm = np.ones(1280)
    sm[0:1152:128] = 0
    sm[1152:1280:8] = 0
    add("scan", np.broadcast_to(sm[None, :], (128, 1280)))
    add("seqm", (i[:, None] // 8) == np.arange(16)[None, :])
    return np.ascontiguousarray(np.concatenate(cols, 1)), pos[0]


class StopBuild(Exception):
    pass


STOP = None
DBG_TILE = 0
DUMPS = {}


def build_nc(NV, NCONST):
    nc = bass.Bass("TRN2", target_bir_lowering=False)
    S = Sched(nc)

    def ck(name):
        if STOP == name:
            raise StopBuild()

    def dump(name, ap, rd, dt=F32):
        if STOP is None:
            return
        d = nc.dram_tensor("dbg_" + name, list(ap.shape), dt, kind="ExternalOutput").ap()
        S.dma("sp", lambda e: e.dma_start(out=d, in_=ap), rd, [])

    def din(name, shape):
        return nc.dram_tensor(name, list(shape), F32, kind="ExternalInput").ap()

    def dout(name, shape):
        return nc.dram_tensor(name, list(shape), F32, kind="ExternalOutput").ap()

    xT = din("xT", [D, NT])
    flag = din("flag", [128, 1])
    memT = din("memT", [D, 256])
    kTs = din("kTs", [16, D, 256])
    vs = din("vs", [16, 256, D])
    convst = din("convst", [1024, 16, 30])
    shiftst = din("shiftst", [128, 28, 16])
    wkvT = din("wkvT", [8, 128, 16, 64])
    ffnst = din("ffnst", [88, 128, 16, 2])
    vecs_d = din("vecs_in", [128, NV])
    consts_d = din("consts_in", [128, NCONST])
    w_in_l = din("w_in_l", [44, 128, 16, 128])
    w_lora_l = din("w_lora_l", [128, 1024])
    a_lora_l = din("a_lora_l", [128, 1024])
    g_lora_l = din("g_lora_l", [128, 2, 1024])
    w_out_l = din("w_out_l", [16, 128, 16, 128])
    w_q_l = din("w_q_l", [16, 128, 16, 128])
    w_k_l = din("w_k_l", [16, 128, 16, 128])
    w_v_l = din("w_v_l", [4, 128, 16, 512])
    w_o_l = din("w_o_l", [16, 128, 16, 128])
    w_up_l = din("w_up_l", [88, 128, 16, 128])
    w_down_l = din("w_down_l", [16, 128, 44, 128])

    yT = dout("yT", [D, NCH])
    ncp = dout("ncp", [1024, 30])
    ncs = dout("ncs", [1024, 16, 30])
    nsh = dout("nsh", [128, 28, 17])
    nwp = dout("nwp", [8, 128, 64])
    nws = dout("nws", [8, 128, 16, 64])
    nfp = dout("nfp", [88, 128, 2])
    nfs = dout("nfs", [88, 128, 16, 2])
    mkT = dout("mkT", [D, 256])
    mv = dout("mv", [256, D])

    x1T = nc.dram_tensor("x1T", [D, NCH], F32, kind="Internal").ap()
    x2T = nc.dram_tensor("x2T", [D, NCH], F32, kind="Internal").ap()
    mixS = nc.dram_tensor("mixS", [D, NCH], BF16, kind="Internal").ap()

    V = VEC_LAYOUT
    C = CONST_LAYOUT

    def ACT(out, in_, func, rd, wr, bias=None, scale=None):
        kw = {}
        if bias is not None:
            kw["bias"] = bias
        if scale is not None:
            kw["scale"] = scale
        S.op("act", lambda e: e.activation(out=out, in_=in_, func=func, **kw), rd, wr)

    def TT(eng, out, in0, in1, op, rd, wr):
        S.op(eng, lambda e: e.tensor_tensor(out=out, in0=in0, in1=in1, op=op), rd, wr)

    def TS(eng, out, in0, s1, s2, op0, op1, rd, wr):
        if s2 is None:
            S.op(eng, lambda e: e.tensor_scalar(out=out, in0=in0, scalar1=s1, scalar2=None, op0=op0), rd, wr)
        else:
            S.op(eng, lambda e: e.tensor_scalar(out=out, in0=in0, scalar1=s1, scalar2=s2, op0=op0, op1=op1), rd, wr)

    def STT(out, in0, sc, in1, op0, op1, rd, wr):
        S.op("dve", lambda e: e.scalar_tensor_tensor(out=out, in0=in0, scalar=sc, in1=in1, op0=op0, op1=op1), rd, wr)

    def MM(out, lhsT, rhs, start, stop, rd, wr):
        S.op("pe", lambda e: e.matmul(out, lhsT=lhsT, rhs=rhs, start=start, stop=stop), rd, wr)

    def CP(eng, out, in_, rd, wr):
        if eng == "act":
            S.op("act", lambda e: e.copy(out=out, in_=in_), rd, wr)
        else:
            S.op(eng, lambda e: e.tensor_copy(out=out, in_=in_), rd, wr)

    def DMA(q, out, in_, rd, wr):
        S.dma(q, lambda e: e.dma_start(out=out, in_=in_), rd, wr)

    def blocks(n, b=512):
        r = []
        t = 0
        while t < n:
            r.append((t, min(b, n - t)))
            t += b
        return r

    try:
      with ExitStack() as top:
          used_names = {}

          def sbt(es, name, shape, dt):
              k = used_names.get(name, 0)
              used_names[name] = k + 1
              if k:
                  name = "%s_%d" % (name, k + 1)
              return es.enter_context(nc.sbuf_tensor(name, list(shape), dt))

          ps = [top.enter_context(nc.psum_tensor("ps%d" % i, [128, 512], F32)) for i in range(8)]
          PSN = ["ps%d" % i for i in range(8)]

          vecs = sbt(top, "vecs", [128, NV], F32)
          cst = sbt(top, "cst", [128, NCONST], F32)
          DMA("sp", vecs[:], vecs_d, [], ["vecs"])
          DMA("sp", cst[:], consts_d, [], ["cst"])
          flg = sbt(top, "flg", [128, 1], F32)
          DMA("sp", flg[:], flag, [], ["flg"])
          identb = sbt(top, "identb", [128, 128], BF16)
          ident4 = sbt(top, "ident4", [128, 4, 128], BF16)
          onesb = sbt(top, "onesb", [128, 128], BF16)
          blkb = sbt(top, "blkb", [128, 128], BF16)
          omka = sbt(top, "omka", [128, 8], F32)
          ommu = sbt(top, "ommu", [128, 28], F32)

          def cv_(name):
              c0, n = C[name]
              return cst[:, c0:c0 + n]

          CP("dve", identb[:], cv_("ident"), ["cst"], ["identb"])
          for q in range(4):
              CP("dve", ident4[:, q, :], cv_("ident"), ["cst"], ["ident4"])
          S.op("pool", lambda e: e.memset(onesb[:], 1.0), [], ["onesb"])
          CP("dve", blkb[:], cv_("blk"), ["cst"], ["blkb"])
          TS("dve", omka[:], vecs[:, V["k_a"]:V["k_a"] + 8], -1.0, 1.0, ALU.mult, ALU.add, ["vecs"], ["omka"])

          def vcol(name, c):
              return vecs[:, V[name] + c:V[name] + c + 1]

          def prenorm(es, src, n, gname, hT, hoff, tag):
              xc = [sbt(es, "xc%s%d" % (tag, i), [128, n], F32) for i in range(2)]
              sq = [sbt(es, "sq%s%d" % (tag, i), [128, n], BF16) for i in range(2)]
              rs = sbt(es, "rs%s" % tag, [128, n], F32)
              bl = blocks(n)
              for c in range(KC):
                  x_ = xc[c % 2]
                  q_ = sq[c % 2]
                  DMA("sp", x_[:], src[c * 128:(c + 1) * 128, :], [], [x_.name])
                  ACT(q_[:], x_[:], AF.Square, [x_.name], [q_.name])
                  for bi, (t0, nn) in enumerate(bl):
                      MM(ps[bi][:, :nn], onesb[:], q_[:, t0:t0 + nn], c == 0, c == KC - 1, ["onesb", q_.name], [PSN[bi]])
              for bi, (t0, nn) in enumerate(bl):
                  ACT(rs[:, t0:t0 + nn], ps[bi][:, :nn], AF.Sqrt, [PSN[bi]], [rs.name], bias=RMS_EPS, scale=1.0 / D)
              S.op("dve", lambda e: e.reciprocal(out=rs[:], in_=rs[:]), [rs.name], [rs.name])
              for c in range(KC):
                  x_ = xc[c % 2]
                  DMA("sp", x_[:], src[c * 128:(c + 1) * 128, :], [], [x_.name])
                  STT(hT[:, c, hoff:hoff + n], x_[:], vcol(gname, c), rs[:], ALU.mult, ALU.mult,
                      [x_.name, rs.name, "vecs"], [hT.name])

          wctr = [0]

          def load_w(wbufs, src):
              b = wbufs[wctr[0] % len(wbufs)]
              wctr[0] += 1
              n1, n2 = b.shape[1], b.shape[2]
              if n1 * n2 <= 2048:
                  DMA("pool", b[:], src, [], [b.name])
              else:
                  g = max(1, 2048 // n2)
                  for k0 in range(0, n1, g):
                      k1 = min(n1, k0 + g)
                      DMA("pool", b[:, k0:k1, :], src[:, k0:k1, :], [], [b.name])
              return b

          lstack = ExitStack()
          Hf = sbt(lstack, "Hf", [128, 8, 64], F32)
          S.op("pool", lambda e: e.memset(Hf[:], 0.0), [], ["Hf"])
          lw = sbt(lstack, "lw", [128, 1024], BF16)
          la = sbt(lstack, "la", [128, 1024], BF16)
          lg = sbt(lstack, "lg", [128, 2, 1024], BF16)
          DMA("pool", lw[:], w_lora_l, [], ["lw"])
          DMA("pool", la[:], a_lora_l, [], ["la"])
          DMA("pool", lg[:], g_lora_l, [], ["lg"])

          def mix_phase(full):
              es = ExitStack()
              tg = "M" if full else "P"
              n = NCH if full else 896
              ntile = n // 128
              npt = 9 if full else 7
              tok_lo = CH0 if full else 0
              hoff = 768 if full else 0
              hn = 1408 if full else 896
              hT = sbt(es, "hT" + tg, [128, KC, hn], BF16)
              with ExitStack() as e2:
                  prenorm(e2, xT[:, hoff:hoff + hn], hn, "norm_mix_pre", hT, 0, tg)
              dump("hT" + tg, hT[:, 0:2, :], [hT.name], BF16)
              ck("prenorm" + tg)
              S.barrier()
              wb = [sbt(es, "wb%s%d" % (tg, i), [128, KC, 128], BF16) for i in range(4)]
              if full:
                  pbl = [(895, 512), (1407, 512), (1919, 257)]
                  pbase = 895
              else:
                  pbl = [(0, 512), (512, 384)]
                  pbase = -1

              def project(w, prbuf, psl):
                  for bi, (t0, nn) in enumerate(pbl):
                      p_ = psl[bi % len(psl)]
                      for kc in range(KC):
                          MM(ps[p_][:, :nn], w[:, kc, :], hT[:, kc, t0 - hoff:t0 - hoff + nn], kc == 0, kc == KC - 1,
                             [w.name, hT.name], [PSN[p_]])
                      CP("act", prbuf[:, t0 - pbase:t0 - pbase + nn], ps[p_][:, :nn], [PSN[p_]], [prbuf.name])

              T = [sbt(es, "T%s%d" % (tg, i), [128, n], F32) for i in range(6)]
              dtmp = T[5]
              prevS = sbt(es, "prevS" + tg, [128, 16, 8], F32)
              shs = sbt(es, "shs" + tg, [128, 28, 16], F32)
              nshs = sbt(es, "nshs" + tg, [128, 28, 17], F32)
              if full:
                  DMA("sp", shs[:], shiftst, [], [shs.name])

              def shiftmix(prbuf, chunk):
                  mu = vcol("mu", chunk)
                  npr = npt * 128
                  if full:
                      cur_s = prbuf[:, 1 + npr:1 + n].rearrange("p (s t) -> p s t", t=8)
                      CP("pool", nshs[:, chunk, 0:1], prbuf[:, npr:npr + 1], [prbuf.name], [nshs.name])
                      CP("pool", nshs[:, chunk, 1:17], cur_s[:, :, 7], [prbuf.name], [nshs.name])
                      CP("pool", prevS[:, :, 1:8], cur_s[:, :, 0:7], [prbuf.name], [prevS.name])
                      CP("pool", prevS[:, :, 0:1], shs[:, chunk, :].unsqueeze(2), [shs.name], [prevS.name])
                      ds = dtmp[:, npr:n].rearrange("p (s t) -> p s t", t=8)
                      TT("dve", ds, prevS[:], cur_s, ALU.subtract, [prevS.name, prbuf.name], [dtmp.name])
                  TT("dve", dtmp[:, 0:npr], prbuf[:, 0:npr], prbuf[:, 1:1 + npr], ALU.subtract, [prbuf.name], [dtmp.name])
                  STT(prbuf[:, 1:1 + n], dtmp[:], mu, prbuf[:, 1:1 + n], ALU.mult, ALU.add,
                      [dtmp.name, prbuf.name, "vecs"], [prbuf.name])

              pr = [sbt(es, "pr%s%d" % (tg, i), [128, n + 1], F32) for i in range(3)]
              if not full:
                  for i in range(3):
                      S.op("pool", lambda e, i=i: e.memset(pr[i][:, 0:1], 0.0), [], [pr[i].name])

              if full:
                  ec = ExitStack()
                  glu2 = [sbt(ec, "glu%d" % i, [128, 1408], F32) for i in range(2)]
                  glub2 = [sbt(ec, "glub%d" % i, [128, 1408], BF16) for i in range(2)]
                  sg2 = [sbt(ec, "sgc%d" % i, [128, 512], F32) for i in range(2)]
                  fullS2 = [sbt(ec, "fullS%d" % i, [128, 16, 38], F32) for i in range(2)]
                  fullSb2 = [sbt(ec, "fullSb%d" % i, [128, 16, 38], BF16) for i in range(2)]
                  dg2 = [sbt(ec, "dg%d" % i, [128, 31, 128], BF16) for i in range(2)]
                  cvp = sbt(ec, "cvp", [128, 8, NCH], BF16)
                  cbl = [(768, 512), (1280, 512), (1792, 384)]
                  for c in range(8):
                      glu, glub, fullS, fullSb, dg = glu2[c % 2], glub2[c % 2], fullS2[c % 2], fullSb2[c % 2], dg2[c % 2]
                      gN, gbN, fN, fbN, dN = glu.name, glub.name, fullS.name, fullSb.name, dg.name
                      if c == 0:
                          wvg_next = (load_w(wb, w_in_l[0]), load_w(wb, w_in_l[8]))
                      wv, wg = wvg_next
                      if c + 1 < 8:
                          wvg_next = (load_w(wb, w_in_l[c + 1]), load_w(wb, w_in_l[8 + c + 1]))
                      for bq, (t0, nn) in enumerate(cbl):
                          sg = sg2[bq % 2]
                          for kc in range(KC):
                              MM(ps[0][:, :nn], wv[:, kc, :], hT[:, kc, t0 - 768:t0 - 768 + nn], kc == 0, kc == KC - 1,
                                 [wv.name, hT.name], ["ps0"])
                          for kc in range(KC):
                              MM(ps[1][:, :nn], wg[:, kc, :], hT[:, kc, t0 - 768:t0 - 768 + nn], kc == 0, kc == KC - 1,
                                 [wg.name, hT.name], ["ps1"])
                          ACT(sg[:, :nn], ps[1][:, :nn], AF.Sigmoid, ["ps1"], [sg.name])
                          TT("dve", glu[:, t0 - 768:t0 - 768 + nn], ps[0][:, :nn], sg[:, :nn], ALU.mult, ["ps0", sg.name], [gN])
                      CP("act", glub[:], glu[:], [gN], [gbN])
                      DMA("sp", ncp[c * 128:(c + 1) * 128, :], glu[:, 2018 - 768:2048 - 768], [gN], [])
                      DMA("sp", fullS[:, :, 0:30], convst[c * 128:(c + 1) * 128, :, :], [], [fN])
                      CP("pool", fullS[:, :, 30:38], glu[:, 1280:1408].rearrange("p (s t) -> p s t", t=8), [gN], [fN])
                      DMA("sp", ncs[c * 128:(c + 1) * 128, :, :], fullS[:, :, 8:38], [fN], [])
                      CP("act", fullSb[:], fullS[:], [fN], [fbN])
                      wcv = vecs[:, V["conv_dw"] + c * 31:V["conv_dw"] + (c + 1) * 31]
                      TT("dve", dg[:], identb[:].unsqueeze(1).to_broadcast([128, 31, 128]),
                         wcv.unsqueeze(2).to_broadcast([128, 31, 128]), ALU.mult, ["identb", "vecs"], [dN])
                      for (c0, nn) in [(0, 512), (512, 512), (1024, 128)]:
                          g0 = c0 + 128 - 30
                          for j in range(31):
                              MM(ps[2][:, :nn], dg[:, j, :], glub[:, g0 + j:g0 + j + nn], j == 0, j == 30, [dN, gbN], ["ps2"])
                          ACT(cvp[:, c, c0:c0 + nn], ps[2][:, :nn], AF.Identity, ["ps2", "vecs"], ["cvp"], bias=vcol("conv_dw_b", c))
                      for j in range(31):
                          MM(ps[3][:, :128], dg[:, j, :], fullSb[:, :, j:j + 8], j == 0, j == 30, [dN, fbN], ["ps3"])
                      ACT(cvp[:, c, 1152:1280], ps[3][:, :128], AF.Identity, ["ps3", "vecs"], ["cvp"], bias=vcol("conv_dw_b", c))
                  sqb = sbt(ec, "sqb", [128, 512], BF16)
                  mean = sbt(ec, "lnmean", [128, 512], F32)
                  rstd = sbt(ec, "lnrstd", [128, 512], F32)
                  tmpf = sbt(ec, "lntmp", [128, 512], F32)
                  cvo = sbt(ec, "cvo", [128, 512], BF16)
                  for (t0, nn) in blocks(NCH):
                      for c in range(8):
                          ACT(sqb[:, :nn], cvp[:, c, t0:t0 + nn], AF.Square, ["cvp"], ["sqb"])
                          MM(ps[4][:, :nn], onesb[:], cvp[:, c, t0:t0 + nn], c == 0, c == 7, ["onesb", "cvp"], ["ps4"])
                          MM(ps[5][:, :nn], onesb[:], sqb[:, :nn], c == 0, c == 7, ["onesb", "sqb"], ["ps5"])
                      ACT(mean[:, :nn], ps[4][:, :nn], AF.Copy, ["ps4"], ["lnmean"], scale=1.0 / 1024)
                      ACT(tmpf[:, :nn], ps[4][:, :nn], AF.Square, ["ps4"], ["lntmp"], scale=1.0 / 1024)
                      STT(rstd[:, :nn], ps[5][:, :nn], 1.0 / 1024, tmpf[:, :nn], ALU.mult, ALU.subtract, ["ps5", "lntmp"], ["lnrstd"])
                      ACT(rstd[:, :nn], rstd[:, :nn], AF.Sqrt, ["lnrstd"], ["lnrstd"], bias=LN_EPS)
                      S.op("dve", lambda e, nn=nn: e.reciprocal(out=rstd[:, :nn], in_=rstd[:, :nn]), ["lnrstd"], ["lnrstd"])
                      for c in range(8):
                          TT("dve", tmpf[:, :nn], cvp[:, c, t0:t0 + nn], mean[:, :nn], ALU.subtract, ["cvp", "lnmean"], ["lntmp"])
                          TT("dve", tmpf[:, :nn], tmpf[:, :nn], rstd[:, :nn], ALU.mult, ["lntmp", "lnrstd"], ["lntmp"])
                          ACT(cvo[:, :nn], tmpf[:, :nn], AF.Silu, ["lntmp", "vecs"], ["cvo"], bias=vcol("conv_ln_b", c),
                              scale=vcol("conv_ln_g", c))
                          DMA("sp", mixS[c * 128:(c + 1) * 128, t0:t0 + nn], cvo[:, :nn], ["cvo"], ["mixS"])
                  S.barrier()
                  ec.close()
                  ck("convM")

              twd = sbt(es, "twd" + tg, [128, n], BF16)
              adb = sbt(es, "adb" + tg, [128, n], BF16)
              sgd = sbt(es, "sgd" + tg, [128, 2, n], BF16)
              for q in ([40, 41, 42, 43] if full else [40, 41]):
                  w = load_w(wb, w_in_l[q])
                  project(w, pr[0], [0, 1])
                  shiftmix(pr[0], 24 + (q - 40))
                  if q == 40:
                      ACT(twd[:], pr[0][:, 1:1 + n], AF.Tanh, [pr[0].name], [twd.name])
                  elif q == 41:
                      CP("act", adb[:], pr[0][:, 1:1 + n], [pr[0].name], [adb.name])
                  else:
                      ACT(sgd[:, q - 42, :], pr[0][:, 1:1 + n], AF.Sigmoid, [pr[0].name], [sgd.name])
              dump("twd" + tg, twd[:], [twd.name], BF16)
              ck("lora" + tg)

              Tb = sbt(es, "Tb" + tg, [128, n], BF16)
              PAR = sbt(es, "PAR" + tg, [128, ntile, 2, 128], BF16)
              Pb = sbt(es, "Pb" + tg, [128, n], BF16)
              Pk = sbt(es, "Pk" + tg, [128, n], BF16)
              Pv = sbt(es, "Pv" + tg, [128, n], BF16)
              yTb = sbt(es, "yT" + tg, [128, n], F32)
              NBT = 4 if full else 7
              NI = 2 * NBT
              Q = sbt(es, "Q" + tg, [128, NI, 128], BF16)
              QT = sbt(es, "QT" + tg, [128, NI, 128], BF16)
              IQ = sbt(es, "IQ" + tg, [128, NI, 128], BF16)
              Tt = [sbt(es, "Tt%s%d" % (tg, i), [128, NI, 128], BF16) for i in range(2)]
              Aak = sbt(es, "Aak" + tg, [128, NI, 128], BF16)
              Arb = sbt(es, "Arb" + tg, [128, NI, 128], BF16)
              Ark = sbt(es, "Ark" + tg, [128, NI, 128], BF16)

              def gk(buf, gi):
                  return "%s_g%d" % (buf.name, gi)
              tkm = sbt(es, "tkm" + tg, [128, 3, 128], BF16)
              Xb = sbt(es, "Xb" + tg, [128, 128], BF16)
              Ub = sbt(es, "Ub" + tg, [128, 128], BF16)
              Hb = sbt(es, "Hb" + tg, [128, 128], BF16)
              S.op("pool", lambda e: e.memset(Hb[:], 0.0), [], [Hb.name])
              HG = sbt(es, "HG" + tg, [128, 64], F32)
              if full:
                  Hs = sbt(es, "Hs", [128, 16, 64], F32)
                  Hsb = sbt(es, "Hsb", [128, 16, 128], BF16)
                  S.op("pool", lambda e: e.memset(Hsb[:], 0.0), [], ["Hsb"])
                  am = sbt(es, "am", [128, 16, 128], BF16)
                  rm = sbt(es, "rm", [128, 16, 128], BF16)
                  Ue = sbt(es, "Ue", [128, 16, 128], BF16)
                  Ve = sbt(es, "Ve", [128, 16, 128], BF16)
                  S.op("pool", lambda e: e.memset(am[:], 0.0), [], ["am"])
                  S.op("pool", lambda e: e.memset(rm[:], 0.0), [], ["rm"])
              scanm = cv_("scan")

              def proj_loads(cc):
                  lst = []
                  if full:
                      lst.append((load_w(wb, w_in_l[16 + cc]), pr[0]))
                  lst.append((load_w(wb, w_in_l[24 + cc]), pr[1]))
                  lst.append((load_w(wb, w_in_l[32 + cc]), pr[2]))
                  return lst

              def proj_units(lst):
                  u = 0
                  for w, prbuf in lst:
                      for (t0, nn) in pbl:
                          p_ = 6 + (u % 2)
                          u += 1
                          for kc in range(KC):
                              MM(ps[p_][:, :nn], w[:, kc, :], hT[:, kc, t0 - hoff:t0 - hoff + nn], kc == 0, kc == KC - 1,
                                 [w.name, hT.name], [PSN[p_]])
                          CP("act", prbuf[:, t0 - pbase:t0 - pbase + nn], ps[p_][:, :nn], [PSN[p_]], [prbuf.name])
                          yield

              def fill(gen, k=1):
                  if gen is None:
                      return
                  for _ in range(k):
                      try:
                          next(gen)
                      except StopIteration:
                          return

              filler = proj_units(proj_loads(0))
              fill(filler, 100)
              for c in range(8):
                  nxt_loads = proj_loads(c + 1) if c + 1 < 8 else None
                  if full:
                      shiftmix(pr[0], c)
                  shiftmix(pr[1], 8 + c)
                  shiftmix(pr[2], 16 + c)
                  xr = pr[0][:, 1:1 + n]
                  xk = pr[1][:, 1:1 + n]
                  xv = pr[2][:, 1:1 + n]
                  T1, T2, T3, T4, T5, T6 = [t[:] for t in T]
                  n1, n2, n3, n4, n5, n6 = [t.name for t in T]
                  bl = blocks(n)
                  for (t0, nn) in bl:
                      MM(ps[4][:, :nn], lw[0:96, c * 128:(c + 1) * 128], twd[0:96, t0:t0 + nn], True, True, ["lw", twd.name], ["ps4"])
                      ACT(T1[:, t0:t0 + nn], ps[4][:, :nn], AF.Sigmoid, ["ps4", "vecs"], [n1], bias=vcol("w0", c))
                      MM(ps[5][:, :nn], la[0:96, c * 128:(c + 1) * 128], adb[0:96, t0:t0 + nn], True, True, ["la", adb.name], ["ps5"])
                      ACT(T4[:, t0:t0 + nn], ps[5][:, :nn], AF.Sigmoid, ["ps5", "vecs"], [n4], bias=vcol("a0", c))
                  S.op("dve", lambda e: e.tensor_tensor_scan(out=T2, data0=scanm[:, 0:n], data1=T1, initial=0.0,
                                                             op0=ALU.mult, op1=ALU.add), ["cst", n1], [n2])
                  TT("dve", T1, T2, T1, ALU.subtract, [n1, n2], [n1])
                  ACT(T1, T1, AF.Exp, [n1], [n1], scale=-LDK)
                  ACT(T3, T2, AF.Exp, [n2], [n3], scale=-LDK)
                  ACT(T2, T2, AF.Exp, [n2], [n2], scale=LDK)
                  ACT(Tb[:], xk, AF.Square, [pr[1].name, "vecs"], [Tb.name], scale=vcol("k_k", c))
                  for (t0, nn) in bl:
                      MM(ps[6][:, :nn], blkb[:], Tb[:, t0:t0 + nn], True, True, ["blkb", Tb.name], ["ps6"])
                      TS("dve", T5[:, t0:t0 + nn], ps[6][:, :nn], 1e-24, None, ALU.max, None, ["ps6"], [n5])
                  ACT(T5, T5, AF.Sqrt, [n5], [n5])
                  S.op("dve", lambda e: e.reciprocal(out=T5, in_=T5), [n5], [n5])
                  STT(T5, xk, vcol("k_k", c), T5, ALU.mult, ALU.mult, [pr[1].name, n5, "vecs"], [n5])
                  TS("dve", T6, T4, vcol("k_a", c), omka[:, c:c + 1], ALU.mult, ALU.add, [n4, "vecs", "omka"], [n6])
                  TT("dve", T6, T6, xk, ALU.mult, [n6, pr[1].name], [n6])
                  PARa = PAR[:, :, 0, :]
                  PARr = PAR[:, :, 1, :]
                  STT(PARa, T5.rearrange("p (a b) -> p a b", b=128), -1.0, T1.rearrange("p (a b) -> p a b", b=128),
                      ALU.mult, ALU.mult, [n5, n1], [PAR.name])
                  TT("dve", T4, T5, T4, ALU.mult, [n5, n4], [n4])
                  TT("dve", Pb[:], T4, T2, ALU.mult, [n4, n2], [Pb.name])
                  TT("dve", Pk[:], T6, T2, ALU.mult, [n6, n2], [Pk.name])
                  CP("act", Pv[:], xv, [pr[2].name], [Pv.name])
                  if full:
                      TT("dve", PARr, xr.rearrange("p (a b) -> p a b", b=128), T3.rearrange("p (a b) -> p a b", b=128),
                         ALU.mult, [pr[0].name, n3], [PAR.name])
                      STT(Tb[:], xr, vcol("r_k", c), T6, ALU.mult, ALU.mult, [pr[0].name, n6, "vecs"], [Tb.name])
                      for (t0, nn) in bl:
                          MM(ps[6][:, :nn], blkb[:], Tb[:, t0:t0 + nn], True, True, ["blkb", Tb.name], ["ps6"])
                          TT("dve", T6[:, t0:t0 + nn], ps[6][:, :nn], xv[:, t0:t0 + nn], ALU.mult, ["ps6", pr[2].name], [n6])
                      for (t0, nn) in bl:
                          MM(ps[7][:, :nn], lg[:, 0, c * 128:(c + 1) * 128], sgd[:, 0, t0:t0 + nn], True, False, ["lg", sgd.name], ["ps7"])
                          MM(ps[7][:, :nn], lg[:, 1, c * 128:(c + 1) * 128], sgd[:, 1, t0:t0 + nn], False, True, ["lg", sgd.name], ["ps7"])
                          CP("act", T1[:, t0:t0 + nn], ps[7][:, :nn], ["ps7"], [n1])
                      DMA("sp", Hs[:], wkvT[c], [], ["Hs"])
                      CP("act", Hsb[0:64, :, 0:64], Hs[0:64, :, :], ["Hs"], ["Hsb"])
                      CP("act", Hsb[64:128, :, 64:128], Hs[64:128, :, :], ["Hs"], ["Hsb"])
                      for s_ in range(16):
                          CP("pool", am[:, s_, 8 * s_:8 * s_ + 8], PAR[:, ntile - 1, 0, 8 * s_:8 * s_ + 8], [PAR.name], ["am"])
                          CP("pool", rm[:, s_, 8 * s_:8 * s_ + 8], PAR[:, ntile - 1, 1, 8 * s_:8 * s_ + 8], [PAR.name], ["rm"])
                  filler = proj_units(nxt_loads) if nxt_loads is not None else None
                  CP("act", Hb[0:64, 0:64], Hf[0:64, c, :], ["Hf"], [Hb.name])
                  CP("act", Hb[64:128, 64:128], Hf[64:128, c, :], ["Hf"], [Hb.name])
                  if c == 0:
                      dump("PAR" + tg, PAR[:], [PAR.name], BF16)
                      dump("Pb" + tg, Pb[:], [Pb.name], BF16)
                      dump("Pk" + tg, Pk[:], [Pk.name], BF16)
                      dump("T3" + tg, T[2][:], [T[2].name])
                      ck("prep" + tg)

                  for g0 in range(0, ntile, NBT):
                      tiles = [t_ for t_ in range(g0, min(ntile, g0 + NBT))]
                      items = [(t_, hh) for t_ in tiles for hh in (0, 1)]
                      for ii, (t_, hh) in enumerate(items):
                          gi = ii // 4
                          smp = full and t_ == ntile - 1
                          sfx = "_bd" if smp else ""
                          hs = slice(64 * hh, 64 * hh + 64)
                          cs = slice(t_ * 128, t_ * 128 + 128)
                          pa = ii % 2
                          ncol = 256 if full else 128
                          rhs_ar = PAR[hs, t_, :, :].rearrange("p a b -> p (a b)")[:, 0:ncol]
                          MM(ps[pa][:, 0:ncol], Pb[hs, cs], rhs_ar, True, True, [Pb.name, PAR.name], [PSN[pa]])
                          MM(ps[2 + pa][:, 0:ncol], Pk[hs, cs], rhs_ar, True, True, [Pk.name, PAR.name], [PSN[2 + pa]])
                          MM(ps[4 + pa][:, 0:128], PAR[hs, t_, 0, :], Pb[hs, cs], True, True, [Pb.name, PAR.name], [PSN[4 + pa]])
                          TT("dve", QT[:, ii, :], ps[pa][:, 0:128], cv_("su" + sfx), ALU.mult, [PSN[pa], "cst"], [gk(QT, gi)])
                          TT("dve", Aak[:, ii, :], ps[2 + pa][:, 0:128], cv_("su" + sfx), ALU.mult, [PSN[2 + pa], "cst"], [gk(Aak, gi)])
                          TT("dve", Q[:, ii, :], ps[4 + pa][:, 0:128], cv_("sl" + sfx), ALU.mult, [PSN[4 + pa], "cst"], [gk(Q, gi)])
                          if full:
                              TT("dve", Arb[:, ii, :], ps[pa][:, 128:256], cv_("iu" + sfx), ALU.mult, [PSN[pa], "cst"], [gk(Arb, gi)])
                              TT("dve", Ark[:, ii, :], ps[2 + pa][:, 128:256], cv_("iu" + sfx), ALU.mult, [PSN[2 + pa], "cst"], [gk(Ark, gi)])
                      ni_all = len(items)
                      groups = [(gi, gi * 4, min(4, ni_all - gi * 4)) for gi in range((ni_all + 3) // 4)]
                      for gi, i0, ni in groups:
                          TT("dve", Tt[0][:, i0:i0 + ni, :], QT[:, i0:i0 + ni, :], ident4[:, 0:ni, :], ALU.add,
                             [gk(QT, gi), "ident4"], [gk(Tt[0], gi)])
                      cur = 0
                      nlev = 7
                      for k in range(1, nlev):
                          for gi, i0, ni in groups:
                              pb = 3 * (gi % 2)
                              for ii in range(ni):
                                  MM(ps[pb][:, ii * 128:(ii + 1) * 128], QT[:, i0 + ii, :], Q[:, i0 + ii, :], True, True,
                                     [gk(QT, gi), gk(Q, gi)], [PSN[pb]])
                              if k < nlev - 1:
                                  for ii in range(ni):
                                      MM(ps[pb + 1][:, ii * 128:(ii + 1) * 128], Q[:, i0 + ii, :], QT[:, i0 + ii, :], True, True,
                                         [gk(QT, gi), gk(Q, gi)], [PSN[pb + 1]])
                              CP("act", Q[:, i0:i0 + ni, :], ps[pb][:, 0:ni * 128].rearrange("p (a b) -> p a b", b=128), [PSN[pb]], [gk(Q, gi)])
                              if k < nlev - 1:
                                  CP("act" if gi % 2 else "dve", QT[:, i0:i0 + ni, :], ps[pb + 1][:, 0:ni * 128].rearrange("p (a b) -> p a b", b=128),
                                     [PSN[pb + 1]], [gk(QT, gi)])
                              for ii in range(ni):
                                  MM(ps[pb + 2][:, ii * 128:(ii + 1) * 128], Q[:, i0 + ii, :], Tt[cur][:, i0 + ii, :], True, True,
                                     [gk(Q, gi), gk(Tt[cur], gi)], [PSN[pb + 2]])
                              TT("dve", Tt[1 - cur][:, i0:i0 + ni, :], ps[pb + 2][:, 0:ni * 128].rearrange("p (a b) -> p a b", b=128),
                                 Tt[cur][:, i0:i0 + ni, :], ALU.add, [PSN[pb + 2], gk(Tt[cur], gi)], [gk(Tt[1 - cur], gi)])
                          cur = 1 - cur
                      TTf = Tt[cur]
                      if c == 0 and g0 == 0:
                          dump("TTf" + tg, TTf[:, 0:4, :], [gk(TTf, 0)], BF16)
                          dump("Aak" + tg, Aak[:, 0:4, :], [gk(Aak, 0)], BF16)
                          ck("tinv" + tg)
                      for ti, t_ in enumerate(tiles):
                          smp = full and t_ == ntile - 1
                          cs = slice(t_ * 128, t_ * 128 + 128)
                          pT = ps[3].bitcast(BF16)
                          for q_, src_ in enumerate((Pb, Pk, Pv)):
                              S.op("pe", lambda e, q_=q_, src_=src_, cs=cs, pT=pT: e.transpose(out=pT[:, q_ * 128:(q_ + 1) * 128], in_=src_[:, cs],
                                                                                identity=identb[:]), [src_.name, "identb"], ["ps3"])
                          CP("act", tkm[:], pT[:, 0:384].rearrange("p (a b) -> p a b", b=128), ["ps3"], [tkm.name])
                          bT, kT_, vT = tkm[:, 0, :], tkm[:, 1, :], tkm[:, 2, :]
                          if not smp:
                              MM(ps[4][:, 0:128], PAR[:, t_, 0, :], Hb[:], True, False, [PAR.name, Hb.name], ["ps4"])
                          else:
                              for s_ in range(16):
                                  MM(ps[4][:, 0:128], am[:, s_, :], Hsb[:, s_, :], s_ == 0, False, ["am", "Hsb"], ["ps4"])
                          for hh in (0, 1):
                              ii = ti * 2 + hh
                              hs = slice(64 * hh, 64 * hh + 64)
                              MM(ps[4][:, hh * 64:hh * 64 + 64], Aak[:, ii, :], vT[:, hs], False, hh == 1, [gk(Aak, ii // 4), tkm.name], ["ps4"])
                          CP("act", Xb[:], ps[4][:, 0:128], ["ps4"], [Xb.name])
                          if c == 0 and t_ == DBG_TILE:
                              dump("Hbx" + tg, Hb[:], [Hb.name], BF16)
                              dump("PARx" + tg, PAR[:, t_, 0, :], [PAR.name], BF16)
                          for hh in (0, 1):
                              ii = ti * 2 + hh
                              hs = slice(64 * hh, 64 * hh + 64)
                              MM(ps[5][:, hh * 64:hh * 64 + 64], TTf[:, ii, :], Xb[:, hs], True, True, [gk(TTf, ii // 4), Xb.name], ["ps5"])
                          CP("dve", Ub[:], ps[5][:, 0:128], ["ps5"], [Ub.name])
                          if full:
                              for hh in (0, 1):
                                  ii = ti * 2 + hh
                                  hs = slice(64 * hh, 64 * hh + 64)
                                  if not smp:
                                      MM(ps[6][hs, 0:128], Hb[:, hs], PAR[:, t_, 1, :], True, False, [Hb.name, PAR.name], ["ps6"])
                                  else:
                                      for s_ in range(16):
                                          MM(ps[6][hs, 0:128], Hsb[:, s_, hs], rm[:, s_, :], s_ == 0, False, ["Hsb", "rm"], ["ps6"])
                                  MM(ps[6][hs, 0:128], Ub[:, hs], Arb[:, ii, :], False, False, [Ub.name, gk(Arb, ii // 4)], ["ps6"])
                                  MM(ps[6][hs, 0:128], vT[:, hs], Ark[:, ii, :], False, True, [tkm.name, gk(Ark, ii // 4)], ["ps6"])
                              CP("act", yTb[:, cs], ps[6][:, 0:128], ["ps6"], [yTb.name])
                          gam = T[2][:, t_ * 128 + 127:t_ * 128 + 128]
                          if not smp:
                              for hh in (0, 1):
                                  hs = slice(64 * hh, 64 * hh + 64)
                                  MM(ps[7][hs, 0:64], bT[:, hs], Ub[:, hs], True, False, [tkm.name, Ub.name], ["ps7"])
                                  MM(ps[7][hs, 0:64], kT_[:, hs], vT[:, hs], False, True, [tkm.name], ["ps7"])
                              ACT(HG[:], Hf[:, c, :], AF.Copy, ["Hf", n3], [HG.name], scale=gam)
                              STT(Hf[:, c, :], ps[7][:, 0:64], gam, HG[:], ALU.mult, ALU.add, ["ps7", HG.name, n3], ["Hf"])
                              CP("act", Hb[0:64, 0:64], Hf[0:64, c, :], ["Hf"], [Hb.name])
                              CP("act", Hb[64:128, 64:128], Hf[64:128, c, :], ["Hf"], [Hb.name])
                              if c == 0 and t_ == DBG_TILE - 1:
                                  dump("HfE" + tg, Hf[:, 0, :], ["Hf"])
                                  dump("HbE" + tg, Hb[:], [Hb.name], BF16)
                                  dump("HGE" + tg, HG[:], [HG.name])
                              if c == 0 and t_ == DBG_TILE:
                                  dump("tkm" + tg, tkm[:], [tkm.name], BF16)
                                  dump("Xb" + tg, Xb[:], [Xb.name], BF16)
                                  dump("Ub" + tg, Ub[:], [Ub.name], BF16)
                                  dump("Hf0" + tg, Hf[:, 0, :], ["Hf"])
                                  ck("tile0" + tg)
                          else:
                              seqm = cv_("seqm")
                              seqb = seqm.unsqueeze(2).to_broadcast([128, 16, 128])
                              TT("dve", Ue[:], Ub[:].unsqueeze(1).to_broadcast([128, 16, 128]), seqb, ALU.mult, [Ub.name, "cst"], ["Ue"])
                              TT("dve", Ve[:], vT.unsqueeze(1).to_broadcast([128, 16, 128]), seqb, ALU.mult, [tkm.name, "cst"], ["Ve"])
                              gs = T[2][:, (ntile - 1) * 128:n].rearrange("p (s t) -> p s t", t=8)[:, :, 7:8]
                              HsG = T[1][:, 0:1024].rearrange("p (s i) -> p s i", i=64)
                              TT("dve", HsG, Hs[:], gs.to_broadcast([128, 16, 64]), ALU.mult, ["Hs", n3], [n2])
                              for half in (0, 1):
                                  pz = ps[half]
                                  for hh in (0, 1):
                                      hs = slice(64 * hh, 64 * hh + 64)
                                      MM(pz[hs, :], bT[:, hs], Ue[:, 8 * half:8 * half + 8, hs], True, False, [tkm.name, "Ue"], [PSN[half]])
                                      MM(pz[hs, :], kT_[:, hs], Ve[:, 8 * half:8 * half + 8, hs], False, True, [tkm.name, "Ve"], [PSN[half]])
                                  sl = slice(8 * half, 8 * half + 8)
                                  TT("dve", Hs[:, sl, :], pz[:, :].rearrange("p (s i) -> p s i", i=64),
                                     gs[:, sl, :].to_broadcast([128, 8, 64]), ALU.mult, [PSN[half], n3], ["Hs"])
                              TT("dve", Hs[:], Hs[:], HsG, ALU.add, ["Hs", n2], ["Hs"])
                              DMA("sp", nws[c], Hs[:], ["Hs"], [])
                  if c == 0:
                      dump("Hf" + tg, Hf[:, 0, :], ["Hf"])
                      if full:
                          dump("yTb", yTb[:], [yTb.name])
                      ck("pair0" + tg)
                  if full:
                      DMA("sp", nwp[c], Hf[:, c, :], ["Hf"], [])
                      T1, T6 = T[0][:], T[5][:]
                      gns, gnr, rwo = T[1], T[3], Pb
                      ACT(Tb[:], yTb[:], AF.Square, [yTb.name], [Tb.name])
                      CP("act", Pv[:], yTb[:], [yTb.name], [Pv.name])
                      for (t0, nn) in bl:
                          MM(ps[4][:, :nn], blkb[:], Pv[:, t0:t0 + nn], True, True, ["blkb", Pv.name], ["ps4"])
                          MM(ps[5][:, :nn], blkb[:], Tb[:, t0:t0 + nn], True, True, ["blkb", Tb.name], ["ps5"])
                          ACT(gns[:, :nn], ps[4][:, :nn], AF.Square, ["ps4"], [T[1].name], scale=1.0 / 64)
                          STT(gnr[:, :nn], ps[5][:, :nn], 1.0 / 64, gns[:, :nn], ALU.mult, ALU.subtract, ["ps5", T[1].name], [T[3].name])
                          ACT(gnr[:, :nn], gnr[:, :nn], AF.Sqrt, [T[3].name], [T[3].name], bias=GN_EPS)
                          S.op("dve", lambda e, nn=nn: e.reciprocal(out=gnr[:, :nn], in_=gnr[:, :nn]), [T[3].name], [T[3].name])
                          ysl = yTb[:, t0:t0 + nn]
                          STT(ysl, ps[4][:, :nn], -1.0 / 64, ysl, ALU.mult, ALU.add, ["ps4", yTb.name], [yTb.name])
                          TT("dve", ysl, ysl, gnr[:, :nn], ALU.mult, [yTb.name, T[3].name], [yTb.name])
                          TS("dve", ysl, ysl, vcol("ln_x_g", c), vcol("ln_x_b", c), ALU.mult, ALU.add, [yTb.name, "vecs"], [yTb.name])
                          TT("dve", ysl, ysl, T6[:, t0:t0 + nn], ALU.add, [yTb.name, T[5].name], [yTb.name])
                          TT("dve", rwo[:, t0:t0 + nn], ysl, T1[:, t0:t0 + nn], ALU.mult, [yTb.name, T[0].name], [Pb.name])
                      DMA("sp", mixS[1024 + c * 128:1024 + (c + 1) * 128, :], Pb[:], [Pb.name], ["mixS"])
                  fill(filler, 100)
              if full:
                  DMA("sp", nsh, nshs[:], [nshs.name], [])
              S.barrier()
              es.close()

          if not SKIP_MIX:
              mix_phase(False)
              ck("mixP")
              mix_phase(True)
          lstack.close()
          if not SKIP_MIX:
              dump("mixS", mixS, ["mixS"], BF16)
          ck("mixdone")

          def boundary(es, produce, xsrc, gpost, gpre, xdst, hT, final=False):
              sT = sbt(es, "sT", [128, KC, NCH], BF16)
              sq = sbt(es, "bsq", [128, 512], BF16)
              rs = sbt(es, "brs", [128, NCH], F32)
              xc = [sbt(es, "bxc%d" % i, [128, NCH], F32) for i in range(2)]
              sq2 = sbt(es, "bsq2", [128, NCH], BF16)
              bl = blocks(NCH)
              for c in range(KC):
                  def consume(pi, t0, nn, c=c):
                      CP("act", sT[:, c, t0:t0 + nn], ps[pi][:, :nn], [PSN[pi]], ["sT"])
                      ACT(sq[:, :nn], ps[pi][:, :nn], AF.Square, [PSN[pi]], ["bsq"])
                      bi = t0 // 512
                      MM(ps[5 + bi][:, :nn], onesb[:], sq[:, :nn], c == 0, c == KC - 1, ["onesb", "bsq"], [PSN[5 + bi]])
                  produce(c, consume)
              for bi, (t0, nn) in enumerate(bl):
                  ACT(rs[:, t0:t0 + nn], ps[5 + bi][:, :nn], AF.Sqrt, [PSN[5 + bi]], ["brs"], bias=RMS_EPS, scale=1.0 / D)
              S.op("dve", lambda e: e.reciprocal(out=rs[:], in_=rs[:]), ["brs"], ["brs"])
              for c in range(KC):
                  x_ = xc[c % 2]
                  DMA("sp", x_[:], xsrc[c * 128:(c + 1) * 128, :], [], [x_.name])
                  TT("dve", sq2[:], sT[:, c, :], rs[:], ALU.mult, ["sT", "brs"], ["bsq2"])
                  STT(x_[:], sq2[:], vcol(gpost, c), x_[:], ALU.mult, ALU.add, ["bsq2", x_.name, "vecs"], [x_.name])
                  DMA("sp", xdst[c * 128:(c + 1) * 128, :], x_[:], [x_.name], ["xdst"])
                  if not final:
                      ACT(sq2[:], x_[:], AF.Square, [x_.name], ["bsq2"])
                      for bi, (t0, nn) in enumerate(bl):
                          MM(ps[5 + bi][:, :nn], onesb[:], sq2[:, t0:t0 + nn], c == 0, c == KC - 1, ["onesb", "bsq2"], [PSN[5 + bi]])
              if final:
                  return
              for bi, (t0, nn) in enumerate(bl):
                  ACT(rs_keep[:, t0:t0 + nn], ps[5 + bi][:, :nn], AF.Sqrt, [PSN[5 + bi]], ["rs_keep"], bias=RMS_EPS, scale=1.0 / D)
              S.op("dve", lambda e: e.reciprocal(out=rs_keep[:], in_=rs_keep[:]), ["rs_keep"], ["rs_keep"])

          def prenorm_apply(es, hT, xsrc, gpre):
              xc = [sbt(es, "pxc%d" % i, [128, NCH], F32) for i in range(2)]
              for c in range(KC):
                  x_ = xc[c % 2]
                  DMA("sp", x_[:], xsrc[c * 128:(c + 1) * 128, :], ["xdst"], [x_.name])
                  STT(hT[:, c, :], x_[:], vcol(gpre, c), rs_keep[:], ALU.mult, ALU.mult, [x_.name, "rs_keep", "vecs"], [hT.name])

          rs_keep = sbt(top, "rs_keep", [128, NCH], F32)


          with ExitStack() as es:
           if not SKIP_MIX:
              wb = [sbt(es, "wbo%d" % i, [128, KC, 128], BF16) for i in range(3)]
              mx = sbt(es, "mx", [128, KC, NCH], BF16)
              for kc in range(KC):
                  DMA("sp", mx[:, kc, :], mixS[kc * 128:(kc + 1) * 128, :], ["mixS"], ["mx"])

              def prod(c, consume):
                  w = load_w(wb, w_out_l[c])
                  for bi, (t0, nn) in enumerate(blocks(NCH)):
                      pi = (c * 3 + bi) % 4
                      for kc in range(KC):
                          MM(ps[pi][:, :nn], w[:, kc, :], mx[:, kc, t0:t0 + nn], kc == 0, kc == KC - 1, [w.name, "mx"], [PSN[pi]])
                      consume(pi, t0, nn)
              boundary(es, prod, xT[:, CH0:NT], "norm_mix_post", "norm_xa_pre", x1T, None)
              S.barrier()
              dump("x1T", x1T, ["xdst"])
              ck("b1")

          with ExitStack() as eso:
            oT = sbt(eso, "oT", [128, KC, NCH], BF16)
            with ExitStack() as es:
              hT2 = sbt(es, "hT2", [128, KC, NCH], BF16)
              with ExitStack() as e2:
                  if not SKIP_MIX:
                      prenorm_apply(e2, hT2, x1T, "norm_xa_pre")
                  S.barrier()
              ck("pa1")
              kTp = sbt(es, "kTp", [128, KC, 256], BF16)
              vp = sbt(es, "vp", [128, 2, D], BF16)
              with ExitStack() as ea:
                  wbk = [sbt(ea, "wbk%d" % i, [128, KC, 128], BF16) for i in range(3)]
                  mnT = sbt(ea, "mnT", [128, KC, 256], BF16)
                  with ExitStack() as e2:
                      prenorm(e2, memT, 256, "norm_mem", mnT, 0, "mem")
                      S.barrier()
                  ck("mn")
                  ktf = sbt(ea, "ktf", [128, 256], F32)
                  vpf = sbt(ea, "vpf", [128, 512], F32)
                  for c in range(KC):
                      w = load_w(wbk, w_k_l[c])
                      for kc in range(KC):
                          MM(ps[0][:, :256], w[:, kc, :], mnT[:, kc, :], kc == 0, kc == KC - 1, [w.name, "mnT"], ["ps0"])
                      CP("act", kTp[:, c, :], ps[0][:, :256], ["ps0"], ["kTp"])
                      CP("dve", ktf[:], ps[0][:, :256], ["ps0"], ["ktf"])
                      DMA("sp", mkT[c * 128:(c + 1) * 128, :], ktf[:], ["ktf"], [])
                  ck("kproj")
                  wvb = [sbt(ea, "wvb%d" % i, [128, KC, 512], BF16) for i in range(2)]
                  for cb in range(4):
                      w = load_w(wvb, w_v_l[cb])
                      for mt in range(2):
                          for kc in range(KC):
                              MM(ps[1 + mt][:, :], mnT[:, kc, mt * 128:(mt + 1) * 128], w[:, kc, :], kc == 0, kc == KC - 1,
                                 [w.name, "mnT"], [PSN[1 + mt]])
                          CP("act", vp[:, mt, cb * 512:(cb + 1) * 512], ps[1 + mt][:, :], [PSN[1 + mt]], ["vp"])
                          CP("dve", vpf[:], ps[1 + mt][:, :], [PSN[1 + mt]], ["vpf"])
                          DMA("sp", mv[mt * 128:(mt + 1) * 128, cb * 512:(cb + 1) * 512], vpf[:], ["vpf"], [])
                  S.barrier()
              ck("memkv")
              wb = [sbt(es, "wba%d" % i, [128, KC, 128], BF16) for i in range(2)]
              qT = sbt(es, "qT", [128, KC, NCH], BF16)
              for c in range(KC):
                  w = load_w(wb, w_q_l[c])
                  for bi, (t0, nn) in enumerate(blocks(NCH)):
                      pi = 3 + (c * 3 + bi) % 3
                      for kc in range(KC):
                          MM(ps[pi][:, :nn], w[:, kc, :], hT2[:, kc, t0:t0 + nn], kc == 0, kc == KC - 1, [w.name, hT2.name], [PSN[pi]])
                      ACT(qT[:, c, t0:t0 + nn], ps[pi][:, :nn], AF.Copy, [PSN[pi]], ["qT"], scale=512.0 ** -0.5)
              mx8 = sbt(es, "mx8", [128, 8], F32)
              sm8 = sbt(es, "sm8", [128, 8], F32)
              att = sbt(es, "att", [128, 4, 256], BF16)
              attT = sbt(es, "attT", [128, 2, 4, 128], BF16)
              for t_ in range(9):
                  cs = slice(t_ * 128, (t_ + 1) * 128)
                  for h in range(4):
                      pi = h % 2
                      for dc in range(4):
                          MM(ps[pi][:, 0:256], qT[:, 4 * h + dc, cs], kTp[:, 4 * h + dc, :], dc == 0, dc == 3, ["qT", "kTp"], [PSN[pi]])
                      S.op("dve", lambda e, pi=pi, h=h: e.reduce_max(out=mx8[:, h:h + 1], in_=ps[pi][:, 0:256], axis=mybir.AxisListType.X),
                           [PSN[pi]], ["mx8"])
                      TS("dve", mx8[:, 4 + h:5 + h], mx8[:, h:h + 1], -1.0, None, ALU.mult, None, ["mx8"], ["mx8"])
                      S.op("act", lambda e, pi=pi, h=h: e.activation(out=att[:, h, :], in_=ps[pi][:, 0:256], func=AF.Exp,
                                                                    bias=mx8[:, 4 + h:5 + h], accum_out=sm8[:, h:h + 1]),
                           [PSN[pi], "mx8"], ["att", "sm8"])
                      S.op("dve", lambda e, h=h: e.reciprocal(out=sm8[:, 4 + h:5 + h], in_=sm8[:, h:h + 1]), ["sm8"], ["sm8"])
                      TS("dve", att[:, h, :], att[:, h, :], sm8[:, 4 + h:5 + h], None, ALU.mult, None, ["att", "sm8"], ["att"])
                  pT = ps[2].bitcast(BF16)
                  for h in range(4):
                      for mt in range(2):
                          S.op("pe", lambda e, h=h, mt=mt: e.transpose(out=pT[:, (mt * 4 + h) * 128:(mt * 4 + h + 1) * 128],
                                                                      in_=att[:, h, mt * 128:(mt + 1) * 128], identity=identb[:]),
                               ["att", "identb"], ["ps2"])
                  CP("act", attT[:], pT[:, 0:1024].rearrange("p (a b c) -> p a b c", a=2, b=4), ["ps2"], ["attT"])
                  for dc in range(KC):
                      h = dc // 4
                      pi = 3 + dc % 2
                      for mt in range(2):
                          MM(ps[pi][:, 0:128], vp[:, mt, dc * 128:(dc + 1) * 128], attT[:, mt, h, :], mt == 0, mt == 1, ["vp", "attT"], [PSN[pi]])
                      CP("act" if dc % 2 else "dve", oT[:, dc, cs], ps[pi][:, 0:128], [PSN[pi]], ["oT"])
              dump("oTp", oT[:, :, 0:1152], ["oT"], BF16)
              ck("attnP")
              kts = [sbt(es, "kts%d" % i, [128, KC, 256], BF16) for i in range(2)]
              vss = [sbt(es, "vss%d" % i, [128, 2, D], BF16) for i in range(2)]
              sc8 = sbt(es, "sc8", [8, 4, 256], F32)
              at8 = sbt(es, "at8", [8, 4, 256], BF16)
              m8 = sbt(es, "m8", [8, 8], F32)
              a8T = sbt(es, "a8T", [128, 8, 8], BF16)
              for s_ in range(16):
                  kt = kts[s_ % 2]
                  vv = vss[s_ % 2]
                  DMA("pool", kt[:], kTs[s_].rearrange("(c p) m -> p c m", p=128), [], [kt.name])
                  DMA("pool", vv[:], vs[s_].rearrange("(t p) d -> p t d", p=128), [], [vv.name])
                  cs = slice(1152 + 8 * s_, 1152 + 8 * s_ + 8)
                  for h in range(4):
                      pi = h % 2
                      for dc in range(4):
                          MM(ps[pi][0:8, 0:256], qT[:, 4 * h + dc, cs], kt[:, 4 * h + dc, :], dc == 0, dc == 3, ["qT", kt.name], [PSN[pi]])
                      CP("act", sc8[:, h, :], ps[pi][0:8, 0:256], [PSN[pi]], ["sc8"])
                  S.op("dve", lambda e: e.tensor_reduce(out=m8[:, 0:4], in_=sc8[:], axis=mybir.AxisListType.X, op=ALU.max), ["sc8"], ["m8"])
                  TT("dve", sc8[:], sc8[:], m8[:, 0:4].unsqueeze(2).to_broadcast([8, 4, 256]), ALU.subtract, ["sc8", "m8"], ["sc8"])
                  ACT(sc8[:], sc8[:], AF.Exp, ["sc8"], ["sc8"])
                  S.op("dve", lambda e: e.tensor_reduce(out=m8[:, 4:8], in_=sc8[:], axis=mybir.AxisListType.X, op=ALU.add), ["sc8"], ["m8"])
                  S.op("dve", lambda e: e.reciprocal(out=m8[:, 4:8], in_=m8[:, 4:8]), ["m8"], ["m8"])
                  TT("dve", at8[:], sc8[:], m8[:, 4:8].unsqueeze(2).to_broadcast([8, 4, 256]), ALU.mult, ["sc8", "m8"], ["at8"])
                  pT = ps[2].bitcast(BF16)
                  for h in range(4):
                      for mt in range(2):
                          S.op("pe", lambda e, h=h, mt=mt: e.transpose(out=pT[:, (mt * 4 + h) * 8:(mt * 4 + h + 1) * 8],
                                                                      in_=at8[:, h, mt * 128:(mt + 1) * 128], identity=identb[0:8, 0:8]),
                               ["at8", "identb"], ["ps2"])
                  CP("act", a8T[:], pT[:, 0:64].rearrange("p (a b) -> p a b", b=8), ["ps2"], ["a8T"])
                  for dc in range(KC):
                      h = dc // 4
                      for mt in range(2):
                          MM(ps[3][:, dc * 8:dc * 8 + 8], vv[:, mt, dc * 128:(dc + 1) * 128], a8T[:, mt * 4 + h, :], mt == 0, mt == 1,
                             [vv.name, "a8T"], ["ps3"])
                  CP("act", oT[:, :, cs], ps[3][:, 0:128].rearrange("p (a b) -> p a b", b=8), ["ps3"], ["oT"])
              S.barrier()
              dump("oT", oT[:], ["oT"], BF16)
              ck("attnS")
            with ExitStack() as es:
              wb = [sbt(es, "wbo2%d" % i, [128, KC, 128], BF16) for i in range(3)]

              def prod(c, consume):
                  w = load_w(wb, w_o_l[c])
                  for bi, (t0, nn) in enumerate(blocks(NCH)):
                      pi = (c * 3 + bi) % 4
                      for kc in range(KC):
                          MM(ps[pi][:, :nn], w[:, kc, :], oT[:, kc, t0:t0 + nn], kc == 0, kc == KC - 1, [w.name, "oT"], [PSN[pi]])
                      consume(pi, t0, nn)
              boundary(es, prod, x1T, "norm_xa_post", "norm_ffn_pre", x2T, None)
              S.barrier()
              dump("x2T", x2T, ["xdst"])
              ck("b2")

          with ExitStack() as es:
              actT = sbt(es, "actT", [128, NFC, NCH], BF16)
              eu = ExitStack()
              hT2 = sbt(eu, "hT2", [128, KC, NCH], BF16)
              with ExitStack() as e2:
                  prenorm_apply(e2, hT2, x2T, "norm_ffn_pre")
                  S.barrier()
              wb = [sbt(eu, "wbf%d" % i, [128, KC, 128], BF16) for i in range(4)]
              TS("dve", hT2[:, :, 0:128], hT2[:, :, 0:128], flg[:, 0:1], None, ALU.mult, None, [hT2.name, "flg"], [hT2.name])
              up = [sbt(eu, "up%d" % i, [128, 2 + 1152], F32) for i in range(2)]
              ups4 = [[sbt(eu, "ups%d_%d" % (i, q), [128, 16, 10], F32) for q in range(2)] for i in range(2)]
              stgp = [[sbt(eu, "stgp%d_%d" % (i, q), [128, 2], F32) for q in range(2)] for i in range(2)]
              stgs = [[sbt(eu, "stgs%d_%d" % (i, q), [128, 16, 2], F32) for q in range(2)] for i in range(2)]
              stgh = [[sbt(eu, "stgh%d" % i, [128, 16, 2], F32)] * 2 for i in range(2)]
              uc = [sbt(eu, "uc%d" % i, [128, NCH], F32) for i in range(2)]
              for i in range(2):
                  S.op("pool", lambda e, i=i: e.memset(up[i][:, 0:2], 0.0), [], [up[i].name + "A"])
              pctr = [0]
              ws_next = [load_w(wb, w_up_l[0]), load_w(wb, w_up_l[NFC])]
              for c in range(NFC):
                  ws = ws_next
                  if c + 1 < NFC:
                      ws_next = [load_w(wb, w_up_l[c + 1]), load_w(wb, w_up_l[NFC + c + 1])]
                  ups = [ups4[0][c % 2], ups4[1][c % 2]]
                  for vi, ch in enumerate((c, NFC + c)):
                      sh_ = stgh[vi][c % 2]
                      DMA("sp", sh_[:], ffnst[ch], [], [sh_.name])
                      CP("pool", ups[vi][:, :, 0:2], sh_[:], [sh_.name], [ups[vi].name])
                  for reg, rblocks in (("A", [(0, 512), (512, 128)]), ("B", [(640, 512), (1152, 128)])):
                      for vi, ch in enumerate((c, NFC + c)):
                          w = ws[vi]
                          u_, us_, uc_ = up[vi], ups[vi], uc[vi]
                          uk = u_.name + reg
                          for (t0, nn) in rblocks:
                              pi = pctr[0] % 6
                              pctr[0] += 1
                              for kc in range(KC):
                                  MM(ps[pi][:, :nn], w[:, kc, :], hT2[:, kc, t0:t0 + nn], kc == 0, kc == KC - 1, [w.name, hT2.name], [PSN[pi]])
                              if t0 < 1152:
                                  CP("act", u_[:, 2 + t0:2 + t0 + nn], ps[pi][:, :nn], [PSN[pi]], [uk])
                              else:
                                  CP("act", us_[:, :, 2:10], ps[pi][:, :nn].rearrange("p (s t) -> p s t", t=8), [PSN[pi]], [us_.name])
                          fw0 = V["ffn_dw"] + ch * 3
                          wj = [vecs[:, fw0 + j:fw0 + j + 1] for j in range(3)]
                          bj = vcol("ffn_dw_b", ch)
                          ck_ = uc_.name + reg
                          if reg == "A":
                              lo, hi = 0, 640
                              rdk = [uk, "vecs"]
                          else:
                              lo, hi = 640, 1152
                              rdk = [u_.name + "A", uk, "vecs"]
                              sp_, ss_ = stgp[vi][c % 2], stgs[vi][c % 2]
                              CP("pool", sp_[:], u_[:, 1152:1154], [uk], [sp_.name])
                              CP("pool", ss_[:], us_[:, :, 8:10], [us_.name], [ss_.name])
                              DMA("sp", nfp[ch], sp_[:], [sp_.name], [])
                              DMA("sp", nfs[ch], ss_[:], [ss_.name], [])
                          TS("dve", uc_[:, lo:hi], u_[:, 2 + lo:2 + hi], wj[2], bj, ALU.mult, ALU.add, rdk, [ck_])
                          STT(uc_[:, lo:hi], u_[:, 1 + lo:1 + hi], wj[1], uc_[:, lo:hi], ALU.mult, ALU.add, rdk + [ck_], [ck_])
                          STT(uc_[:, lo:hi], u_[:, lo:hi], wj[0], uc_[:, lo:hi], ALU.mult, ALU.add, rdk + [ck_], [ck_])
                          if reg == "B":
                              ucs = uc_[:, 1152:1280].rearrange("p (s t) -> p s t", t=8)
                              TS("dve", ucs, us_[:, :, 2:10], wj[2], bj, ALU.mult, ALU.add, [us_.name, "vecs"], [ck_])
                              STT(ucs, us_[:, :, 1:9], wj[1], ucs, ALU.mult, ALU.add, [us_.name, ck_, "vecs"], [ck_])
                              STT(ucs, us_[:, :, 0:8], wj[0], ucs, ALU.mult, ALU.add, [us_.name, ck_, "vecs"], [ck_])
                      lo, hi = (0, 640) if reg == "A" else (640, 1280)
                      k0, k1 = uc[0].name + reg, uc[1].name + reg
                      ACT(uc[0][:, lo:hi], uc[0][:, lo:hi], AF.Silu, [k0], [k0])
                      TT("dve", actT[:, c, lo:hi], uc[0][:, lo:hi], uc[1][:, lo:hi], ALU.mult, [k0, k1], ["actT" + reg])
              S.barrier()
              dump("actT", actT[:], ["actTA", "actTB"], BF16)
              ck("ffnup")
              eu.close()
              wdb = [sbt(es, "wdb%d" % i, [128, 22, 128], BF16) for i in range(3)]

              def prod(c, consume):
                  wh = [load_w(wdb, w_down_l[c][:, 0:22, :]), load_w(wdb, w_down_l[c][:, 22:44, :])]
                  for bi, (t0, nn) in enumerate(blocks(NCH)):
                      pi = (c * 3 + bi) % 4
                      for kc in range(NFC):
                          w = wh[kc // 22]
                          MM(ps[pi][:, :nn], w[:, kc % 22, :], actT[:, kc, t0:t0 + nn], kc == 0, kc == NFC - 1, [w.name, "actTA", "actTB"], [PSN[pi]])
                      consume(pi, t0, nn)
              boundary(es, prod, x2T, "norm_ffn_post", None, yT, None, final=True)

    except StopBuild:
        pass
    S.emit()
    return nc


_CACHE = {}


def make_in_maps(inp):
    inp = {k: np.asarray(v) for k, v in inp.items()}
    f32 = np.float32
    vecs, NV = build_vecs(inp)
    consts, NCONST = build_consts()
    w_in = inp["w_in"][0]
    Wp = np.zeros((D, 44 * 128), f32)
    Wp[:, :5120] = w_in[:, :5120]
    Wp[:, 5120:5216] = w_in[:, 5120:5216]
    Wp[:, 5248:5344] = w_in[:, 5216:5312]
    Wp[:, 5376:5632] = w_in[:, 5312:5568]
    shared = {
        "vecs_in": vecs, "consts_in": consts,
        "w_in_l": relayout_w(Wp, 128),
        "w_lora_l": np.concatenate([inp["w_lora"][0], np.zeros((32, 1024), f32)], 0),
        "a_lora_l": np.concatenate([inp["a_lora"][0], np.zeros((32, 1024), f32)], 0),
        "g_lora_l": np.ascontiguousarray(inp["g_lora"][0].reshape(2, 128, 1024).transpose(1, 0, 2)),
        "w_out_l": relayout_w(inp["w_out"][0], 128),
        "w_q_l": relayout_w(inp["w_q"][0], 128),
        "w_k_l": relayout_w(inp["w_k"][0], 128),
        "w_v_l": relayout_w(inp["w_v"][0], 512),
        "w_o_l": relayout_w(inp["w_o"][0], 128),
        "w_up_l": relayout_w(inp["w_up"][0], 128),
        "w_down_l": relayout_w(inp["w_down"][0], 128),
    }
    xp, xs = inp["x_prompt"], inp["x_sample"]
    in_maps = []
    for c in range(8):
        b, half = c // 2, c % 2
        sq = slice(16 * c, 16 * c + 16)
        xT = np.zeros((D, NT), f32)
        if half == 1:
            xT[:, :2048] = xp[b].T
        else:
            xT[:, 1024:2048] = xp[b, :1024].T
        xT[:, 2048:] = xs[sq].reshape(128, D).T
        ss = inp["state_shift"][0, sq]
        shp = np.zeros((16, 28 * 128), f32)
        shp[:, :3072] = ss[:, :3072]
        shp[:, 3072:3168] = ss[:, 3072:3168]
        shp[:, 3200:3296] = ss[:, 3168:3264]
        shp[:, 3328:3584] = ss[:, 3264:3520]
        wk = inp["state_wkv"][0, sq]
        wkT = wk.reshape(16, 8, 2, 64, 64).transpose(1, 2, 4, 0, 3).reshape(8, 128, 16, 64)
        m = dict(shared)
        m.update({
            "xT": xT,
            "flag": np.full((128, 1), float(half), f32),
            "memT": np.ascontiguousarray(inp["mem_prompt"][b].T),
            "kTs": np.ascontiguousarray(inp["cache_mem_k"][0, sq].reshape(16, 256, D).transpose(0, 2, 1)),
            "vs": np.ascontiguousarray(inp["cache_mem_v"][0, sq].reshape(16, 256, D)),
            "convst": np.ascontiguousarray(inp["state_conv"][0, sq].transpose(2, 0, 1)),
            "shiftst": np.ascontiguousarray(shp.reshape(16, 28, 128).transpose(2, 1, 0)),
            "wkvT": np.ascontiguousarray(wkT),
            "ffnst": np.ascontiguousarray(inp["state_ffn"][0, sq].reshape(16, 2, 88, 128).transpose(2, 3, 0, 1)),
        })
        in_maps.append(m)
    return in_maps, NV, NCONST


def kernel(**inp):
    f32 = np.float32
    in_maps, NV, NCONST = make_in_maps(inp)
    key = (NV, NCONST)
    if key not in _CACHE:
        _CACHE[key] = build_nc(NV, NCONST)
    nc = _CACHE[key]
    res = run_bass_kernel_spmd(nc, in_maps, core_ids=list(range(8))).results

    y_p = np.zeros((4, 2048, D), f32)
    y_s = np.zeros((128, 8, D), f32)
    conv_p = np.zeros((1, 4, 30, 1024), f32)
    conv_s = np.zeros((1, 128, 30, 1024), f32)
    sh_p = np.zeros((1, 4, 3520), f32)
    sh_s = np.zeros((1, 128, 3520), f32)
    wkv_p = np.zeros((1, 4, 16, 64, 64), f32)
    wkv_s = np.zeros((1, 128, 16, 64, 64), f32)
    ffn_p = np.zeros((1, 4, 2, 11264), f32)
    ffn_s = np.zeros((1, 128, 2, 11264), f32)
    mk = np.zeros((1, 4, 256, 4, 512), f32)
    mvv = np.zeros((1, 4, 256, 4, 512), f32)

    def unshift(a):
        return np.concatenate([a[:3072], a[3072:3168], a[3200:3296], a[3328:3584]], 0)

    for c in range(8):
        r = res[c]
        b, half = c // 2, c % 2
        sq = slice(16 * c, 16 * c + 16)
        yT = r["yT"]
        y_p[b, half * 1024:(half + 1) * 1024] = yT[:, 128:1152].T
        y_s[sq] = yT[:, 1152:1280].T.reshape(16, 8, D)
        conv_s[0, sq] = r["ncs"].transpose(1, 2, 0)
        nshf = r["nsh"].transpose(1, 0, 2).reshape(28 * 128, 17)
        sh_s[0, sq] = unshift(nshf[:, 1:17]).T
        wkv_s[0, sq] = r["nws"].reshape(8, 2, 64, 16, 64).transpose(3, 0, 1, 4, 2).reshape(16, 16, 64, 64)
        ffn_s[0, sq] = r["nfs"].transpose(2, 3, 0, 1).reshape(16, 2, 11264)
        if half == 1:
            conv_p[0, b] = r["ncp"].T
            sh_p[0, b] = unshift(nshf[:, 0:1])[:, 0]
            wkv_p[0, b] = r["nwp"].reshape(8, 2, 64, 64).transpose(0, 1, 3, 2).reshape(16, 64, 64)
            ffn_p[0, b] = r["nfp"].transpose(2, 0, 1).reshape(2, 11264)
            mk[0, b] = r["mkT"].T.reshape(256, 4, 512)
            mvv[0, b] = r["mv"].reshape(256, 4, 512)
    return (y_p, y_s, conv_p, conv_s, sh_p, sh_s, wkv_p, wkv_s, ffn_p, ffn_s, mk, mvv)
```

```python
import numpy as np
from contextlib import ExitStack
import concourse.bass as bass
import concourse.mybir as mybir
from concourse.bass_utils import run_bass_kernel_spmd

F32 = mybir.dt.float32
BF16 = mybir.dt.bfloat16
AF = mybir.ActivationFunctionType
ALU = mybir.AluOpType
ENGS = ("pe", "act", "dve", "pool", "sp")
SERIAL = False
SKIP_MIX = False

D = 2048
KC = 16
NT = 2176
NCH = 1280
CH0 = 896
DFF = 5632
NFC = 44
RMS_EPS = 1e-6
LN_EPS = 1e-5
GN_EPS = 64e-5
LDK = 0.6065306597126334


class Res:
    __slots__ = ("last_write", "reads")

    def __init__(self):
        self.last_write = None
        self.reads = []


class Op:
    __slots__ = ("eng", "fn", "deps", "signal", "idx", "is_dma", "dma_sem", "dma_cnt", "sigcount", "prewait")


class Sched:
    def __init__(self, nc, n_dma_sems=80):
        self.nc = nc
        self.ops = []
        self.per_eng = {e: [] for e in ENGS}
        self.n_dma_sems = n_dma_sems
        self.dma_rr = {"hw": 0, "sw": 0}
        self.dma_uses = [0] * n_dma_sems
        self.dma_last_op = [None] * n_dma_sems
        self.res = {}
        self.pending = {e: set() for e in ENGS}
        self.dma_unconsumed = set()
        self.excl = set("ps%d" % i for i in range(8))

    def R(self, name):
        r = self.res.get(name)
        if r is None:
            r = Res()
            self.res[name] = r
        return r

    def barrier(self):
        last = set()
        for e in ENGS:
            if self.per_eng[e]:
                last.add(self.per_eng[e][-1])
        last |= self.dma_unconsumed
        for e in ENGS:
            self.pending[e] |= last
        self.dma_unconsumed = set()

    def _mk(self, eng, fn, reads, writes, is_dma):
        op = Op()
        op.eng = eng
        op.fn = fn
        op.is_dma = is_dma
        op.signal = False
        op.dma_sem = None
        op.dma_cnt = 0
        op.prewait = None
        oid = len(self.ops)
        deps = set(self.pending[eng])
        self.pending[eng] = set()
        if SERIAL:
            for e_ in ENGS:
                if self.per_eng[e_]:
                    deps.add(self.per_eng[e_][-1])
        writes = list(writes) + [r for r in reads if r in self.excl and r not in writes]
        reads = [r for r in reads if r not in self.excl]
        rl = [self.R(r) for r in reads]
        wl = [self.R(w) for w in writes]
        for r in rl:
            if r.last_write is not None:
                deps.add(r.last_write)
        for w in wl:
            if w.last_write is not None:
                deps.add(w.last_write)
            deps.update(w.reads)
        for r in rl:
            r.reads.append(oid)
        for w in wl:
            w.last_write = oid
            w.reads = []
        deps.discard(oid)
        for d in deps:
            self.dma_unconsumed.discard(d)
        op.deps = deps
        op.idx = len(self.per_eng[eng])
        self.ops.append(op)
        self.per_eng[eng].append(oid)
        if is_dma:
            half = self.n_dma_sems // 2
            kind = "sw" if eng == "pool" else "hw"
            k = self.dma_rr[kind] + (half if kind == "sw" else 0)
            self.dma_rr[kind] = (self.dma_rr[kind] + 1) % half
            op.dma_sem = k
            self.dma_uses[k] += 1
            op.dma_cnt = self.dma_uses[k]
            op.prewait = self.dma_last_op[k]
            self.dma_last_op[k] = oid
            self.dma_unconsumed.add(oid)
        return oid

    def op(self, eng, fn, reads=(), writes=()):
        return self._mk(eng, fn, reads, writes, False)

    def dma(self, eng, fn, reads=(), writes=()):
        return self._mk(eng, fn, reads, writes, True)

    def emit(self):
        nc = self.nc
        ops = self.ops
        known = {e: {f: -1 for f in ENGS} for e in ENGS}
        dma_known = {e: set() for e in ENGS}
        need = []
        for oid, op in enumerate(ops):
            e = op.eng
            lst = []
            cmax = {}
            for d in op.deps:
                dop = ops[d]
                if dop.is_dma:
                    if d not in dma_known[e]:
                        lst.append(("d", d))
                        dma_known[e].add(d)
                else:
                    f = dop.eng
                    if f == "pe" and e == "pe" and not op.is_dma:
                        continue
                    if dop.idx <= known[e][f]:
                        continue
                    if f not in cmax or ops[cmax[f]].idx < dop.idx:
                        cmax[f] = d
            for f, d in cmax.items():
                lst.append(("c", d))
                known[e][f] = ops[d].idx
                ops[d].signal = True
            if op.is_dma and op.prewait is not None and op.prewait not in dma_known[e]:
                lst.append(("d", op.prewait))
                dma_known[e].add(op.prewait)
            need.append(lst)
        for e in ENGS:
            c = 0
            for oid in self.per_eng[e]:
                op = ops[oid]
                if op.signal and not op.is_dma:
                    c += 1
                op.sigcount = c
        with ExitStack() as es:
            csem = {e: es.enter_context(nc.semaphore("cs_" + e)) for e in ENGS}
            dsem = [es.enter_context(nc.semaphore("ds_%d" % i)) for i in range(self.n_dma_sems)]
            block = es.enter_context(nc.Block())

            def run(e, engobj):
                for oid in self.per_eng[e]:
                    op = ops[oid]
                    for kind, d in need[oid]:
                        dop = ops[d]
                        if kind == "c":
                            engobj.wait_ge(csem[dop.eng], dop.sigcount)
                        else:
                            engobj.wait_ge(dsem[dop.dma_sem], 16 * dop.dma_cnt)
                    ins = op.fn(engobj)
                    if op.is_dma:
                        ins.then_inc(dsem[op.dma_sem], 16)
                    elif op.signal:
                        ins.then_inc(csem[e], 1)
                last = {}
                for oid in self.per_eng[e]:
                    op = ops[oid]
                    if op.is_dma:
                        last[op.dma_sem] = max(last.get(op.dma_sem, 0), op.dma_cnt)
                for k, cnt in last.items():
                    engobj.wait_ge(dsem[k], 16 * cnt)

            @block.tensor
            def _(eng):
                run("pe", eng)

            @block.scalar
            def _(eng):
                run("act", eng)

            @block.vector
            def _(eng):
                run("dve", eng)

            @block.gpsimd
            def _(eng):
                run("pool", eng)

            @block.sync
            def _(eng):
                run("sp", eng)


def relayout_w(W, ncol):
    K, N = W.shape
    return np.ascontiguousarray(W.reshape(K // 128, 128, N // ncol, ncol).transpose(2, 1, 0, 3))


def fm(v):
    v = np.asarray(v, np.float32).reshape(-1)
    n = v.shape[0]
    nch = (n + 127) // 128
    p = np.zeros(nch * 128, np.float32)
    p[:n] = v
    return p.reshape(nch, 128).T


VEC_LAYOUT = {}


def build_vecs(inp):
    cols = []
    pos = [0]

    def add(name, arr):
        VEC_LAYOUT[name] = pos[0]
        cols.append(arr)
        pos[0] += arr.shape[1]

    for nm in ["norm_mix_pre", "norm_mix_post", "norm_xa_pre", "norm_xa_post", "norm_ffn_pre",
               "norm_ffn_post", "norm_mem"]:
        add(nm, fm(inp[nm][0]))
    add("conv_dw", np.ascontiguousarray(inp["conv_dw"][0].reshape(31, 8, 128).transpose(2, 1, 0)).reshape(128, 248))
    for nm in ["conv_dw_b", "conv_ln_g", "conv_ln_b"]:
        add(nm, fm(inp[nm][0]))
    mu = inp["rwkv_mu"][0]
    add("mu", np.concatenate([fm(mu[:3072]), fm(mu[3072:3168]), fm(mu[3168:3264]), fm(mu[3264:3520])], 1))
    for nm in ["w0", "a0", "k_k", "k_a", "r_k", "ln_x_g", "ln_x_b"]:
        add(nm, fm(inp[nm][0]))
    add("ffn_dw", np.ascontiguousarray(inp["ffn_dw"][0].reshape(3, 88, 128).transpose(2, 1, 0)).reshape(128, 264))
    add("ffn_dw_b", fm(inp["ffn_dw_b"][0]))
    return np.ascontiguousarray(np.concatenate(cols, 1)), pos[0]


CONST_LAYOUT = {}


def build_consts():
    cols = []
    pos = [0]

    def add(name, arr):
        CONST_LAYOUT[name] = (pos[0], arr.shape[1])
        cols.append(arr.astype(np.float32))
        pos[0] += arr.shape[1]

    i = np.arange(128)
    s, t = i[:, None], i[None, :]
    same = (s // 8) == (t // 8)
    add("ident", (s == t))
    add("su", (t > s))
    add("iu", (t >= s))
    add("sl", (t < s))
    add("su_bd", (t > s) & same)
    add("iu_bd", (t >= s) & same)
    add("sl_bd", (t < s) & same)
    add("blk", (s // 64) == (t // 64))
    sm = np.ones(1280)
    sm[0:1152:128] = 0
    sm[1152:1280:8] = 0
    add("scan", np.broadcast_to(sm[None, :], (128, 1280)))
    add("seqm", (i[:, None] // 8) == np.arange(16)[None, :])
    return np.ascontiguousarray(np.concatenate(cols, 1)), pos[0]


class StopBuild(Exception):
    pass


STOP = None
DBG_TILE = 0
DUMPS = {}


def build_nc(NV, NCONST):
    nc = bass.Bass("TRN2", target_bir_lowering=False)
    S = Sched(nc)

    def ck(name):
        if STOP == name:
            raise StopBuild()

    def dump(name, ap, rd, dt=F32):
        if STOP is None:
            return
        d = nc.dram_tensor("dbg_" + name, list(ap.shape), dt, kind="ExternalOutput").ap()
        S.dma("sp", lambda e: e.dma_start(out=d, in_=ap), rd, [])

    def din(name, shape):
        return nc.dram_tensor(name, list(shape), F32, kind="ExternalInput").ap()

    def dout(name, shape):
        return nc.dram_tensor(name, list(shape), F32, kind="ExternalOutput").ap()

    xT = din("xT", [D, NT])
    flag = din("flag", [128, 1])
    memT = din("memT", [D, 256])
    kTs = din("kTs", [16, D, 256])
    vs = din("vs", [16, 256, D])
    convst = din("convst", [1024, 16, 30])
    shiftst = din("shiftst", [128, 28, 16])
    wkvT = din("wkvT", [8, 128, 16, 64])
    ffnst = din("ffnst", [88, 128, 16, 2])
    vecs_d = din("vecs_in", [128, NV])
    consts_d = din("consts_in", [128, NCONST])
    w_in_l = din("w_in_l", [44, 128, 16, 128])
    w_lora_l = din("w_lora_l", [128, 1024])
    a_lora_l = din("a_lora_l", [128, 1024])
    g_lora_l = din("g_lora_l", [128, 2, 1024])
    w_out_l = din("w_out_l", [16, 128, 16, 128])
    w_q_l = din("w_q_l", [16, 128, 16, 128])
    w_k_l = din("w_k_l", [16, 128, 16, 128])
    w_v_l = din("w_v_l", [4, 128, 16, 512])
    w_o_l = din("w_o_l", [16, 128, 16, 128])
    w_up_l = din("w_up_l", [88, 128, 16, 128])
    w_down_l = din("w_down_l", [16, 128, 44, 128])

    yT = dout("yT", [D, NCH])
    ncp = dout("ncp", [1024, 30])
    ncs = dout("ncs", [1024, 16, 30])
    nsh = dout("nsh", [128, 28, 17])
    nwp = dout("nwp", [8, 128, 64])
    nws = dout("nws", [8, 128, 16, 64])
    nfp = dout("nfp", [88, 128, 2])
    nfs = dout("nfs", [88, 128, 16, 2])
    mkT = dout("mkT", [D, 256])
    mv = dout("mv", [256, D])

    x1T = nc.dram_tensor("x1T", [D, NCH], F32, kind="Internal").ap()
    x2T = nc.dram_tensor("x2T", [D, NCH], F32, kind="Internal").ap()
    mixS = nc.dram_tensor("mixS", [D, NCH], BF16, kind="Internal").ap()

    V = VEC_LAYOUT
    C = CONST_LAYOUT

    def ACT(out, in_, func, rd, wr, bias=None, scale=None):
        kw = {}
        if bias is not None:
            kw["bias"] = bias
        if scale is not None:
            kw["scale"] = scale
        S.op("act", lambda e: e.activation(out=out, in_=in_, func=func, **kw), rd, wr)

    def TT(eng, out, in0, in1, op, rd, wr):
        S.op(eng, lambda e: e.tensor_tensor(out=out, in0=in0, in1=in1, op=op), rd, wr)

    def TS(eng, out, in0, s1, s2, op0, op1, rd, wr):
        if s2 is None:
            S.op(eng, lambda e: e.tensor_scalar(out=out, in0=in0, scalar1=s1, scalar2=None, op0=op0), rd, wr)
        else:
            S.op(eng, lambda e: e.tensor_scalar(out=out, in0=in0, scalar1=s1, scalar2=s2, op0=op0, op1=op1), rd, wr)

    def STT(out, in0, sc, in1, op0, op1, rd, wr):
        S.op("dve", lambda e: e.scalar_tensor_tensor(out=out, in0=in0, scalar=sc, in1=in1, op0=op0, op1=op1), rd, wr)

    def MM(out, lhsT, rhs, start, stop, rd, wr):
        S.op("pe", lambda e: e.matmul(out, lhsT=lhsT, rhs=rhs, start=start, stop=stop), rd, wr)

    def CP(eng, out, in_, rd, wr):
        if eng == "act":
            S.op("act", lambda e: e.copy(out=out, in_=in_), rd, wr)
        else:
            S.op(eng, lambda e: e.tensor_copy(out=out, in_=in_), rd, wr)

    def DMA(q, out, in_, rd, wr):
        S.dma(q, lambda e: e.dma_start(out=out, in_=in_), rd, wr)

    def blocks(n, b=512):
        r = []
        t = 0
        while t < n:
            r.append((t, min(b, n - t)))
            t += b
        return r

    try:
      with ExitStack() as top:
          used_names = {}

          def sbt(es, name, shape, dt):
              k = used_names.get(name, 0)
              used_names[name] = k + 1
              if k:
                  name = "%s_%d" % (name, k + 1)
              return es.enter_context(nc.sbuf_tensor(name, list(shape), dt))

          ps = [top.enter_context(nc.psum_tensor("ps%d" % i, [128, 512], F32)) for i in range(8)]
          PSN = ["ps%d" % i for i in range(8)]

          vecs = sbt(top, "vecs", [128, NV], F32)
          cst = sbt(top, "cst", [128, NCONST], F32)
          DMA("sp", vecs[:], vecs_d, [], ["vecs"])
          DMA("sp", cst[:], consts_d, [], ["cst"])
          flg = sbt(top, "flg", [128, 1], F32)
          DMA("sp", flg[:], flag, [], ["flg"])
          identb = sbt(top, "identb", [128, 128], BF16)
          ident4 = sbt(top, "ident4", [128, 4, 128], BF16)
          onesb = sbt(top, "onesb", [128, 128], BF16)
          blkb = sbt(top, "blkb", [128, 128], BF16)
          omka = sbt(top, "omka", [128, 8], F32)
          ommu = sbt(top, "ommu", [128, 28], F32)

          def cv_(name):
              c0, n = C[name]
              return cst[:, c0:c0 + n]

          CP("dve", identb[:], cv_("ident"), ["cst"], ["identb"])
          for q in range(4):
              CP("dve", ident4[:, q, :], cv_("ident"), ["cst"], ["ident4"])
          S.op("pool", lambda e: e.memset(onesb[:], 1.0), [], ["onesb"])
          CP("dve", blkb[:], cv_("blk"), ["cst"], ["blkb"])
          TS("dve", omka[:], vecs[:, V["k_a"]:V["k_a"] + 8], -1.0, 1.0, ALU.mult, ALU.add, ["vecs"], ["omka"])

          def vcol(name, c):
              return vecs[:, V[name] + c:V[name] + c + 1]

          def prenorm(es, src, n, gname, hT, hoff, tag):
              xc = [sbt(es, "xc%s%d" % (tag, i), [128, n], F32) for i in range(2)]
              sq = [sbt(es, "sq%s%d" % (tag, i), [128, n], BF16) for i in range(2)]
              rs = sbt(es, "rs%s" % tag, [128, n], F32)
              bl = blocks(n)
              for c in range(KC):
                  x_ = xc[c % 2]
                  q_ = sq[c % 2]
                  DMA("sp", x_[:], src[c * 128:(c + 1) * 128, :], [], [x_.name])
                  ACT(q_[:], x_[:], AF.Square, [x_.name], [q_.name])
                  for bi, (t0, nn) in enumerate(bl):
                      MM(ps[bi][:, :nn], onesb[:], q_[:, t0:t0 + nn], c == 0, c == KC - 1, ["onesb", q_.name], [PSN[bi]])
              for bi, (t0, nn) in enumerate(bl):
                  ACT(rs[:, t0:t0 + nn], ps[bi][:, :nn], AF.Sqrt, [PSN[bi]], [rs.name], bias=RMS_EPS, scale=1.0 / D)
              S.op("dve", lambda e: e.reciprocal(out=rs[:], in_=rs[:]), [rs.name], [rs.name])
              for c in range(KC):
                  x_ = xc[c % 2]
                  DMA("sp", x_[:], src[c * 128:(c + 1) * 128, :], [], [x_.name])
                  STT(hT[:, c, hoff:hoff + n], x_[:], vcol(gname, c), rs[:], ALU.mult, ALU.mult,
                      [x_.name, rs.name, "vecs"], [hT.name])

          wctr = [0]

          def load_w(wbufs, src):
              b = wbufs[wctr[0] % len(wbufs)]
              wctr[0] += 1
              n1, n2 = b.shape[1], b.shape[2]
              if n1 * n2 <= 2048:
                  DMA("pool", b[:], src, [], [b.name])
              else:
                  g = max(1, 2048 // n2)
                  for k0 in range(0, n1, g):
                      k1 = min(n1, k0 + g)
                      DMA("pool", b[:, k0:k1, :], src[:, k0:k1, :], [], [b.name])
              return b

          lstack = ExitStack()
          Hf = sbt(lstack, "Hf", [128, 8, 64], F32)
          S.op("pool", lambda e: e.memset(Hf[:], 0.0), [], ["Hf"])
          lw = sbt(lstack, "lw", [128, 1024], BF16)
          la = sbt(lstack, "la", [128, 1024], BF16)
          lg = sbt(lstack, "lg", [128, 2, 1024], BF16)
          DMA("pool", lw[:], w_lora_l, [], ["lw"])
          DMA("pool", la[:], a_lora_l, [], ["la"])
          DMA("pool", lg[:], g_lora_l, [], ["lg"])

          def mix_phase(full):
              es = ExitStack()
              tg = "M" if full else "P"
              n = NCH if full else 896
              ntile = n // 128
              npt = 9 if full else 7
              tok_lo = CH0 if full else 0
              hoff = 768 if full else 0
              hn = 1408 if full else 896
              hT = sbt(es, "hT" + tg, [128, KC, hn], BF16)
              with ExitStack() as e2:
                  prenorm(e2, xT[:, hoff:hoff + hn], hn, "norm_mix_pre", hT, 0, tg)
              dump("hT" + tg, hT[:, 0:2, :], [hT.name], BF16)
              ck("prenorm" + tg)
              S.barrier()
              wb = [sbt(es, "wb%s%d" % (tg, i), [128, KC, 128], BF16) for i in range(4)]
              if full:
                  pbl = [(895, 512), (1407, 512), (1919, 257)]
                  pbase = 895
              else:
                  pbl = [(0, 512), (512, 384)]
                  pbase = -1

              def project(w, prbuf, psl):
                  for bi, (t0, nn) in enumerate(pbl):
                      p_ = psl[bi % len(psl)]
                      for kc in range(KC):
                          MM(ps[p_][:, :nn], w[:, kc, :], hT[:, kc, t0 - hoff:t0 - hoff + nn], kc == 0, kc == KC - 1,
                             [w.name, hT.name], [PSN[p_]])
                      CP("act", prbuf[:, t0 - pbase:t0 - pbase + nn], ps[p_][:, :nn], [PSN[p_]], [prbuf.name])

              T = [sbt(es, "T%s%d" % (tg, i), [128, n], F32) for i in range(6)]
              dtmp = T[5]
              prevS = sbt(es, "prevS" + tg, [128, 16, 8], F32)
              shs = sbt(es, "shs" + tg, [128, 28, 16], F32)
              nshs = sbt(es, "nshs" + tg, [128, 28, 17], F32)
              if full:
                  DMA("sp", shs[:], shiftst, [], [shs.name])

              def shiftmix(prbuf, chunk):
                  mu = vcol("mu", chunk)
                  npr = npt * 128
                  if full:
                      cur_s = prbuf[:, 1 + npr:1 + n].rearrange("p (s t) -> p s t", t=8)
                      CP("pool", nshs[:, chunk, 0:1], prbuf[:, npr:npr + 1], [prbuf.name], [nshs.name])
                      CP("pool", nshs[:, chunk, 1:17], cur_s[:, :, 7], [prbuf.name], [nshs.name])
                      CP("pool", prevS[:, :, 1:8], cur_s[:, :, 0:7], [prbuf.name], [prevS.name])
                      CP("pool", prevS[:, :, 0:1], shs[:, chunk, :].unsqueeze(2), [shs.name], [prevS.name])
                      ds = dtmp[:, npr:n].rearrange("p (s t) -> p s t", t=8)
                      TT("dve", ds, prevS[:], cur_s, ALU.subtract, [prevS.name, prbuf.name], [dtmp.name])
                  TT("dve", dtmp[:, 0:npr], prbuf[:, 0:npr], prbuf[:, 1:1 + npr], ALU.subtract, [prbuf.name], [dtmp.name])
                  STT(prbuf[:, 1:1 + n], dtmp[:], mu, prbuf[:, 1:1 + n], ALU.mult, ALU.add,
                      [dtmp.name, prbuf.name, "vecs"], [prbuf.name])

              pr = [sbt(es, "pr%s%d" % (tg, i), [128, n + 1], F32) for i in range(3)]
              if not full:
                  for i in range(3):
                      S.op("pool", lambda e, i=i: e.memset(pr[i][:, 0:1], 0.0), [], [pr[i].name])

              if full:
                  ec = ExitStack()
                  glu2 = [sbt(ec, "glu%d" % i, [128, 1408], F32) for i in range(2)]
                  glub2 = [sbt(ec, "glub%d" % i, [128, 1408], BF16) for i in range(2)]
                  sg2 = [sbt(ec, "sgc%d" % i, [128, 512], F32) for i in range(2)]
                  fullS2 = [sbt(ec, "fullS%d" % i, [128, 16, 38], F32) for i in range(2)]
                  fullSb2 = [sbt(ec, "fullSb%d" % i, [128, 16, 38], BF16) for i in range(2)]
                  dg2 = [sbt(ec, "dg%d" % i, [128, 31, 128], BF16) for i in range(2)]
                  cvp = sbt(ec, "cvp", [128, 8, NCH], BF16)
                  cbl = [(768, 512), (1280, 512), (1792, 384)]
                  for c in range(8):
                      glu, glub, fullS, fullSb, dg = glu2[c % 2], glub2[c % 2], fullS2[c % 2], fullSb2[c % 2], dg2[c % 2]
                      gN, gbN, fN, fbN, dN = glu.name, glub.name, fullS.name, fullSb.name, dg.name
                      if c == 0:
                          wvg_next = (load_w(wb, w_in_l[0]), load_w(wb, w_in_l[8]))
                      wv, wg = wvg_next
                      if c + 1 < 8:
                          wvg_next = (load_w(wb, w_in_l[c + 1]), load_w(wb, w_in_l[8 + c + 1]))
                      for bq, (t0, nn) in enumerate(cbl):
                          sg = sg2[bq % 2]
                          for kc in range(KC):
                              MM(ps[0][:, :nn], wv[:, kc, :], hT[:, kc, t0 - 768:t0 - 768 + nn], kc == 0, kc == KC - 1,
                                 [wv.name, hT.name], ["ps0"])
                          for kc in range(KC):
                              MM(ps[1][:, :nn], wg[:, kc, :], hT[:, kc, t0 - 768:t0 - 768 + nn], kc == 0, kc == KC - 1,
                                 [wg.name, hT.name], ["ps1"])
                          ACT(sg[:, :nn], ps[1][:, :nn], AF.Sigmoid, ["ps1"], [sg.name])
                          TT("dve", glu[:, t0 - 768:t0 - 768 + nn], ps[0][:, :nn], sg[:, :nn], ALU.mult, ["ps0", sg.name], [gN])
                      CP("act", glub[:], glu[:], [gN], [gbN])
                      DMA("sp", ncp[c * 128:(c + 1) * 128, :], glu[:, 2018 - 768:2048 - 768], [gN], [])
                      DMA("sp", fullS[:, :, 0:30], convst[c * 128:(c + 1) * 128, :, :], [], [fN])
                      CP("pool", fullS[:, :, 30:38], glu[:, 1280:1408].rearrange("p (s t) -> p s t", t=8), [gN], [fN])
                      DMA("sp", ncs[c * 128:(c + 1) * 128, :, :], fullS[:, :, 8:38], [fN], [])
                      CP("act", fullSb[:], fullS[:], [fN], [fbN])
                      wcv = vecs[:, V["conv_dw"] + c * 31:V["conv_dw"] + (c + 1) * 31]
                      TT("dve", dg[:], identb[:].unsqueeze(1).to_broadcast([128, 31, 128]),
                         wcv.unsqueeze(2).to_broadcast([128, 31, 128]), ALU.mult, ["identb", "vecs"], [dN])
                      for (c0, nn) in [(0, 512), (512, 512), (1024, 128)]:
                          g0 = c0 + 128 - 30
                          for j in range(31):
                              MM(ps[2][:, :nn], dg[:, j, :], glub[:, g0 + j:g0 + j + nn], j == 0, j == 30, [dN, gbN], ["ps2"])
                          ACT(cvp[:, c, c0:c0 + nn], ps[2][:, :nn], AF.Identity, ["ps2", "vecs"], ["cvp"], bias=vcol("conv_dw_b", c))
                      for j in range(31):
                          MM(ps[3][:, :128], dg[:, j, :], fullSb[:, :, j:j + 8], j == 0, j == 30, [dN, fbN], ["ps3"])
                      ACT(cvp[:, c, 1152:1280], ps[3][:, :128], AF.Identity, ["ps3", "vecs"], ["cvp"], bias=vcol("conv_dw_b", c))
                  sqb = sbt(ec, "sqb", [128, 512], BF16)
                  mean = sbt(ec, "lnmean", [128, 512], F32)
                  rstd = sbt(ec, "lnrstd", [128, 512], F32)
                  tmpf = sbt(ec, "lntmp", [128, 512], F32)
                  cvo = sbt(ec, "cvo", [128, 512], BF16)
                  for (t0, nn) in blocks(NCH):
                      for c in range(8):
                          ACT(sqb[:, :nn], cvp[:, c, t0:t0 + nn], AF.Square, ["cvp"], ["sqb"])
                          MM(ps[4][:, :nn], onesb[:], cvp[:, c, t0:t0 + nn], c == 0, c == 7, ["onesb", "cvp"], ["ps4"])
                          MM(ps[5][:, :nn], onesb[:], sqb[:, :nn], c == 0, c == 7, ["onesb", "sqb"], ["ps5"])
                      ACT(mean[:, :nn], ps[4][:, :nn], AF.Copy, ["ps4"], ["lnmean"], scale=1.0 / 1024)
                      ACT(tmpf[:, :nn], ps[4][:, :nn], AF.Square, ["ps4"], ["lntmp"], scale=1.0 / 1024)
                      STT(rstd[:, :nn], ps[5][:, :nn], 1.0 / 1024, tmpf[:, :nn], ALU.mult, ALU.subtract, ["ps5", "lntmp"], ["lnrstd"])
                      ACT(rstd[:, :nn], rstd[:, :nn], AF.Sqrt, ["lnrstd"], ["lnrstd"], bias=LN_EPS)
                      S.op("dve", lambda e, nn=nn: e.reciprocal(out=rstd[:, :nn], in_=rstd[:, :nn]), ["lnrstd"], ["lnrstd"])
                      for c in range(8):
                          TT("dve", tmpf[:, :nn], cvp[:, c, t0:t0 + nn], mean[:, :nn], ALU.subtract, ["cvp", "lnmean"], ["lntmp"])
                          TT("dve", tmpf[:, :nn], tmpf[:, :nn], rstd[:, :nn], ALU.mult, ["lntmp", "lnrstd"], ["lntmp"])
                          ACT(cvo[:, :nn], tmpf[:, :nn], AF.Silu, ["lntmp", "vecs"], ["cvo"], bias=vcol("conv_ln_b", c),
                              scale=vcol("conv_ln_g", c))
                          DMA("sp", mixS[c * 128:(c + 1) * 128, t0:t0 + nn], cvo[:, :nn], ["cvo"], ["mixS"])
                  S.barrier()
                  ec.close()
                  ck("convM")

              twd = sbt(es, "twd" + tg, [128, n], BF16)
              adb = sbt(es, "adb" + tg, [128, n], BF16)
              sgd = sbt(es, "sgd" + tg, [128, 2, n], BF16)
              for q in ([40, 41, 42, 43] if full else [40, 41]):
                  w = load_w(wb, w_in_l[q])
                  project(w, pr[0], [0, 1])
                  shiftmix(pr[0], 24 + (q - 40))
                  if q == 40:
                      ACT(twd[:], pr[0][:, 1:1 + n], AF.Tanh, [pr[0].name], [twd.name])
                  elif q == 41:
                      CP("act", adb[:], pr[0][:, 1:1 + n], [pr[0].name], [adb.name])
                  else:
                      ACT(sgd[:, q - 42, :], pr[0][:, 1:1 + n], AF.Sigmoid, [pr[0].name], [sgd.name])
              dump("twd" + tg, twd[:], [twd.name], BF16)
              ck("lora" + tg)

              Tb = sbt(es, "Tb" + tg, [128, n], BF16)
              PAR = sbt(es, "PAR" + tg, [128, ntile, 2, 128], BF16)
              Pb = sbt(es, "Pb" + tg, [128, n], BF16)
              Pk = sbt(es, "Pk" + tg, [128, n], BF16)
              Pv = sbt(es, "Pv" + tg, [128, n], BF16)
              yTb = sbt(es, "yT" + tg, [128, n], F32)
              NBT = 4 if full else 7
              NI = 2 * NBT
              Q = sbt(es, "Q" + tg, [128, NI, 128], BF16)
              QT = sbt(es, "QT" + tg, [128, NI, 128], BF16)
              IQ = sbt(es, "IQ" + tg, [128, NI, 128], BF16)
              Tt = [sbt(es, "Tt%s%d" % (tg, i), [128, NI, 128], BF16) for i in range(2)]
              Aak = sbt(es, "Aak" + tg, [128, NI, 128], BF16)
              Arb = sbt(es, "Arb" + tg, [128, NI, 128], BF16)
              Ark = sbt(es, "Ark" + tg, [128, NI, 128], BF16)

              def gk(buf, gi):
                  return "%s_g%d" % (buf.name, gi)
              tkm = sbt(es, "tkm" + tg, [128, 3, 128], BF16)
              Xb = sbt(es, "Xb" + tg, [128, 128], BF16)
              Ub = sbt(es, "Ub" + tg, [128, 128], BF16)
              Hb = sbt(es, "Hb" + tg, [128, 128], BF16)
              S.op("pool", lambda e: e.memset(Hb[:], 0.0), [], [Hb.name])
              HG = sbt(es, "HG" + tg, [128, 64], F32)
              if full:
                  Hs = sbt(es, "Hs", [128, 16, 64], F32)
                  Hsb = sbt(es, "Hsb", [128, 16, 128], BF16)
                  S.op("pool", lambda e: e.memset(Hsb[:], 0.0), [], ["Hsb"])
                  am = sbt(es, "am", [128, 16, 128], BF16)
                  rm = sbt(es, "rm", [128, 16, 128], BF16)
                  Ue = sbt(es, "Ue", [128, 16, 128], BF16)
                  Ve = sbt(es, "Ve", [128, 16, 128], BF16)
                  S.op("pool", lambda e: e.memset(am[:], 0.0), [], ["am"])
                  S.op("pool", lambda e: e.memset(rm[:], 0.0), [], ["rm"])
              scanm = cv_("scan")

              def proj_loads(cc):
                  lst = []
                  if full:
                      lst.append((load_w(wb, w_in_l[16 + cc]), pr[0]))
                  lst.append((load_w(wb, w_in_l[24 + cc]), pr[1]))
                  lst.append((load_w(wb, w_in_l[32 + cc]), pr[2]))
                  return lst

              def proj_units(lst):
                  u = 0
                  for w, prbuf in lst:
                      for (t0, nn) in pbl:
                          p_ = 6 + (u % 2)
                          u += 1
                          for kc in range(KC):
                              MM(ps[p_][:, :nn], w[:, kc, :], hT[:, kc, t0 - hoff:t0 - hoff + nn], kc == 0, kc == KC - 1,
                                 [w.name, hT.name], [PSN[p_]])
                          CP("act", prbuf[:, t0 - pbase:t0 - pbase + nn], ps[p_][:, :nn], [PSN[p_]], [prbuf.name])
                          yield

              def fill(gen, k=1):
                  if gen is None:
                      return
                  for _ in range(k):
                      try:
                          next(gen)
                      except StopIteration:
                          return

              filler = proj_units(proj_loads(0))
              fill(filler, 100)
              for c in range(8):
                  nxt_loads = proj_loads(c + 1) if c + 1 < 8 else None
                  if full:
                      shiftmix(pr[0], c)
                  shiftmix(pr[1], 8 + c)
                  shiftmix(pr[2], 16 + c)
                  xr = pr[0][:, 1:1 + n]
                  xk = pr[1][:, 1:1 + n]
                  xv = pr[2][:, 1:1 + n]
                  T1, T2, T3, T4, T5, T6 = [t[:] for t in T]
                  n1, n2, n3, n4, n5, n6 = [t.name for t in T]
                  bl = blocks(n)
                  for (t0, nn) in bl:
                      MM(ps[4][:, :nn], lw[0:96, c * 128:(c + 1) * 128], twd[0:96, t0:t0 + nn], True, True, ["lw", twd.name], ["ps4"])
                      ACT(T1[:, t0:t0 + nn], ps[4][:, :nn], AF.Sigmoid, ["ps4", "vecs"], [n1], bias=vcol("w0", c))
                      MM(ps[5][:, :nn], la[0:96, c * 128:(c + 1) * 128], adb[0:96, t0:t0 + nn], True, True, ["la", adb.name], ["ps5"])
                      ACT(T4[:, t0:t0 + nn], ps[5][:, :nn], AF.Sigmoid, ["ps5", "vecs"], [n4], bias=vcol("a0", c))
                  S.op("dve", lambda e: e.tensor_tensor_scan(out=T2, data0=scanm[:, 0:n], data1=T1, initial=0.0,
                                                             op0=ALU.mult, op1=ALU.add), ["cst", n1], [n2])
                  TT("dve", T1, T2, T1, ALU.subtract, [n1, n2], [n1])
                  ACT(T1, T1, AF.Exp, [n1], [n1], scale=-LDK)
                  ACT(T3, T2, AF.Exp, [n2], [n3], scale=-LDK)
                  ACT(T2, T2, AF.Exp, [n2], [n2], scale=LDK)
                  ACT(Tb[:], xk, AF.Square, [pr[1].name, "vecs"], [Tb.name], scale=vcol("k_k", c))
                  for (t0, nn) in bl:
                      MM(ps[6][:, :nn], blkb[:], Tb[:, t0:t0 + nn], True, True, ["blkb", Tb.name], ["ps6"])
                      TS("dve", T5[:, t0:t0 + nn], ps[6][:, :nn], 1e-24, None, ALU.max, None, ["ps6"], [n5])
                  ACT(T5, T5, AF.Sqrt, [n5], [n5])
                  S.op("dve", lambda e: e.reciprocal(out=T5, in_=T5), [n5], [n5])
                  STT(T5, xk, vcol("k_k", c), T5, ALU.mult, ALU.mult, [pr[1].name, n5, "vecs"], [n5])
                  TS("dve", T6, T4, vcol("k_a", c), omka[:, c:c + 1], ALU.mult, ALU.add, [n4, "vecs", "omka"], [n6])
                  TT("dve", T6, T6, xk, ALU.mult, [n6, pr[1].name], [n6])
                  PARa = PAR[:, :, 0, :]
                  PARr = PAR[:, :, 1, :]
                  STT(PARa, T5.rearrange("p (a b) -> p a b", b=128), -1.0, T1.rearrange("p (a b) -> p a b", b=128),
                      ALU.mult, ALU.mult, [n5, n1], [PAR.name])
                  TT("dve", T4, T5, T4, ALU.mult, [n5, n4], [n4])
                  TT("dve", Pb[:], T4, T2, ALU.mult, [n4, n2], [Pb.name])
                  TT("dve", Pk[:], T6, T2, ALU.mult, [n6, n2], [Pk.name])
                  CP("act", Pv[:], xv, [pr[2].name], [Pv.name])
                  if full:
                      TT("dve", PARr, xr.rearrange("p (a b) -> p a b", b=128), T3.rearrange("p (a b) -> p a b", b=128),
                         ALU.mult, [pr[0].name, n3], [PAR.name])
                      STT(Tb[:], xr, vcol("r_k", c), T6, ALU.mult, ALU.mult, [pr[0].name, n6, "vecs"], [Tb.name])
                      for (t0, nn) in bl:
                          MM(ps[6][:, :nn], blkb[:], Tb[:, t0:t0 + nn], True, True, ["blkb", Tb.name], ["ps6"])
                          TT("dve", T6[:, t0:t0 + nn], ps[6][:, :nn], xv[:, t0:t0 + nn], ALU.mult, ["ps6", pr[2].name], [n6])
                      for (t0, nn) in bl:
                          MM(ps[7][:, :nn], lg[:, 0, c * 128:(c + 1) * 128], sgd[:, 0, t0:t0 + nn], True, False, ["lg", sgd.name], ["ps7"])
                          MM(ps[7][:, :nn], lg[:, 1, c * 128:(c + 1) * 128], sgd[:, 1, t0:t0 + nn], False, True, ["lg", sgd.name], ["ps7"])
                          CP("act", T1[:, t0:t0 + nn], ps[7][:, :nn], ["ps7"], [n1])
                      DMA("sp", Hs[:], wkvT[c], [], ["Hs"])
                      CP("act", Hsb[0:64, :, 0:64], Hs[0:64, :, :], ["Hs"], ["Hsb"])
                      CP("act", Hsb[64:128, :, 64:128], Hs[64:128, :, :], ["Hs"], ["Hsb"])
                      for s_ in range(16):
                          CP("pool", am[:, s_, 8 * s_:8 * s_ + 8], PAR[:, ntile - 1, 0, 8 * s_:8 * s_ + 8], [PAR.name], ["am"])
                          CP("pool", rm[:, s_, 8 * s_:8 * s_ + 8], PAR[:, ntile - 1, 1, 8 * s_:8 * s_ + 8], [PAR.name], ["rm"])
                  filler = proj_units(nxt_loads) if nxt_loads is not None else None
                  CP("act", Hb[0:64, 0:64], Hf[0:64, c, :], ["Hf"], [Hb.name])
                  CP("act", Hb[64:128, 64:128], Hf[64:128, c, :], ["Hf"], [Hb.name])
                  if c == 0:
                      dump("PAR" + tg, PAR[:], [PAR.name], BF16)
                      dump("Pb" + tg, Pb[:], [Pb.name], BF16)
                      dump("Pk" + tg, Pk[:], [Pk.name], BF16)
                      dump("T3" + tg, T[2][:], [T[2].name])
                      ck("prep" + tg)

                  for g0 in range(0, ntile, NBT):
                      tiles = [t_ for t_ in range(g0, min(ntile, g0 + NBT))]
                      items = [(t_, hh) for t_ in tiles for hh in (0, 1)]
                      for ii, (t_, hh) in enumerate(items):
                          gi = ii // 4
                          smp = full and t_ == ntile - 1
                          sfx = "_bd" if smp else ""
                          hs = slice(64 * hh, 64 * hh + 64)
                          cs = slice(t_ * 128, t_ * 128 + 128)
                          pa = ii % 2
                          ncol = 256 if full else 128
                          rhs_ar = PAR[hs, t_, :, :].rearrange("p a b -> p (a b)")[:, 0:ncol]
                          MM(ps[pa][:, 0:ncol], Pb[hs, cs], rhs_ar, True, True, [Pb.name, PAR.name], [PSN[pa]])
                          MM(ps[2 + pa][:, 0:ncol], Pk[hs, cs], rhs_ar, True, True, [Pk.name, PAR.name], [PSN[2 + pa]])
                          MM(ps[4 + pa][:, 0:128], PAR[hs, t_, 0, :], Pb[hs, cs], True, True, [Pb.name, PAR.name], [PSN[4 + pa]])
                          TT("dve", QT[:, ii, :], ps[pa][:, 0:128], cv_("su" + sfx), ALU.mult, [PSN[pa], "cst"], [gk(QT, gi)])
                          TT("dve", Aak[:, ii, :], ps[2 + pa][:, 0:128], cv_("su" + sfx), ALU.mult, [PSN[2 + pa], "cst"], [gk(Aak, gi)])
                          TT("dve", Q[:, ii, :], ps[4 + pa][:, 0:128], cv_("sl" + sfx), ALU.mult, [PSN[4 + pa], "cst"], [gk(Q, gi)])
                          if full:
                              TT("dve", Arb[:, ii, :], ps[pa][:, 128:256], cv_("iu" + sfx), ALU.mult, [PSN[pa], "cst"], [gk(Arb, gi)])
                              TT("dve", Ark[:, ii, :], ps[2 + pa][:, 128:256], cv_("iu" + sfx), ALU.mult, [PSN[2 + pa], "cst"], [gk(Ark, gi)])
                      ni_all = len(items)
                      groups = [(gi, gi * 4, min(4, ni_all - gi * 4)) for gi in range((ni_all + 3) // 4)]
                      for gi, i0, ni in groups:
                          TT("dve", Tt[0][:, i0:i0 + ni, :], QT[:, i0:i0 + ni, :], ident4[:, 0:ni, :], ALU.add,
                             [gk(QT, gi), "ident4"], [gk(Tt[0], gi)])
                      cur = 0
                      nlev = 7
                      for k in range(1, nlev):
                          for gi, i0, ni in groups:
                              pb = 3 * (gi % 2)
                              for ii in range(ni):
                                  MM(ps[pb][:, ii * 128:(ii + 1) * 128], QT[:, i0 + ii, :], Q[:, i0 + ii, :], True, True,
                                     [gk(QT, gi), gk(Q, gi)], [PSN[pb]])
                              if k < nlev - 1:
                                  for ii in range(ni):
                                      MM(ps[pb + 1][:, ii * 128:(ii + 1) * 128], Q[:, i0 + ii, :], QT[:, i0 + ii, :], True, True,
                                         [gk(QT, gi), gk(Q, gi)], [PSN[pb + 1]])
                              CP("act", Q[:, i0:i0 + ni, :], ps[pb][:, 0:ni * 128].rearrange("p (a b) -> p a b", b=128), [PSN[pb]], [gk(Q, gi)])
                              if k < nlev - 1:
                                  CP("act" if gi % 2 else "dve", QT[:, i0:i0 + ni, :], ps[pb + 1][:, 0:ni * 128].rearrange("p (a b) -> p a b", b=128),
                                     [PSN[pb + 1]], [gk(QT, gi)])
                              for ii in range(ni):
                                  MM(ps[pb + 2][:, ii * 128:(ii + 1) * 128], Q[:, i0 + ii, :], Tt[cur][:, i0 + ii, :], True, True,
                                     [gk(Q, gi), gk(Tt[cur], gi)], [PSN[pb + 2]])
                              TT("dve", Tt[1 - cur][:, i0:i0 + ni, :], ps[pb + 2][:, 0:ni * 128].rearrange("p (a b) -> p a b", b=128),
                                 Tt[cur][:, i0:i0 + ni, :], ALU.add, [PSN[pb + 2], gk(Tt[cur], gi)], [gk(Tt[1 - cur], gi)])
                          cur = 1 - cur
                      TTf = Tt[cur]
                      if c == 0 and g0 == 0:
                          dump("TTf" + tg, TTf[:, 0:4, :], [gk(TTf, 0)], BF16)
                          dump("Aak" + tg, Aak[:, 0:4, :], [gk(Aak, 0)], BF16)
                          ck("tinv" + tg)
                      for ti, t_ in enumerate(tiles):
                          smp = full and t_ == ntile - 1
                          cs = slice(t_ * 128, t_ * 128 + 128)
                          pT = ps[3].bitcast(BF16)
                          for q_, src_ in enumerate((Pb, Pk, Pv)):
                              S.op("pe", lambda e, q_=q_, src_=src_, cs=cs, pT=pT: e.transpose(out=pT[:, q_ * 128:(q_ + 1) * 128], in_=src_[:, cs],
                                                                                identity=identb[:]), [src_.name, "identb"], ["ps3"])
                          CP("act", tkm[:], pT[:, 0:384].rearrange("p (a b) -> p a b", b=128), ["ps3"], [tkm.name])
                          bT, kT_, vT = tkm[:, 0, :], tkm[:, 1, :], tkm[:, 2, :]
                          if not smp:
                              MM(ps[4][:, 0:128], PAR[:, t_, 0, :], Hb[:], True, False, [PAR.name, Hb.name], ["ps4"])
                          else:
                              for s_ in range(16):
                                  MM(ps[4][:, 0:128], am[:, s_, :], Hsb[:, s_, :], s_ == 0, False, ["am", "Hsb"], ["ps4"])
                          for hh in (0, 1):
                              ii = ti * 2 + hh
                              hs = slice(64 * hh, 64 * hh + 64)
                              MM(ps[4][:, hh * 64:hh * 64 + 64], Aak[:, ii, :], vT[:, hs], False, hh == 1, [gk(Aak, ii // 4), tkm.name], ["ps4"])
                          CP("act", Xb[:], ps[4][:, 0:128], ["ps4"], [Xb.name])
                          if c == 0 and t_ == DBG_TILE:
                              dump("Hbx" + tg, Hb[:], [Hb.name], BF16)
                              dump("PARx" + tg, PAR[:, t_, 0, :], [PAR.name], BF16)
                          for hh in (0, 1):
                              ii = ti * 2 + hh
                              hs = slice(64 * hh, 64 * hh + 64)
                              MM(ps[5][:, hh * 64:hh * 64 + 64], TTf[:, ii, :], Xb[:, hs], True, True, [gk(TTf, ii // 4), Xb.name], ["ps5"])
                          CP("dve", Ub[:], ps[5][:, 0:128], ["ps5"], [Ub.name])
                          if full:
                              for hh in (0, 1):
                                  ii = ti * 2 + hh
                                  hs = slice(64 * hh, 64 * hh + 64)
                                  if not smp:
                                      MM(ps[6][hs, 0:128], Hb[:, hs], PAR[:, t_, 1, :], True, False, [Hb.name, PAR.name], ["ps6"])
                                  else:
                                      for s_ in range(16):
                                          MM(ps[6][hs, 0:128], Hsb[:, s_, hs], rm[:, s_, :], s_ == 0, False, ["Hsb", "rm"], ["ps6"])
                                  MM(ps[6][hs, 0:128], Ub[:, hs], Arb[:, ii, :], False, False, [Ub.name, gk(Arb, ii // 4)], ["ps6"])
                                  MM(ps[6][hs, 0:128], vT[:, hs], Ark[:, ii, :], False, True, [tkm.name, gk(Ark, ii // 4)], ["ps6"])
                              CP("act", yTb[:, cs], ps[6][:, 0:128], ["ps6"], [yTb.name])
                          gam = T[2][:, t_ * 128 + 127:t_ * 128 + 128]
                          if not smp:
                              for hh in (0, 1):
                                  hs = slice(64 * hh, 64 * hh + 64)
                                  MM(ps[7][hs, 0:64], bT[:, hs], Ub[:, hs], True, False, [tkm.name, Ub.name], ["ps7"])
                                  MM(ps[7][hs, 0:64], kT_[:, hs], vT[:, hs], False, True, [tkm.name], ["ps7"])
                              ACT(HG[:], Hf[:, c, :], AF.Copy, ["Hf", n3], [HG.name], scale=gam)
                              STT(Hf[:, c, :], ps[7][:, 0:64], gam, HG[:], ALU.mult, ALU.add, ["ps7", HG.name, n3], ["Hf"])
                              CP("act", Hb[0:64, 0:64], Hf[0:64, c, :], ["Hf"], [Hb.name])
                              CP("act", Hb[64:128, 64:128], Hf[64:128, c, :], ["Hf"], [Hb.name])
                              if c == 0 and t_ == DBG_TILE - 1:
                                  dump("HfE" + tg, Hf[:, 0, :], ["Hf"])
                                  dump("HbE" + tg, Hb[:], [Hb.name], BF16)
                                  dump("HGE" + tg, HG[:], [HG.name])
                              if c == 0 and t_ == DBG_TILE:
                                  dump("tkm" + tg, tkm[:], [tkm.name], BF16)
                                  dump("Xb" + tg, Xb[:], [Xb.name], BF16)
                                  dump("Ub" + tg, Ub[:], [Ub.name], BF16)
                                  dump("Hf0" + tg, Hf[:, 0, :], ["Hf"])
                                  ck("tile0" + tg)
                          else:
                              seqm = cv_("seqm")
                              seqb = seqm.unsqueeze(2).to_broadcast([128, 16, 128])
                              TT("dve", Ue[:], Ub[:].unsqueeze(1).to_broadcast([128, 16, 128]), seqb, ALU.mult, [Ub.name, "cst"], ["Ue"])
                              TT("dve", Ve[:], vT.unsqueeze(1).to_broadcast([128, 16, 128]), seqb, ALU.mult, [tkm.name, "cst"], ["Ve"])
                              gs = T[2][:, (ntile - 1) * 128:n].rearrange("p (s t) -> p s t", t=8)[:, :, 7:8]
                              HsG = T[1][:, 0:1024].rearrange("p (s i) -> p s i", i=64)
                              TT("dve", HsG, Hs[:], gs.to_broadcast([128, 16, 64]), ALU.mult, ["Hs", n3], [n2])
                              for half in (0, 1):
                                  pz = ps[half]
                                  for hh in (0, 1):
                                      hs = slice(64 * hh, 64 * hh + 64)
                                      MM(pz[hs, :], bT[:, hs], Ue[:, 8 * half:8 * half + 8, hs], True, False, [tkm.name, "Ue"], [PSN[half]])
                                      MM(pz[hs, :], kT_[:, hs], Ve[:, 8 * half:8 * half + 8, hs], False, True, [tkm.name, "Ve"], [PSN[half]])
                                  sl = slice(8 * half, 8 * half + 8)
                                  TT("dve", Hs[:, sl, :], pz[:, :].rearrange("p (s i) -> p s i", i=64),
                                     gs[:, sl, :].to_broadcast([128, 8, 64]), ALU.mult, [PSN[half], n3], ["Hs"])
                              TT("dve", Hs[:], Hs[:], HsG, ALU.add, ["Hs", n2], ["Hs"])
                              DMA("sp", nws[c], Hs[:], ["Hs"], [])
                  if c == 0:
                      dump("Hf" + tg, Hf[:, 0, :], ["Hf"])
                      if full:
                          dump("yTb", yTb[:], [yTb.name])
                      ck("pair0" + tg)
                  if full:
                      DMA("sp", nwp[c], Hf[:, c, :], ["Hf"], [])
                      T1, T6 = T[0][:], T[5][:]
                      gns, gnr, rwo = T[1], T[3], Pb
                      ACT(Tb[:], yTb[:], AF.Square, [yTb.name], [Tb.name])
                      CP("act", Pv[:], yTb[:], [yTb.name], [Pv.name])
                      for (t0, nn) in bl:
                          MM(ps[4][:, :nn], blkb[:], Pv[:, t0:t0 + nn], True, True, ["blkb", Pv.name], ["ps4"])
                          MM(ps[5][:, :nn], blkb[:], Tb[:, t0:t0 + nn], True, True, ["blkb", Tb.name], ["ps5"])
                          ACT(gns[:, :nn], ps[4][:, :nn], AF.Square, ["ps4"], [T[1].name], scale=1.0 / 64)
                          STT(gnr[:, :nn], ps[5][:, :nn], 1.0 / 64, gns[:, :nn], ALU.mult, ALU.subtract, ["ps5", T[1].name], [T[3].name])
                          ACT(gnr[:, :nn], gnr[:, :nn], AF.Sqrt, [T[3].name], [T[3].name], bias=GN_EPS)
                          S.op("dve", lambda e, nn=nn: e.reciprocal(out=gnr[:, :nn], in_=gnr[:, :nn]), [T[3].name], [T[3].name])
                          ysl = yTb[:, t0:t0 + nn]
                          STT(ysl, ps[4][:, :nn], -1.0 / 64, ysl, ALU.mult, ALU.add, ["ps4", yTb.name], [yTb.name])
                          TT("dve", ysl, ysl, gnr[:, :nn], ALU.mult, [yTb.name, T[3].name], [yTb.name])
                          TS("dve", ysl, ysl, vcol("ln_x_g", c), vcol("ln_x_b", c), ALU.mult, ALU.add, [yTb.name, "vecs"], [yTb.name])
                          TT("dve", ysl, ysl, T6[:, t0:t0 + nn], ALU.add, [yTb.name, T[5].name], [yTb.name])
                          TT("dve", rwo[:, t0:t0 + nn], ysl, T1[:, t0:t0 + nn], ALU.mult, [yTb.name, T[0].name], [Pb.name])
                      DMA("sp", mixS[1024 + c * 128:1024 + (c + 1) * 128, :], Pb[:], [Pb.name], ["mixS"])
                  fill(filler, 100)
              if full:
                  DMA("sp", nsh, nshs[:], [nshs.name], [])
              S.barrier()
              es.close()

          if not SKIP_MIX:
              mix_phase(False)
              ck("mixP")
              mix_phase(True)
          lstack.close()
          if not SKIP_MIX:
              dump("mixS", mixS, ["mixS"], BF16)
          ck("mixdone")

          def boundary(es, produce, xsrc, gpost, gpre, xdst, hT, final=False):
              sT = sbt(es, "sT", [128, KC, NCH], BF16)
              sqs = [sbt(es, "bsq%d" % i, [128, 512], BF16) for i in range(2)]
              pend = []
              cctr = [0]
              rs = sbt(es, "brs", [128, NCH], F32)
              xc = [sbt(es, "bxc%d" % i, [128, NCH], F32) for i in range(2)]
              sq2 = sbt(es, "bsq2", [128, NCH], BF16)
              bl = blocks(NCH)
              for c in range(KC):
                  def consume(pi, t0, nn, c=c):
                      while pend:
                          pend.pop(0)()
                      sq = sqs[cctr[0] % 2]
                      cctr[0] += 1
                      CP("act", sT[:, c, t0:t0 + nn], ps[pi][:, :nn], [PSN[pi]], ["sT"])
                      ACT(sq[:, :nn], ps[pi][:, :nn], AF.Square, [PSN[pi]], [sq.name])
                      bi = t0 // 512
                      pend.append(lambda sq=sq, bi=bi, nn=nn, c=c: MM(ps[5 + bi][:, :nn], onesb[:], sq[:, :nn], c == 0, c == KC - 1,
                                                                      ["onesb", sq.name], [PSN[5 + bi]]))
                  produce(c, consume)
              while pend:
                  pend.pop(0)()
              for bi, (t0, nn) in enumerate(bl):
                  ACT(rs[:, t0:t0 + nn], ps[5 + bi][:, :nn], AF.Sqrt, [PSN[5 + bi]], ["brs"], bias=RMS_EPS, scale=1.0 / D)
              S.op("dve", lambda e: e.reciprocal(out=rs[:], in_=rs[:]), ["brs"], ["brs"])
              for c in range(KC):
                  x_ = xc[c % 2]
                  DMA("sp", x_[:], xsrc[c * 128:(c + 1) * 128, :], [], [x_.name])
                  TT("dve", sq2[:], sT[:, c, :], rs[:], ALU.mult, ["sT", "brs"], ["bsq2"])
                  STT(x_[:], sq2[:], vcol(gpost, c), x_[:], ALU.mult, ALU.add, ["bsq2", x_.name, "vecs"], [x_.name])
                  DMA("sp", xdst[c * 128:(c + 1) * 128, :], x_[:], [x_.name], ["xdst"])
                  if not final:
                      ACT(sq2[:], x_[:], AF.Square, [x_.name], ["bsq2"])
                      for bi, (t0, nn) in enumerate(bl):
                          MM(ps[5 + bi][:, :nn], onesb[:], sq2[:, t0:t0 + nn], c == 0, c == KC - 1, ["onesb", "bsq2"], [PSN[5 + bi]])
              if final:
                  return
              for bi, (t0, nn) in enumerate(bl):
                  ACT(rs_keep[:, t0:t0 + nn], ps[5 + bi][:, :nn], AF.Sqrt, [PSN[5 + bi]], ["rs_keep"], bias=RMS_EPS, scale=1.0 / D)
              S.op("dve", lambda e: e.reciprocal(out=rs_keep[:], in_=rs_keep[:]), ["rs_keep"], ["rs_keep"])

          def prenorm_apply(es, hT, xsrc, gpre):
              xc = [sbt(es, "pxc%d" % i, [128, NCH], F32) for i in range(2)]
              for c in range(KC):
                  x_ = xc[c % 2]
                  DMA("sp", x_[:], xsrc[c * 128:(c + 1) * 128, :], ["xdst"], [x_.name])
                  STT(hT[:, c, :], x_[:], vcol(gpre, c), rs_keep[:], ALU.mult, ALU.mult, [x_.name, "rs_keep", "vecs"], [hT.name])

          rs_keep = sbt(top, "rs_keep", [128, NCH], F32)


          with ExitStack() as es:
           if not SKIP_MIX:
              wb = [sbt(es, "wbo%d" % i, [128, KC, 128], BF16) for i in range(3)]
              mx = sbt(es, "mx", [128, KC, NCH], BF16)
              for kc in range(KC):
                  DMA("sp", mx[:, kc, :], mixS[kc * 128:(kc + 1) * 128, :], ["mixS"], ["mx"])

              def prod(c, consume):
                  w = load_w(wb, w_out_l[c])
                  for bi, (t0, nn) in enumerate(blocks(NCH)):
                      pi = (c * 3 + bi) % 4
                      for kc in range(KC):
                          MM(ps[pi][:, :nn], w[:, kc, :], mx[:, kc, t0:t0 + nn], kc == 0, kc == KC - 1, [w.name, "mx"], [PSN[pi]])
                      consume(pi, t0, nn)
              boundary(es, prod, xT[:, CH0:NT], "norm_mix_post", "norm_xa_pre", x1T, None)
              S.barrier()
              dump("x1T", x1T, ["xdst"])
              ck("b1")

          with ExitStack() as eso:
            oT = sbt(eso, "oT", [128, KC, NCH], BF16)
            with ExitStack() as es:
              hT2 = sbt(es, "hT2", [128, KC, NCH], BF16)
              with ExitStack() as e2:
                  if not SKIP_MIX:
                      prenorm_apply(e2, hT2, x1T, "norm_xa_pre")
                  S.barrier()
              ck("pa1")
              kTp = sbt(es, "kTp", [128, KC, 256], BF16)
              vp = sbt(es, "vp", [128, 2, D], BF16)
              with ExitStack() as ea:
                  wbk = [sbt(ea, "wbk%d" % i, [128, KC, 128], BF16) for i in range(3)]
                  mnT = sbt(ea, "mnT", [128, KC, 256], BF16)
                  with ExitStack() as e2:
                      prenorm(e2, memT, 256, "norm_mem", mnT, 0, "mem")
                      S.barrier()
                  ck("mn")
                  ktf = sbt(ea, "ktf", [128, 256], F32)
                  vpf = sbt(ea, "vpf", [128, 512], F32)
                  for c in range(KC):
                      w = load_w(wbk, w_k_l[c])
                      for kc in range(KC):
                          MM(ps[0][:, :256], w[:, kc, :], mnT[:, kc, :], kc == 0, kc == KC - 1, [w.name, "mnT"], ["ps0"])
                      CP("act", kTp[:, c, :], ps[0][:, :256], ["ps0"], ["kTp"])
                      CP("dve", ktf[:], ps[0][:, :256], ["ps0"], ["ktf"])
                      DMA("sp", mkT[c * 128:(c + 1) * 128, :], ktf[:], ["ktf"], [])
                  ck("kproj")
                  wvb = [sbt(ea, "wvb%d" % i, [128, KC, 512], BF16) for i in range(2)]
                  for cb in range(4):
                      w = load_w(wvb, w_v_l[cb])
                      for mt in range(2):
                          for kc in range(KC):
                              MM(ps[1 + mt][:, :], mnT[:, kc, mt * 128:(mt + 1) * 128], w[:, kc, :], kc == 0, kc == KC - 1,
                                 [w.name, "mnT"], [PSN[1 + mt]])
                          CP("act", vp[:, mt, cb * 512:(cb + 1) * 512], ps[1 + mt][:, :], [PSN[1 + mt]], ["vp"])
                          CP("dve", vpf[:], ps[1 + mt][:, :], [PSN[1 + mt]], ["vpf"])
                          DMA("sp", mv[mt * 128:(mt + 1) * 128, cb * 512:(cb + 1) * 512], vpf[:], ["vpf"], [])
                  S.barrier()
              ck("memkv")
              wb = [sbt(es, "wba%d" % i, [128, KC, 128], BF16) for i in range(2)]
              qT = sbt(es, "qT", [128, KC, NCH], BF16)
              for c in range(KC):
                  w = load_w(wb, w_q_l[c])
                  for bi, (t0, nn) in enumerate(blocks(NCH)):
                      pi = 3 + (c * 3 + bi) % 3
                      for kc in range(KC):
                          MM(ps[pi][:, :nn], w[:, kc, :], hT2[:, kc, t0:t0 + nn], kc == 0, kc == KC - 1, [w.name, hT2.name], [PSN[pi]])
                      ACT(qT[:, c, t0:t0 + nn], ps[pi][:, :nn], AF.Copy, [PSN[pi]], ["qT"], scale=512.0 ** -0.5)
              mx8 = sbt(es, "mx8", [128, 8], F32)
              sm8 = sbt(es, "sm8", [128, 8], F32)
              att = sbt(es, "att", [128, 4, 256], BF16)
              attT = sbt(es, "attT", [128, 2, 4, 128], BF16)
              for t_ in range(9):
                  cs = slice(t_ * 128, (t_ + 1) * 128)
                  for h in range(4):
                      pi = h % 2
                      for dc in range(4):
                          MM(ps[pi][:, 0:256], qT[:, 4 * h + dc, cs], kTp[:, 4 * h + dc, :], dc == 0, dc == 3, ["qT", "kTp"], [PSN[pi]])
                      S.op("dve", lambda e, pi=pi, h=h: e.reduce_max(out=mx8[:, h:h + 1], in_=ps[pi][:, 0:256], axis=mybir.AxisListType.X),
                           [PSN[pi]], ["mx8"])
                      TS("dve", mx8[:, 4 + h:5 + h], mx8[:, h:h + 1], -1.0, None, ALU.mult, None, ["mx8"], ["mx8"])
                      S.op("act", lambda e, pi=pi, h=h: e.activation(out=att[:, h, :], in_=ps[pi][:, 0:256], func=AF.Exp,
                                                                    bias=mx8[:, 4 + h:5 + h], accum_out=sm8[:, h:h + 1]),
                           [PSN[pi], "mx8"], ["att", "sm8"])
                      S.op("dve", lambda e, h=h: e.reciprocal(out=sm8[:, 4 + h:5 + h], in_=sm8[:, h:h + 1]), ["sm8"], ["sm8"])
                      TS("dve", att[:, h, :], att[:, h, :], sm8[:, 4 + h:5 + h], None, ALU.mult, None, ["att", "sm8"], ["att"])
                  pT = ps[2].bitcast(BF16)
                  for h in range(4):
                      for mt in range(2):
                          S.op("pe", lambda e, h=h, mt=mt: e.transpose(out=pT[:, (mt * 4 + h) * 128:(mt * 4 + h + 1) * 128],
                                                                      in_=att[:, h, mt * 128:(mt + 1) * 128], identity=identb[:]),
                               ["att", "identb"], ["ps2"])
                  CP("act", attT[:], pT[:, 0:1024].rearrange("p (a b c) -> p a b c", a=2, b=4), ["ps2"], ["attT"])
                  for dc in range(KC):
                      h = dc // 4
                      pi = 3 + dc % 2
                      for mt in range(2):
                          MM(ps[pi][:, 0:128], vp[:, mt, dc * 128:(dc + 1) * 128], attT[:, mt, h, :], mt == 0, mt == 1, ["vp", "attT"], [PSN[pi]])
                      CP("act" if dc % 2 else "dve", oT[:, dc, cs], ps[pi][:, 0:128], [PSN[pi]], ["oT"])
              dump("oTp", oT[:, :, 0:1152], ["oT"], BF16)
              ck("attnP")
              kts = [sbt(es, "kts%d" % i, [128, KC, 256], BF16) for i in range(2)]
              vss = [sbt(es, "vss%d" % i, [128, 2, D], BF16) for i in range(2)]
              sc8 = sbt(es, "sc8", [8, 4, 256], F32)
              at8 = sbt(es, "at8", [8, 4, 256], BF16)
              m8 = sbt(es, "m8", [8, 8], F32)
              a8T = sbt(es, "a8T", [128, 8, 8], BF16)
              for s_ in range(16):
                  kt = kts[s_ % 2]
                  vv = vss[s_ % 2]
                  DMA("pool", kt[:], kTs[s_].rearrange("(c p) m -> p c m", p=128), [], [kt.name])
                  DMA("pool", vv[:], vs[s_].rearrange("(t p) d -> p t d", p=128), [], [vv.name])
                  cs = slice(1152 + 8 * s_, 1152 + 8 * s_ + 8)
                  for h in range(4):
                      pi = h % 2
                      for dc in range(4):
                          MM(ps[pi][0:8, 0:256], qT[:, 4 * h + dc, cs], kt[:, 4 * h + dc, :], dc == 0, dc == 3, ["qT", kt.name], [PSN[pi]])
                      CP("act", sc8[:, h, :], ps[pi][0:8, 0:256], [PSN[pi]], ["sc8"])
                  S.op("dve", lambda e: e.tensor_reduce(out=m8[:, 0:4], in_=sc8[:], axis=mybir.AxisListType.X, op=ALU.max), ["sc8"], ["m8"])
                  TT("dve", sc8[:], sc8[:], m8[:, 0:4].unsqueeze(2).to_broadcast([8, 4, 256]), ALU.subtract, ["sc8", "m8"], ["sc8"])
                  ACT(sc8[:], sc8[:], AF.Exp, ["sc8"], ["sc8"])
                  S.op("dve", lambda e: e.tensor_reduce(out=m8[:, 4:8], in_=sc8[:], axis=mybir.AxisListType.X, op=ALU.add), ["sc8"], ["m8"])
                  S.op("dve", lambda e: e.reciprocal(out=m8[:, 4:8], in_=m8[:, 4:8]), ["m8"], ["m8"])
                  TT("dve", at8[:], sc8[:], m8[:, 4:8].unsqueeze(2).to_broadcast([8, 4, 256]), ALU.mult, ["sc8", "m8"], ["at8"])
                  pT = ps[2].bitcast(BF16)
                  for h in range(4):
                      for mt in range(2):
                          S.op("pe", lambda e, h=h, mt=mt: e.transpose(out=pT[:, (mt * 4 + h) * 8:(mt * 4 + h + 1) * 8],
                                                                      in_=at8[:, h, mt * 128:(mt + 1) * 128], identity=identb[0:8, 0:8]),
                               ["at8", "identb"], ["ps2"])
                  CP("act", a8T[:], pT[:, 0:64].rearrange("p (a b) -> p a b", b=8), ["ps2"], ["a8T"])
                  for dc in range(KC):
                      h = dc // 4
                      for mt in range(2):
                          MM(ps[3][:, dc * 8:dc * 8 + 8], vv[:, mt, dc * 128:(dc + 1) * 128], a8T[:, mt * 4 + h, :], mt == 0, mt == 1,
                             [vv.name, "a8T"], ["ps3"])
                  CP("act", oT[:, :, cs], ps[3][:, 0:128].rearrange("p (a b) -> p a b", b=8), ["ps3"], ["oT"])
              S.barrier()
              dump("oT", oT[:], ["oT"], BF16)
              ck("attnS")
            with ExitStack() as es:
              wb = [sbt(es, "wbo2%d" % i, [128, KC, 128], BF16) for i in range(3)]

              def prod(c, consume):
                  w = load_w(wb, w_o_l[c])
                  for bi, (t0, nn) in enumerate(blocks(NCH)):
                      pi = (c * 3 + bi) % 4
                      for kc in range(KC):
                          MM(ps[pi][:, :nn], w[:, kc, :], oT[:, kc, t0:t0 + nn], kc == 0, kc == KC - 1, [w.name, "oT"], [PSN[pi]])
                      consume(pi, t0, nn)
              boundary(es, prod, x1T, "norm_xa_post", "norm_ffn_pre", x2T, None)
              S.barrier()
              dump("x2T", x2T, ["xdst"])
              ck("b2")

          with ExitStack() as es:
              actT = sbt(es, "actT", [128, NFC, NCH], BF16)
              eu = ExitStack()
              hT2 = sbt(eu, "hT2", [128, KC, NCH], BF16)
              with ExitStack() as e2:
                  prenorm_apply(e2, hT2, x2T, "norm_ffn_pre")
                  S.barrier()
              wb = [sbt(eu, "wbf%d" % i, [128, KC, 128], BF16) for i in range(4)]
              TS("dve", hT2[:, :, 0:128], hT2[:, :, 0:128], flg[:, 0:1], None, ALU.mult, None, [hT2.name, "flg"], [hT2.name])
              up = [sbt(eu, "up%d" % i, [128, 2 + 1152], F32) for i in range(2)]
              ups4 = [[sbt(eu, "ups%d_%d" % (i, q), [128, 16, 10], F32) for q in range(2)] for i in range(2)]
              stgp = [[sbt(eu, "stgp%d_%d" % (i, q), [128, 2], F32) for q in range(2)] for i in range(2)]
              stgs = [[sbt(eu, "stgs%d_%d" % (i, q), [128, 16, 2], F32) for q in range(2)] for i in range(2)]
              stgh = [[sbt(eu, "stgh%d" % i, [128, 16, 2], F32)] * 2 for i in range(2)]
              uc = [sbt(eu, "uc%d" % i, [128, NCH], F32) for i in range(2)]
              for i in range(2):
                  S.op("pool", lambda e, i=i: e.memset(up[i][:, 0:2], 0.0), [], [up[i].name + "A"])
              pctr = [0]
              ws_next = [load_w(wb, w_up_l[0]), load_w(wb, w_up_l[NFC])]
              for c in range(NFC):
                  ws = ws_next
                  if c + 1 < NFC:
                      ws_next = [load_w(wb, w_up_l[c + 1]), load_w(wb, w_up_l[NFC + c + 1])]
                  ups = [ups4[0][c % 2], ups4[1][c % 2]]
                  for vi, ch in enumerate((c, NFC + c)):
                      sh_ = stgh[vi][c % 2]
                      DMA("sp", sh_[:], ffnst[ch], [], [sh_.name])
                      CP("pool", ups[vi][:, :, 0:2], sh_[:], [sh_.name], [ups[vi].name])
                  for reg, rblocks in (("A", [(0, 512), (512, 128)]), ("B", [(640, 512), (1152, 128)])):
                      for vi, ch in enumerate((c, NFC + c)):
                          w = ws[vi]
                          u_, us_, uc_ = up[vi], ups[vi], uc[vi]
                          uk = u_.name + reg
                          for (t0, nn) in rblocks:
                              pi = pctr[0] % 6
                              pctr[0] += 1
                              for kc in range(KC):
                                  MM(ps[pi][:, :nn], w[:, kc, :], hT2[:, kc, t0:t0 + nn], kc == 0, kc == KC - 1, [w.name, hT2.name], [PSN[pi]])
                              if t0 < 1152:
                                  CP("act", u_[:, 2 + t0:2 + t0 + nn], ps[pi][:, :nn], [PSN[pi]], [uk])
                              else:
                                  CP("act", us_[:, :, 2:10], ps[pi][:, :nn].rearrange("p (s t) -> p s t", t=8), [PSN[pi]], [us_.name])
                          fw0 = V["ffn_dw"] + ch * 3
                          wj = [vecs[:, fw0 + j:fw0 + j + 1] for j in range(3)]
                          bj = vcol("ffn_dw_b", ch)
                          ck_ = uc_.name + reg
                          if reg == "A":
                              lo, hi = 0, 640
                              rdk = [uk, "vecs"]
                          else:
                              lo, hi = 640, 1152
                              rdk = [u_.name + "A", uk, "vecs"]
                              sp_, ss_ = stgp[vi][c % 2], stgs[vi][c % 2]
                              CP("pool", sp_[:], u_[:, 1152:1154], [uk], [sp_.name])
                              CP("pool", ss_[:], us_[:, :, 8:10], [us_.name], [ss_.name])
                              DMA("sp", nfp[ch], sp_[:], [sp_.name], [])
                              DMA("sp", nfs[ch], ss_[:], [ss_.name], [])
                          TS("dve", uc_[:, lo:hi], u_[:, 2 + lo:2 + hi], wj[2], bj, ALU.mult, ALU.add, rdk, [ck_])
                          STT(uc_[:, lo:hi], u_[:, 1 + lo:1 + hi], wj[1], uc_[:, lo:hi], ALU.mult, ALU.add, rdk + [ck_], [ck_])
                          STT(uc_[:, lo:hi], u_[:, lo:hi], wj[0], uc_[:, lo:hi], ALU.mult, ALU.add, rdk + [ck_], [ck_])
                          if reg == "B":
                              ucs = uc_[:, 1152:1280].rearrange("p (s t) -> p s t", t=8)
                              TS("dve", ucs, us_[:, :, 2:10], wj[2], bj, ALU.mult, ALU.add, [us_.name, "vecs"], [ck_])
                              STT(ucs, us_[:, :, 1:9], wj[1], ucs, ALU.mult, ALU.add, [us_.name, ck_, "vecs"], [ck_])
                              STT(ucs, us_[:, :, 0:8], wj[0], ucs, ALU.mult, ALU.add, [us_.name, ck_, "vecs"], [ck_])
                      lo, hi = (0, 640) if reg == "A" else (640, 1280)
                      k0, k1 = uc[0].name + reg, uc[1].name + reg
                      ACT(uc[0][:, lo:hi], uc[0][:, lo:hi], AF.Silu, [k0], [k0])
                      TT("dve", actT[:, c, lo:hi], uc[0][:, lo:hi], uc[1][:, lo:hi], ALU.mult, [k0, k1], ["actT" + reg])
              S.barrier()
              dump("actT", actT[:], ["actTA", "actTB"], BF16)
              ck("ffnup")
              eu.close()
              wdb = [sbt(es, "wdb%d" % i, [128, 22, 128], BF16) for i in range(3)]

              def prod(c, consume):
                  wh = [load_w(wdb, w_down_l[c][:, 0:22, :]), load_w(wdb, w_down_l[c][:, 22:44, :])]
                  for bi, (t0, nn) in enumerate(blocks(NCH)):
                      pi = (c * 3 + bi) % 4
                      for kc in range(NFC):
                          w = wh[kc // 22]
                          MM(ps[pi][:, :nn], w[:, kc % 22, :], actT[:, kc, t0:t0 + nn], kc == 0, kc == NFC - 1, [w.name, "actTA", "actTB"], [PSN[pi]])
                      consume(pi, t0, nn)
              boundary(es, prod, x2T, "norm_ffn_post", None, yT, None, final=True)

    except StopBuild:
        pass
    S.emit()
    return nc


_CACHE = {}


def make_in_maps(inp):
    inp = {k: np.asarray(v) for k, v in inp.items()}
    f32 = np.float32
    vecs, NV = build_vecs(inp)
    consts, NCONST = build_consts()
    w_in = inp["w_in"][0]
    Wp = np.zeros((D, 44 * 128), f32)
    Wp[:, :5120] = w_in[:, :5120]
    Wp[:, 5120:5216] = w_in[:, 5120:5216]
    Wp[:, 5248:5344] = w_in[:, 5216:5312]
    Wp[:, 5376:5632] = w_in[:, 5312:5568]
    shared = {
        "vecs_in": vecs, "consts_in": consts,
        "w_in_l": relayout_w(Wp, 128),
        "w_lora_l": np.concatenate([inp["w_lora"][0], np.zeros((32, 1024), f32)], 0),
        "a_lora_l": np.concatenate([inp["a_lora"][0], np.zeros((32, 1024), f32)], 0),
        "g_lora_l": np.ascontiguousarray(inp["g_lora"][0].reshape(2, 128, 1024).transpose(1, 0, 2)),
        "w_out_l": relayout_w(inp["w_out"][0], 128),
        "w_q_l": relayout_w(inp["w_q"][0], 128),
        "w_k_l": relayout_w(inp["w_k"][0], 128),
        "w_v_l": relayout_w(inp["w_v"][0], 512),
        "w_o_l": relayout_w(inp["w_o"][0], 128),
        "w_up_l": relayout_w(inp["w_up"][0], 128),
        "w_down_l": relayout_w(inp["w_down"][0], 128),
    }
    xp, xs = inp["x_prompt"], inp["x_sample"]
    in_maps = []
    for c in range(8):
        b, half = c // 2, c % 2
        sq = slice(16 * c, 16 * c + 16)
        xT = np.zeros((D, NT), f32)
        if half == 1:
            xT[:, :2048] = xp[b].T
        else:
            xT[:, 1024:2048] = xp[b, :1024].T
        xT[:, 2048:] = xs[sq].reshape(128, D).T
        ss = inp["state_shift"][0, sq]
        shp = np.zeros((16, 28 * 128), f32)
        shp[:, :3072] = ss[:, :3072]
        shp[:, 3072:3168] = ss[:, 3072:3168]
        shp[:, 3200:3296] = ss[:, 3168:3264]
        shp[:, 3328:3584] = ss[:, 3264:3520]
        wk = inp["state_wkv"][0, sq]
        wkT = wk.reshape(16, 8, 2, 64, 64).transpose(1, 2, 4, 0, 3).reshape(8, 128, 16, 64)
        m = dict(shared)
        m.update({
            "xT": xT,
            "flag": np.full((128, 1), float(half), f32),
            "memT": np.ascontiguousarray(inp["mem_prompt"][b].T),
            "kTs": np.ascontiguousarray(inp["cache_mem_k"][0, sq].reshape(16, 256, D).transpose(0, 2, 1)),
            "vs": np.ascontiguousarray(inp["cache_mem_v"][0, sq].reshape(16, 256, D)),
            "convst": np.ascontiguousarray(inp["state_conv"][0, sq].transpose(2, 0, 1)),
            "shiftst": np.ascontiguousarray(shp.reshape(16, 28, 128).transpose(2, 1, 0)),
            "wkvT": np.ascontiguousarray(wkT),
            "ffnst": np.ascontiguousarray(inp["state_ffn"][0, sq].reshape(16, 2, 88, 128).transpose(2, 3, 0, 1)),
        })
        in_maps.append(m)
    return in_maps, NV, NCONST


def kernel(**inp):
    f32 = np.float32
    in_maps, NV, NCONST = make_in_maps(inp)
    key = (NV, NCONST)
    if key not in _CACHE:
        _CACHE[key] = build_nc(NV, NCONST)
    nc = _CACHE[key]
    res = run_bass_kernel_spmd(nc, in_maps, core_ids=list(range(8))).results

    y_p = np.zeros((4, 2048, D), f32)
    y_s = np.zeros((128, 8, D), f32)
    conv_p = np.zeros((1, 4, 30, 1024), f32)
    conv_s = np.zeros((1, 128, 30, 1024), f32)
    sh_p = np.zeros((1, 4, 3520), f32)
    sh_s = np.zeros((1, 128, 3520), f32)
    wkv_p = np.zeros((1, 4, 16, 64, 64), f32)
    wkv_s = np.zeros((1, 128, 16, 64, 64), f32)
    ffn_p = np.zeros((1, 4, 2, 11264), f32)
    ffn_s = np.zeros((1, 128, 2, 11264), f32)
    mk = np.zeros((1, 4, 256, 4, 512), f32)
    mvv = np.zeros((1, 4, 256, 4, 512), f32)

    def unshift(a):
        return np.concatenate([a[:3072], a[3072:3168], a[3200:3296], a[3328:3584]], 0)

    for c in range(8):
        r = res[c]
        b, half = c // 2, c % 2
        sq = slice(16 * c, 16 * c + 16)
        yT = r["yT"]
        y_p[b, half * 1024:(half + 1) * 1024] = yT[:, 128:1152].T
        y_s[sq] = yT[:, 1152:1280].T.reshape(16, 8, D)
        conv_s[0, sq] = r["ncs"].transpose(1, 2, 0)
        nshf = r["nsh"].transpose(1, 0, 2).reshape(28 * 128, 17)
        sh_s[0, sq] = unshift(nshf[:, 1:17]).T
        wkv_s[0, sq] = r["nws"].reshape(8, 2, 64, 16, 64).transpose(3, 0, 1, 4, 2).reshape(16, 16, 64, 64)
        ffn_s[0, sq] = r["nfs"].transpose(2, 3, 0, 1).reshape(16, 2, 11264)
        if half == 1:
            conv_p[0, b] = r["ncp"].T
            sh_p[0, b] = unshift(nshf[:, 0:1])[:, 0]
            wkv_p[0, b] = r["nwp"].reshape(8, 2, 64, 64).transpose(0, 1, 3, 2).reshape(16, 64, 64)
            ffn_p[0, b] = r["nfp"].transpose(2, 0, 1).reshape(2, 11264)
            mk[0, b] = r["mkT"].T.reshape(256, 4, 512)
            mvv[0, b] = r["mv"].reshape(256, 4, 512)
    return (y_p, y_s, conv_p, conv_s, sh_p, sh_s, wkv_p, wkv_s, ffn_p, ffn_s, mk, mvv)
```
